# Optimizing a Trainium2 kernel written in Bass

```python
import math
import jax
import jax.numpy as jnp
from jax import lax
import numpy as np


D_MODEL = 2048
BATCH = 2
SEQ = 4096
DEPTH = 4

GRID_W = 64
CTX_LEN = 256
EPS = 1e-6
M_INIT = -1e30
F32 = jnp.float32

MLA_HEADS = 8
Q_LORA = 512
KV_LORA = 256
QK_NOPE = 128
QK_ROPE = 64
V_HEAD = 128
ROPE_BASE = 10000.0
Q_BLOCK = 128

ML_HEADS = 4
ML_DK = 64
ML_DV = 128
ML_CHUNK = 64

GD_HEADS = 4
GD_DK = 128
GD_DV = 128
GD_CHUNK = 64
CONV_W = 5

D_FF = 4 * D_MODEL
MLA_W = MLA_HEADS * V_HEAD
ML_W = ML_HEADS * ML_DV
GD_W = GD_HEADS * GD_DV
MIX_W = MLA_W + ML_W + GD_W

IN_SPLITS = (Q_LORA, KV_LORA, QK_ROPE,
             ML_HEADS * ML_DK, ML_HEADS * ML_DK, ML_W, ML_W, 4 * ML_HEADS,
             GD_HEADS * GD_DK, GD_HEADS * GD_DK, GD_W, GD_W, 4 * GD_HEADS)
D_IN = sum(IN_SPLITS)

kernel_name = 'hybrid_mla_mlstm_gdn_prefix_dit'


def rmsnorm(x, w):
    xf = x.astype(F32)
    y = xf * lax.rsqrt(jnp.mean(xf * xf, axis=-1, keepdims=True) + EPS)
    return (y * w.astype(F32)).astype(x.dtype)


def l2norm(x):
    xf = x.astype(F32)
    return xf * lax.rsqrt(jnp.sum(xf * xf, axis=-1, keepdims=True) + EPS)


def modulate(h, shift, scale):
    return h * (1.0 + scale) + shift


def heads(a, h):
    b, n, _ = a.shape
    return a.reshape(b, n, h, -1).transpose(0, 2, 1, 3)


def merge_heads(a):
    b, h, n, d = a.shape
    return a.transpose(0, 2, 1, 3).reshape(b, n, h * d)


def head_rmsnorm(y, gain, h):
    b, n, _ = y.shape
    return rmsnorm(y.reshape(b, n, h, -1), gain).reshape(b, n, -1)


def split_cols(p):
    return jnp.split(p, [int(i) for i in np.cumsum(IN_SPLITS)[:-1]], axis=-1)


def to_chunks(a, size):
    b, h, t = a.shape[:3]
    return jnp.moveaxis(a.reshape(b, h, t // size, size, *a.shape[3:]), 2, 0)


def from_chunks(a):
    a = jnp.moveaxis(a, 0, 2)
    return a.reshape(a.shape[0], a.shape[1], -1, *a.shape[4:])


def rope_angles(n):
    rows = n // GRID_W
    row = jnp.repeat(jnp.arange(rows, dtype=F32), GRID_W)
    col = jnp.tile(jnp.arange(GRID_W, dtype=F32), rows)
    half = QK_ROPE // 2
    inv = ROPE_BASE ** (-jnp.arange(0, half, 2, dtype=F32) / half)
    return row[:, None] * inv, col[:, None] * inv


def rotate(x, ang):
    m = ang.shape[-1]
    cos = jnp.cos(ang).astype(x.dtype)
    sin = jnp.sin(ang).astype(x.dtype)
    x1, x2 = x[..., :m], x[..., m:]
    return jnp.concatenate([x1 * cos - x2 * sin, x1 * sin + x2 * cos], axis=-1)


def axial_rope(x, ang_r, ang_c):
    half = QK_ROPE // 2
    return jnp.concatenate([rotate(x[..., :half], ang_r), rotate(x[..., half:], ang_c)], axis=-1)


def mla_qkv(c_q, c_kv, k_pe, q_norm, w_uq, kv_norm, w_ukv, rope):
    q = heads(rmsnorm(c_q, q_norm) @ w_uq, MLA_HEADS)
    kv = heads(rmsnorm(c_kv, kv_norm) @ w_ukv, MLA_HEADS)
    q_nope, q_pe = q[..., :QK_NOPE], q[..., QK_NOPE:]
    k_nope, v = kv[..., :QK_NOPE], kv[..., QK_NOPE:]
    k_pe = k_pe[:, None]
    if rope is not None:
        q_pe = axial_rope(q_pe, *rope)
        k_pe = axial_rope(k_pe, *rope)
    q = jnp.concatenate([q_nope, q_pe], axis=-1)
    k = jnp.concatenate([k_nope, jnp.broadcast_to(k_pe, k_nope.shape[:-1] + (QK_ROPE,))], axis=-1)
    return q, k, v


def softmax_attention(q, k, v):
    s = jnp.einsum('bhqd,bhkd->bhqk', q, k).astype(F32) * (QK_NOPE + QK_ROPE) ** -0.5
    p = jax.nn.softmax(s, axis=-1).astype(v.dtype)
    return jnp.einsum('bhqk,bhkd->bhqd', p, v)


def blocked_attention(q, k, v):
    b, h, n, dq = q.shape
    qb = jnp.moveaxis(q.reshape(b, h, n // Q_BLOCK, Q_BLOCK, dq), 2, 0)
    out = lax.map(lambda qi: softmax_attention(qi, k, v), qb)
    return jnp.moveaxis(out, 0, 2).reshape(b, h, n, -1)


def mlstm_inputs(p, gate_bias):
    q, k, v, o, gt = p
    b, n, _ = q.shape
    g = jnp.moveaxis((gt.astype(F32) + gate_bias.astype(F32)).reshape(b, n, 4, ML_HEADS), 1, -1)
    fwd = (g[:, 0], jax.nn.log_sigmoid(g[:, 1]))
    bwd = (g[:, 2], jax.nn.log_sigmoid(g[:, 3]))
    return (heads(q, ML_HEADS), heads(k, ML_HEADS) * ML_DK ** -0.5, heads(v, ML_HEADS), o, fwd, bwd)


def mlstm_scan(q, k, v, ig, lf, state):
    out_dtype = v.dtype
    causal = jnp.tril(jnp.ones((ML_CHUNK, ML_CHUNK), dtype=bool))
    xs = tuple(to_chunks(a.astype(F32), ML_CHUNK) for a in (q, k, v, ig, lf))

    def step(carry, inp):
        cmat, nvec, m = carry
        qc, kc, vc, ic, fc = inp
        b = jnp.cumsum(fc, axis=-1)
        log_d = jnp.where(causal, b[..., :, None] - b[..., None, :] + ic[..., None, :], -jnp.inf)
        log_inter = b + m[..., None]
        m_t = jnp.maximum(log_inter, jnp.max(log_d, axis=-1))
        d = jnp.exp(log_d - m_t[..., None])
        w_inter = jnp.exp(log_inter - m_t)
        s = jnp.einsum('bhtd,bhsd->bhts', qc, kc) * d
        num = w_inter[..., None] * jnp.einsum('bhtd,bhde->bhte', qc, cmat) + jnp.einsum('bhts,bhse->bhte', s, vc)
        den = w_inter * jnp.einsum('bhtd,bhd->bht', qc, nvec) + jnp.sum(s, axis=-1)
        h = num / jnp.maximum(jnp.abs(den), jnp.exp(-m_t))[..., None]
        b_end = b[..., -1]
        log_w = b_end[..., None] - b + ic
        m_new = jnp.maximum(b_end + m, jnp.max(log_w, axis=-1))
        decay = jnp.exp(b_end + m - m_new)
        w = jnp.exp(log_w - m_new[..., None])
        cmat = decay[..., None, None] * cmat + jnp.einsum('bhs,bhsd,bhse->bhde', w, kc, vc)
        nvec = decay[..., None] * nvec + jnp.einsum('bhs,bhsd->bhd', w, kc)
        return (cmat, nvec, m_new), h

    state, h = lax.scan(step, state, xs)
    return from_chunks(h).astype(out_dtype), state


def depthwise_conv(x, w):
    return lax.conv_general_dilated(x, w[:, None, :].astype(x.dtype), window_strides=(1,),
                                    padding=[(CONV_W // 2, CONV_W // 2)],
                                    dimension_numbers=('NWC', 'WIO', 'NWC'),
                                    feature_group_count=x.shape[-1])


def gdn_inputs(p, conv_w, a_log, dt_bias):
    q, k, v, z, ba = p
    b, n, _ = q.shape
    qkv = jax.nn.silu(depthwise_conv(jnp.concatenate([q, k, v], axis=-1), conv_w))
    nk = GD_HEADS * GD_DK
    q = l2norm(heads(qkv[..., :nk], GD_HEADS)) * GD_DK ** -0.5
    k = l2norm(heads(qkv[..., nk:2 * nk], GD_HEADS))
    v = heads(qkv[..., 2 * nk:], GD_HEADS)
    g = jnp.moveaxis(ba.astype(F32).reshape(b, n, 4, GD_HEADS), 1, -1)
    a = -jnp.exp(a_log.astype(F32))[..., None]
    dtb = dt_bias.astype(F32)[..., None]
    fwd = (jax.nn.sigmoid(g[:, 0]), a[0] * jax.nn.softplus(g[:, 1] + dtb[0]))
    bwd = (jax.nn.sigmoid(g[:, 2]), a[1] * jax.nn.softplus(g[:, 3] + dtb[1]))
    return q, k, v, z, fwd, bwd


def gdn_scan(q, k, v, beta, g, state):
    out_dtype = v.dtype
    size = GD_CHUNK
    q, k, v = (to_chunks(a.astype(F32), size) for a in (q, k, v))
    beta, g = to_chunks(beta, size), to_chunks(g, size)
    gc = jnp.cumsum(g, axis=-1)
    tril = jnp.tril(jnp.ones((size, size), dtype=bool))
    strict = jnp.tril(jnp.ones((size, size), dtype=bool), -1)
    decay = jnp.exp(jnp.where(tril, gc[..., :, None] - gc[..., None, :], -jnp.inf))
    kb = k * beta[..., None]
    t_mat = jnp.where(strict, jnp.einsum('...id,...jd->...ij', kb, k) * decay, 0.0) + jnp.eye(size, dtype=F32)
    u = lax.linalg.triangular_solve(t_mat, v * beta[..., None], left_side=True, lower=True, unit_diagonal=True)
    w = lax.linalg.triangular_solve(t_mat, kb * jnp.exp(gc)[..., None], left_side=True, lower=True, unit_diagonal=True)
    attn = jnp.einsum('...id,...jd->...ij', q, k) * decay

    def step(s_mat, inp):
        qc, kc, uc, wc, ac, gcc = inp
        v_new = uc - jnp.einsum('bhld,bhde->bhle', wc, s_mat)
        o = jnp.einsum('bhld,bhde->bhle', qc * jnp.exp(gcc)[..., None], s_mat) + jnp.einsum('bhij,bhje->bhie', ac, v_new)
        g_end = gcc[..., -1]
        s_mat = s_mat * jnp.exp(g_end)[..., None, None] + jnp.einsum(
            'bhld,bhle->bhde', kc * jnp.exp(g_end[..., None] - gcc)[..., None], v_new)
        return s_mat, o

    state, o = lax.scan(step, state, (q, k, u, w, attn, gc))
    return from_chunks(o).astype(out_dtype), state


def bidirectional(scan_fn, ctx_f, lat_f, ctx_b, lat_b, init):
    flip = lambda t: tuple(jnp.flip(a, 2) for a in t)
    hc_f, s_f = scan_fn(*ctx_f, init)
    hx_f, _ = scan_fn(*lat_f, s_f)
    hc_b, s_b = scan_fn(*flip(ctx_b), init)
    hx_b, _ = scan_fn(*flip(lat_b), s_b)
    return hc_f + jnp.flip(hc_b, 2), hx_f + jnp.flip(hx_b, 2)


def mixer_layer(hc, hx, rope, need_ctx, w_in, q_norm, w_uq, kv_norm, w_ukv, mla_norm,
                ml_bias, ml_norm, conv_w, a_log, dt_bias, gd_norm, w_out):
    pc = split_cols(hc @ w_in)
    px = split_cols(hx @ w_in)
    b = hx.shape[0]
    q_c, k_c, v_c = mla_qkv(pc[0], pc[1], pc[2], q_norm, w_uq, kv_norm, w_ukv, None)
    q_x, k_x, v_x = mla_qkv(px[0], px[1], px[2], q_norm, w_uq, kv_norm, w_ukv, rope)
    a_x = blocked_attention(q_x, jnp.concatenate([k_c, k_x], axis=2), jnp.concatenate([v_c, v_x], axis=2))
    mq_c, mk_c, mv_c, o_c, mf_c, mb_c = mlstm_inputs(pc[3:8], ml_bias)
    mq_x, mk_x, mv_x, o_x, mf_x, mb_x = mlstm_inputs(px[3:8], ml_bias)
    ml_init = (jnp.zeros((b, ML_HEADS, ML_DK, ML_DV), F32), jnp.zeros((b, ML_HEADS, ML_DK), F32),
               jnp.full((b, ML_HEADS), M_INIT, F32))
    b_c, b_x = bidirectional(mlstm_scan, (mq_c, mk_c, mv_c) + mf_c, (mq_x, mk_x, mv_x) + mf_x,
                             (mq_c, mk_c, mv_c) + mb_c, (mq_x, mk_x, mv_x) + mb_x, ml_init)
    gq_c, gk_c, gv_c, z_c, gf_c, gb_c = gdn_inputs(pc[8:13], conv_w, a_log, dt_bias)
    gq_x, gk_x, gv_x, z_x, gf_x, gb_x = gdn_inputs(px[8:13], conv_w, a_log, dt_bias)
    gd_init = jnp.zeros((b, GD_HEADS, GD_DK, GD_DV), F32)
    g_c, g_x = bidirectional(gdn_scan, (gq_c, gk_c, gv_c) + gf_c, (gq_x, gk_x, gv_x) + gf_x,
                             (gq_c, gk_c, gv_c) + gb_c, (gq_x, gk_x, gv_x) + gb_x, gd_init)

    def merge(a, m, g, o, z):
        ya = head_rmsnorm(merge_heads(a), mla_norm.reshape(MLA_HEADS, V_HEAD), MLA_HEADS)
        ym = head_rmsnorm(merge_heads(m), ml_norm.reshape(ML_HEADS, ML_DV), ML_HEADS) * jax.nn.sigmoid(o)
        yg = head_rmsnorm(merge_heads(g), gd_norm, GD_HEADS) * jax.nn.silu(z)
        return jnp.concatenate([ya, ym, yg], axis=-1) @ w_out

    y_x = merge(a_x, b_x, g_x, o_x, z_x)
    y_c = merge(softmax_attention(q_c, k_c, v_c), b_c, g_c, o_c, z_c) if need_ctx else None
    return y_c, y_x


def squared_relu_mlp(h, w1, w2):
    return jnp.square(jax.nn.relu(h @ w1)) @ w2


def setup_inputs(seed: int = 0) -> dict:
    key = jax.random.key(seed)
    ks = jax.random.split(key, 26)
    nrm = lambda k, shape, s: jax.random.normal(k, shape, F32) * s
    gain = lambda k, shape: 1.0 + 0.02 * jax.random.normal(k, shape, F32)
    L, D = DEPTH, D_MODEL
    ig_bias = nrm(ks[14], (L, 2, ML_HEADS), 0.1)
    fg_bias = 3.0 + nrm(ks[15], (L, 2, ML_HEADS), 0.5)
    ml_gate_bias = jnp.stack([ig_bias[:, 0], fg_bias[:, 0], ig_bias[:, 1], fg_bias[:, 1]], axis=1).reshape(L, 4 * ML_HEADS)
    dt = jnp.exp(jax.random.uniform(ks[19], (L, 2, GD_HEADS), F32, math.log(1e-3), math.log(1e-1)))
    return {
        'x': nrm(ks[0], (BATCH, SEQ, D), 1.0),
        'c': nrm(ks[1], (BATCH, D), 1.0),
        'ctx': nrm(ks[2], (BATCH, CTX_LEN, D), 1.0),
        'c_ctx': nrm(ks[3], (D,), 1.0),
        'w_ada': nrm(ks[4], (L, D, 6 * D), 0.5 * D ** -0.5),
        'b_ada': nrm(ks[5], (L, 6 * D), 0.02),
        'norm1': gain(ks[6], (L, D)),
        'norm2': gain(ks[7], (L, D)),
        'w_in': nrm(ks[8], (L, D, D_IN), D ** -0.5),
        'mla_q_norm': gain(ks[9], (L, Q_LORA)),
        'mla_w_uq': nrm(ks[10], (L, Q_LORA, MLA_HEADS * (QK_NOPE + QK_ROPE)), Q_LORA ** -0.5),
        'mla_kv_norm': gain(ks[11], (L, KV_LORA)),
        'mla_w_ukv': nrm(ks[12], (L, KV_LORA, MLA_HEADS * (QK_NOPE + V_HEAD)), KV_LORA ** -0.5),
        'mla_out_norm': gain(ks[13], (L, MLA_W)),
        'ml_gate_bias': ml_gate_bias,
        'ml_out_norm': gain(ks[16], (L, ML_W)),
        'gd_conv': nrm(ks[17], (L, CONV_W, 2 * GD_HEADS * GD_DK + GD_W), CONV_W ** -0.5),
        'gd_a_log': jnp.log(jax.random.uniform(ks[18], (L, 2, GD_HEADS), F32, 1.0, 16.0)),
        'gd_dt_bias': dt + jnp.log(-jnp.expm1(-dt)),
        'gd_out_norm': gain(ks[20], (L, GD_DV)),
        'w_out': nrm(ks[21], (L, MIX_W, D), MIX_W ** -0.5),
        'w_mlp1': nrm(ks[22], (L, D, D_FF), D ** -0.5),
        'w_mlp2': nrm(ks[23], (L, D_FF, D), D_FF ** -0.5),
        'final_norm': gain(ks[24], (D,)),
    }


def reference(x, c, ctx, c_ctx, w_ada, b_ada, norm1, norm2, w_in, mla_q_norm, mla_w_uq,
              mla_kv_norm, mla_w_ukv, mla_out_norm, ml_gate_bias, ml_out_norm, gd_conv,
              gd_a_log, gd_dt_bias, gd_out_norm, w_out, w_mlp1, w_mlp2, final_norm):
    rope = rope_angles(x.shape[1])
    s_lat = jax.nn.silu(c)
    s_ctx = jax.nn.silu(c_ctx)[None]
    xc = ctx
    for l in range(DEPTH):
        last = l == DEPTH - 1
        mx = jnp.split((s_lat @ w_ada[l] + b_ada[l])[:, None, :], 6, axis=-1)
        mc = jnp.split((s_ctx @ w_ada[l] + b_ada[l])[:, None, :], 6, axis=-1)
        hx = modulate(rmsnorm(x, norm1[l]), mx[0], mx[1])
        hc = modulate(rmsnorm(xc, norm1[l]), mc[0], mc[1])
        y_c, y_x = mixer_layer(hc, hx, rope, not last, w_in[l], mla_q_norm[l], mla_w_uq[l],
                               mla_kv_norm[l], mla_w_ukv[l], mla_out_norm[l], ml_gate_bias[l],
                               ml_out_norm[l], gd_conv[l], gd_a_log[l], gd_dt_bias[l],
                               gd_out_norm[l], w_out[l])
        x = x + mx[2] * y_x
        x = x + mx[5] * squared_relu_mlp(modulate(rmsnorm(x, norm2[l]), mx[3], mx[4]), w_mlp1[l], w_mlp2[l])
        if not last:
            xc = xc + mc[2] * y_c
            xc = xc + mc[5] * squared_relu_mlp(modulate(rmsnorm(xc, norm2[l]), mc[3], mc[4]), w_mlp1[l], w_mlp2[l])
    return rmsnorm(x, final_norm)
```

```python
import numpy as np
from contextlib import ExitStack
import concourse.bass as bass
import concourse.mybir as mybir
from concourse.bass_utils import run_bass_kernel_spmd

F32 = mybir.dt.float32
BF16 = mybir.dt.bfloat16
AF = mybir.ActivationFunctionType
ALU = mybir.AluOpType
AX = mybir.AxisListType

SEM_LIMIT = 30000
NDS = 24


class Buf:
    __slots__ = ("name", "w", "r", "excl")

    def __init__(self, name="", excl=False):
        self.name = name
        self.w = None
        self.r = {}
        self.excl = excl


class Prog:
    ENG = ("pe", "act", "dve", "pool", "sp")

    def __init__(self):
        self.nc = bass.Bass("TRN2", target_bir_lowering=False)
        self.es = ExitStack()
        nc = self.nc
        self.eng = dict(pe=nc.tensor, act=nc.scalar, dve=nc.vector, pool=nc.gpsimd, sp=nc.sync)
        self.sem = {}
        self.cnt = {}
        self.nsem = 0
        for e in self.ENG:
            self._newsem(e)
        self.known = {e: {} for e in self.ENG}
        self.dsems = [self.es.enter_context(nc.semaphore(f"dq{i}")) for i in range(NDS)]
        self.dcnt = [0] * NDS
        self.dq_range = dict(sp=(0, 14), pool=(14, 22), act=(22, 24))
        self.dq_next = dict(sp=0, pool=14, act=22)
        self.ninst = 0
        self.last = {}

    def _newsem(self, e):
        self.sem[e] = self.es.enter_context(self.nc.semaphore(f"s_{e}_{self.nsem}"))
        self.nsem += 1
        self.cnt[e] = 0

    def _wait(self, eng, tok):
        sem, val = tok[1], tok[2]
        k = self.known[eng]
        if k.get(id(sem), 0) >= val:
            return
        self.eng[eng].wait_ge(sem, val)
        k[id(sem)] = val

    def _deps(self, eng, reads, writes):
        for b in reads:
            if b.w is not None:
                if not (b.w[0] == "pe" and eng == "pe"):
                    self._wait(eng, b.w)
            if b.excl:
                for t in b.r.values():
                    if t[0] != eng:
                        self._wait(eng, t)
        for b in writes:
            if b.w is not None:
                if not (b.w[0] == "pe" and eng == "pe"):
                    self._wait(eng, b.w)
            for t in b.r.values():
                if not (t[0] == "pe" and eng == "pe"):
                    self._wait(eng, t)

    def _mark(self, tok, reads, writes):
        key = (tok[0], id(tok[1]))
        for b in reads:
            b.r[key] = tok
        for b in writes:
            b.w = tok
            b.r = {}

    def op(self, eng, fn, reads=(), writes=()):
        self._deps(eng, reads, writes)
        inst = fn(self.eng[eng])
        self.cnt[eng] += 1
        inst.then_inc(self.sem[eng], 1)
        tok = (eng, self.sem[eng], self.cnt[eng])
        self.last[eng] = tok
        self._mark(tok, reads, writes)
        if self.cnt[eng] >= SEM_LIMIT:
            self._newsem(eng)
        self.ninst += 1
        return inst

    def dma(self, q, out, in_, reads=(), writes=()):
        self._deps(q, reads, writes)
        s = self.dq_next[q]
        lo, hi = self.dq_range[q]
        self.dq_next[q] = lo + (s + 1 - lo) % (hi - lo)
        if self.dcnt[s] > 0:
            self._wait(q, ("dma", self.dsems[s], self.dcnt[s]))
        if self.dcnt[s] >= SEM_LIMIT:
            self.dsems[s] = self.es.enter_context(self.nc.semaphore(f"dq{s}_{self.nsem}"))
            self.nsem += 1
            self.dcnt[s] = 0
        self.dcnt[s] += 16
        self.eng[q].dma_start(out=out, in_=in_).then_inc(self.dsems[s], 16)
        tok = ("dma", self.dsems[s], self.dcnt[s])
        self._mark(tok, reads, writes)
        self.ninst += 1

    def coll(self, kind, ins, outs, groups, reads=(), writes=()):
        self._deps("pool", reads, writes)
        if not hasattr(self, "csem"):
            self.csem = self.es.enter_context(self.nc.semaphore("ccsem"))
            self.ccnt = 0
        self.ccnt += 1
        self.nc.gpsimd.collective_compute(kind, ALU.bypass, replica_groups=groups, ins=list(ins), outs=list(outs)).then_inc(self.csem, 1)
        tok = ("cc", self.csem, self.ccnt)
        self._mark(tok, reads, writes)

    def finish(self, bufs=()):
        for s in range(NDS):
            if self.dcnt[s] > 0:
                self._wait("sp", ("dma", self.dsems[s], self.dcnt[s]))
        for e in ("pe", "act", "dve", "pool"):
            if e in self.last:
                self._wait("sp", self.last[e])
        if hasattr(self, "csem") and self.ccnt > 0:
            self._wait("sp", ("cc", self.csem, self.ccnt))

    def sb(self, name, shape, dtype, stack=None):
        self.uid = getattr(self, "uid", 0) + 1
        return (stack or self.es).enter_context(self.nc.sbuf_tensor(f"s_{name}_{self.uid}", list(shape), dtype))

    def ps(self, name, shape, dtype, stack=None):
        self.uid = getattr(self, "uid", 0) + 1
        return (stack or self.es).enter_context(self.nc.psum_tensor(f"p_{name}_{self.uid}", list(shape), dtype))

    def dram(self, name, shape, dtype, kind):
        return self.nc.dram_tensor(name, list(shape), dtype, kind=kind)


D = 2048
NT_CORE = 1088
NTOK = 4352
DFF = 8192
EPS = 1e-6
DTILES = [(64, 0)] + [(128, 64 + 128 * i) for i in range(8)]


def bc_mid(ap2d, n):
    p, k = ap2d.shape
    return ap2d.unsqueeze(2).to_broadcast([p, k, n])


class Dense:
    def __init__(self, P, stack):
        self.P = P
        sb = lambda n, s, d: P.sb(n, s, d, stack)
        self.xs = sb("xs", [128, 9, D], F32)
        self.xb = [Buf(f"x{i}") for i in range(9)]
        self.hm = sb("hm", [128, 16, NT_CORE], BF16)
        self.hmb = Buf("hm")
        self.wA = [sb(f"wA{i}", [128, 16, 512], BF16) for i in range(2)]
        self.wAb = [Buf(f"wA{i}") for i in range(2)]
        self.wB = sb("wB", [128, 4, D], BF16)
        self.wBb = Buf("wB")
        self.gl = sb("gate_l", [128, D], F32)
        self.gc = sb("gate_c", [128, D], F32)
        self.glb = Buf("gl")
        self.gcb = Buf("gc")
        self.xn = sb("xn", [128, D], F32)
        self.xnb = Buf("xn")
        self.a1 = [sb(f"a1T{i}", [128, 4, 512], BF16) for i in range(3)]
        self.a1b = [Buf(f"a1T{i}") for i in range(3)]
        self.tmp = [sb(f"dtmp{i}", [128, 512], F32) for i in range(4)]
        self.tmpb = [Buf(f"dtmp{i}") for i in range(4)]
        self.idf = sb("idf", [128, 128], F32)
        self.idfb = Buf("idf")
        self.fm = sb("fmvec", [128, 16, 16], F32)
        self.fmb = Buf("fm")
        self.st = sb("stat", [128, 16], F32)
        self.stb = Buf("stat")
        self.pT = P.ps("pT", [128, 16, 128], F32, stack)
        self.pTb = Buf("pT", True)
        self.pm = [P.ps(f"pm{i}", [128, 512], F32, stack) for i in range(4)]
        self.pmb = [Buf(f"pm{i}", True) for i in range(4)]
        self.pmi = 0
        self.tmi = 0

    def load_ident(self, ident_d):
        self.P.dma("sp", self.idf[:], ident_d, writes=[self.idfb])

    def load_x(self, x_d):
        P = self.P
        for ti, (r, off) in enumerate(DTILES):
            P.dma("sp", self.xs[0:r, ti, :], x_d[off:off + r, :], writes=[self.xb[ti]])

    def store_x(self, x_d):
        P = self.P
        for ti, (r, off) in enumerate(DTILES):
            P.dma("sp", x_d[off:off + r, :], self.xs[0:r, ti, :], reads=[self.xb[ti]])

    def load_fm_vec(self, slot, src1d):
        P = self.P
        with P.nc.allow_non_contiguous_dma(reason="small vector"):
            P.dma("sp", self.fm[:, slot, :], src1d.rearrange("(k p) -> p k", p=128), writes=[self.fmb])

    def prep_mod(self, norm_d, shift_l, scale_l, shift_c, scale_c):
        P = self.P
        fm = self.fm
        self.load_fm_vec(4, norm_d)
        self.load_fm_vec(5, scale_l)
        self.load_fm_vec(1, shift_l)
        self.load_fm_vec(6, scale_c)
        self.load_fm_vec(3, shift_c)
        P.op("dve", lambda e: e.scalar_tensor_tensor(fm[:, 0, :], fm[:, 5, :], 1.0, fm[:, 4, :], ALU.add, ALU.mult),
             reads=[self.fmb], writes=[self.fmb])
        P.op("dve", lambda e: e.scalar_tensor_tensor(fm[:, 2, :], fm[:, 6, :], 1.0, fm[:, 4, :], ALU.add, ALU.mult),
             reads=[self.fmb], writes=[self.fmb])

    def norm_mod_T(self):
        P = self.P
        xs, xn, st, pT, hm, fm, idf = self.xs, self.xn, self.st, self.pT, self.hm, self.fm, self.idf
        for ti, (r, off) in enumerate(DTILES):
            xb = self.xb[ti]
            P.op("act", lambda e: e.activation(xn[0:r, :], xs[0:r, ti, :], AF.Square, accum_out=st[0:r, 0:1]),
                 reads=[xb], writes=[self.xnb, self.stb])
            P.op("act", lambda e: e.activation(st[0:r, 1:2], st[0:r, 0:1], AF.Sqrt, bias=EPS, scale=1.0 / D),
                 reads=[self.stb], writes=[self.stb])
            P.op("dve", lambda e: e.reciprocal(st[0:r, 2:3], st[0:r, 1:2]), reads=[self.stb], writes=[self.stb])
            P.op("act", lambda e: e.activation(xn[0:r, :], xs[0:r, ti, :], AF.Copy, scale=st[0:r, 2:3]),
                 reads=[xb, self.stb], writes=[self.xnb])
            for k in range(16):
                P.op("pe", lambda e: e.transpose(pT[:, k, 0:r], xn[0:r, k * 128:(k + 1) * 128], idf[0:r, 0:r]),
                     reads=[self.xnb, self.idfb], writes=[self.pTb])
            gs, ss = (2, 3) if ti == 0 else (0, 1)
            t3 = self.xn[:].rearrange("p (k t) -> p k t", k=16)
            P.op("dve", lambda e: e.tensor_tensor(t3[:, :, 0:r], pT[:, :, 0:r], bc_mid(fm[:, gs, :], r), ALU.mult),
                 reads=[self.pTb, self.fmb], writes=[self.xnb])
            P.op("pool", lambda e: e.tensor_tensor(hm[:, :, off:off + r], t3[:, :, 0:r], bc_mid(fm[:, ss, :], r), ALU.add),
                 reads=[self.xnb, self.fmb], writes=[self.hmb])

    def store_hT(self, hT_d):
        P = self.P
        if isinstance(hT_d, list):
            for k in range(6):
                P.dma("sp", hT_d[k].rearrange("(b p) t -> p b t", p=128), self.hm[:, 3 * k:3 * k + CHK[k], :], reads=[self.hmb])
            return
        P.dma("sp", hT_d.rearrange("(k p) t -> p k t", p=128), self.hm[:], reads=[self.hmb])

    def load_mixT(self, mixT_d):
        P = self.P
        P.dma("sp", self.hm[:], mixT_d.rearrange("(k p) t -> p k t", p=128), writes=[self.hmb])

    def load_gates(self, g_l, g_c):
        P = self.P
        P.dma("sp", self.gl[:], g_l.partition_broadcast(128), writes=[self.glb])
        P.dma("sp", self.gc[:], g_c.partition_broadcast(128), writes=[self.gcb])

    def _acc(self, ps, psb, ti, r, cc, prescaled=False):
        P = self.P
        if prescaled:
            xsl = self.xs[0:r, ti, cc * 512:(cc + 1) * 512]
            P.op("dve", lambda e: e.tensor_tensor(xsl, ps[0:r, :], xsl, ALU.add), reads=[psb, self.xb[ti]], writes=[self.xb[ti]])
            return
        g, gb = (self.gc, self.gcb) if ti == 0 else (self.gl, self.glb)
        j = self.tmi
        self.tmi = (j + 1) % 4
        tmp, tb = self.tmp[j], self.tmpb[j]
        P.op("dve", lambda e: e.tensor_tensor(tmp[0:r, :], ps[0:r, :], g[0:r, cc * 512:(cc + 1) * 512], ALU.mult),
             reads=[psb, gb], writes=[tb])
        xsl = self.xs[0:r, ti, cc * 512:(cc + 1) * 512]
        P.op("pool", lambda e: e.tensor_tensor(xsl, xsl, tmp[0:r, :], ALU.add), reads=[tb, self.xb[ti]], writes=[self.xb[ti]])

    def _nextpm(self):
        i = self.pmi
        self.pmi = (i + 1) % 4
        return self.pm[i], self.pmb[i]

    def wout(self, w_d):
        P = self.P
        for cc in range(4):
            wa, wab = self.wA[cc % 2], self.wAb[cc % 2]
            P.dma("pool", wa[:], w_d[:, cc * 512:(cc + 1) * 512].rearrange("(k p) c -> p k c", p=128), writes=[wab])
            for ti, (r, off) in enumerate(DTILES):
                if ti == 1:
                    P.op("pool", lambda e: e.tensor_tensor(wa[:], wa[:], bc_midrep(self.gl[:, cc * 512:(cc + 1) * 512], 16), ALU.mult),
                         reads=[wab, self.glb], writes=[wab])
                ps, psb = self._nextpm()
                for k in range(16):
                    P.op("pe", lambda e: e.matmul(ps[0:r, :], self.hm[:, k, off:off + r], wa[:, k, :], start=(k == 0), stop=(k == 15)),
                         reads=[self.hmb, wab], writes=[psb])
                self._acc(ps, psb, ti, r, cc, prescaled=(ti >= 1))

    def mlp(self, w1_d, w2_d, nslices=16):
        P = self.P
        blocks = [(0, 64, [0]), (64, 512, [1, 2, 3, 4]), (576, 512, [5, 6, 7, 8])]
        for s in range(nslices):
            wa, wab = self.wA[s % 2], self.wAb[s % 2]
            P.dma("pool", wa[:], w1_d[:, s * 512:(s + 1) * 512].rearrange("(k p) c -> p k c", p=128), writes=[wab])
            P.dma("pool", self.wB[:], w2_d[s * 512:(s + 1) * 512, :].rearrange("(f p) c -> p f c", p=128), writes=[self.wBb])
            for bi, (boff, bn, tis) in enumerate(blocks):
                a1, a1b = self.a1[bi], self.a1b[bi]
                for f in range(4):
                    ps, psb = self._nextpm()
                    for k in range(16):
                        P.op("pe", lambda e: e.matmul(ps[:, 0:bn], wa[:, k, f * 128:(f + 1) * 128], self.hm[:, k, boff:boff + bn],
                                                      start=(k == 0), stop=(k == 15)),
                             reads=[self.hmb, wab], writes=[psb])
                    j = self.tmi
                    self.tmi = (j + 1) % 4
                    tmp, tb = self.tmp[j], self.tmpb[j]
                    P.op("act", lambda e: e.activation(tmp[:, 0:bn], ps[:, 0:bn], AF.Relu), reads=[psb], writes=[tb])
                    P.op("dve", lambda e: e.tensor_tensor(a1[:, f, 0:bn], tmp[:, 0:bn], tmp[:, 0:bn], ALU.mult), reads=[tb], writes=[a1b])
            for bi, (boff, bn, tis) in enumerate(blocks):
                a1, a1b = self.a1[bi], self.a1b[bi]
                if bi == 1:
                    P.op("pool", lambda e: e.tensor_tensor(self.wB[:], self.wB[:], bc_midrep(self.gl[:], 4), ALU.mult),
                         reads=[self.wBb, self.glb], writes=[self.wBb])
                for ti in tis:
                    r, off = DTILES[ti]
                    lo = off - boff
                    for cc in range(4):
                        ps, psb = self._nextpm()
                        for f in range(4):
                            P.op("pe", lambda e: e.matmul(ps[0:r, :], a1[:, f, lo:lo + r], self.wB[:, f, cc * 512:(cc + 1) * 512],
                                                          start=(f == 0), stop=(f == 3)),
                                 reads=[a1b, self.wBb], writes=[psb])
                        self._acc(ps, psb, ti, r, cc, prescaled=(ti >= 1))

    def final_norm(self, fn_d, y_d):
        P = self.P
        xs, xn, st = self.xs, self.xn, self.st
        P.dma("sp", self.gl[:], fn_d.partition_broadcast(128), writes=[self.glb])
        for ti in range(1, 9):
            r, off = DTILES[ti]
            xb = self.xb[ti]
            P.op("act", lambda e: e.activation(xn[0:r, :], xs[0:r, ti, :], AF.Square, accum_out=st[0:r, 0:1]),
                 reads=[xb], writes=[self.xnb, self.stb])
            P.op("act", lambda e: e.activation(st[0:r, 1:2], st[0:r, 0:1], AF.Sqrt, bias=EPS, scale=1.0 / D),
                 reads=[self.stb], writes=[self.stb])
            P.op("dve", lambda e: e.reciprocal(st[0:r, 2:3], st[0:r, 1:2]), reads=[self.stb], writes=[self.stb])
            P.op("dve", lambda e: e.scalar_tensor_tensor(xn[0:r, :], xs[0:r, ti, :], st[0:r, 2:3], self.gl[0:r, :], ALU.mult, ALU.mult),
                 reads=[xb, self.stb, self.glb], writes=[self.xnb])
            P.dma("sp", y_d[off - 64:off - 64 + r, :], xn[0:r, :], reads=[self.xnb])


def build_dense(mode, nslices=16):
    P = Prog()
    ident = P.dram("ident", [128, 128], F32, "ExternalInput")
    x_in = P.dram("x_in", [NT_CORE, D], F32, "ExternalInput")
    dn = Dense(P, P.es)
    dn.load_ident(ident[:])
    dn.load_x(x_in)
    if mode in ("C", "CF"):
        mixT = P.dram("mixT", [D, NT_CORE], BF16, "ExternalInput")
        modv = P.dram("modv", [2, 6, D], F32, "ExternalInput")
        norm2 = P.dram("norm2", [D], F32, "ExternalInput")
        w_out = P.dram("w_out", [D, D], F32, "ExternalInput")
        w1 = P.dram("w1", [D, DFF], F32, "ExternalInput")
        w2 = P.dram("w2", [DFF, D], F32, "ExternalInput")
        dn.load_mixT(mixT)
        dn.load_gates(modv[0, 2, :], modv[1, 2, :])
        dn.wout(w_out)
        dn.prep_mod(norm2[:], modv[0, 3, :], modv[0, 4, :], modv[1, 3, :], modv[1, 4, :])
        dn.norm_mod_T()
        dn.load_gates(modv[0, 5, :], modv[1, 5, :])
        dn.mlp(w1, w2, nslices)
    if mode in ("A", "C"):
        modn = P.dram("modn", [2, 6, D], F32, "ExternalInput")
        norm1 = P.dram("norm1", [D], F32, "ExternalInput")
        hT = P.dram("hT", [D, NT_CORE], BF16, "ExternalOutput")
        dn.prep_mod(norm1[:], modn[0, 0, :], modn[0, 1, :], modn[1, 0, :], modn[1, 1, :])
        dn.norm_mod_T()
        dn.store_hT(hT)
    if mode == "C":
        x_out = P.dram("x_out", [NT_CORE, D], F32, "ExternalOutput")
        dn.store_x(x_out)
    if mode == "CF":
        fn = P.dram("final_norm", [D], F32, "ExternalInput")
        y = P.dram("y", [1024, D], F32, "ExternalOutput")
        dn.final_norm(fn[:], y)
    P.finish()
    return P


def build_ada():
    P = Prog()
    cT = P.dram("cT", [D, 3], F32, "ExternalInput")
    wada = P.dram("wada", [4, D, 1536], F32, "ExternalInput")
    bada = P.dram("bada", [4, 1536], F32, "ExternalInput")
    modp = P.dram("modp", [4, 3, 1536], F32, "ExternalOutput")
    sT = P.sb("sT", [128, 16, 3], F32)
    sTb = Buf("sT")
    w = [P.sb(f"w{i}", [128, 16, 512], F32) for i in range(2)]
    wb = [Buf(f"w{i}") for i in range(2)]
    bt = [P.sb(f"bt{i}", [3, 512], F32) for i in range(2)]
    btb = [Buf(f"bt{i}") for i in range(2)]
    ot = [P.sb(f"ot{i}", [3, 512], F32) for i in range(2)]
    otb = [Buf(f"ot{i}") for i in range(2)]
    ps = [P.ps(f"ps{i}", [128, 512], F32) for i in range(2)]
    psb = [Buf(f"ps{i}", True) for i in range(2)]
    P.dma("sp", sT[:], cT.rearrange("(k p) r -> p k r", p=128), writes=[sTb])
    P.op("act", lambda e: e.activation(sT[:], sT[:], AF.Silu), reads=[sTb], writes=[sTb])
    i = 0
    for l in range(4):
        for cc in range(3):
            j = i % 2
            i += 1
            P.dma("sp", w[j][:], wada[l, :, cc * 512:(cc + 1) * 512].rearrange("(k p) c -> p k c", p=128), writes=[wb[j]])
            P.dma("sp", bt[j][:], bada[l, cc * 512:(cc + 1) * 512].partition_broadcast(3), writes=[btb[j]])
            for k in range(16):
                P.op("pe", lambda e: e.matmul(ps[j][0:3, :], sT[:, k, :], w[j][:, k, :], start=(k == 0), stop=(k == 15)),
                     reads=[sTb, wb[j]], writes=[psb[j]])
            P.op("dve", lambda e: e.tensor_tensor(ot[j][:], ps[j][0:3, :], bt[j][:], ALU.add), reads=[psb[j], btb[j]], writes=[otb[j]])
            P.dma("sp", modp[2 * l:2 * l + 2, cc * 512:(cc + 1) * 512], ot[j][:], reads=[otb[j]])
    P.finish()
    return P


TCH = [(0, 256)] + [(256 + 512 * i, 512) for i in range(8)]
NTILE = 34
ATT_SCALE = 192 ** -0.5


class Mixer:
    def __init__(self, P, stack):
        self.P = P
        self.stack = stack
        sb = lambda n, s, d: P.sb(n, s, d, stack)
        self.idf = sb("m_idf", [128, 128], F32)
        self.idb = sb("m_idb", [128, 128], BF16)
        self.onesb = sb("m_onesb", [128, 128], BF16)
        self.onesf = sb("m_onesf", [128, 128], F32)
        self.cb = Buf("consts")
        self.pb = [P.ps(f"mpb{i}", [128, 512], F32, stack) for i in range(7)]
        self.pbb = [Buf(f"mpb{i}", True) for i in range(7)]
        self.ptb = P.ps("mptb", [128, 128], BF16, stack)
        self.ptbb = Buf("mptb", True)
        self.pi = 0

    def consts(self, ident_d):
        P = self.P
        P.dma("sp", self.idf[:], ident_d, writes=[self.cb])
        P.dma("pool", self.idb[:], ident_d, writes=[self.cb])
        P.op("dve", lambda e: e.memset(self.onesb[:], 1.0), writes=[self.cb])
        P.op("dve", lambda e: e.memset(self.onesf[:], 1.0), writes=[self.cb])

    def hload(self, h, hb, c0, n, hT_d):
        P = self.P
        if not isinstance(hT_d, tuple):
            P.dma("sp", h[:, :, 0:n], hT_d[:, c0:c0 + n].rearrange("(k p) t -> p k t", p=128), writes=[hb])
            return
        gts, gbuf = hT_d
        if c0 == 0:
            pieces = [(r, 0, 64, r * 64) for r in range(4)]
        else:
            i = (c0 - 256) // 512
            pieces = [(i // 2, 64 + (i % 2) * 512, n, 0)]
        for (r, col0, m, dst0) in pieces:
            for c_ in range(6):
                nk = CHK[c_]
                P.dma("sp", h[:, 3 * c_:3 * c_ + nk, dst0:dst0 + m],
                      gts[c_][r * nk * 128:(r + 1) * nk * 128, col0:col0 + m].rearrange("(b p) t -> p b t", p=128),
                      reads=[gbuf], writes=[hb])

    def mix_store(self, mixT_d, row0, tok0, n, src, reads):
        P = self.P
        if not isinstance(mixT_d, tuple):
            P.dma("sp", mixT_d[row0:row0 + 128, tok0:tok0 + n], src[:, 0:n], reads=reads)
            return
        dt_ = mixT_d[1]
        t = tok0
        while t < tok0 + n:
            if t < 256:
                j, col, seg_end = t // 64, t % 64, (t // 64 + 1) * 64
            else:
                j, col, seg_end = (t - 256) // 1024, 64 + (t - 256) % 1024, 256 + ((t - 256) // 1024 + 1) * 1024
            m = min(seg_end, tok0 + n) - t
            c_, lk = mix_chunk((row0 // 128) * 4 + j)
            P.dma("sp", dt_[c_][lk * 128:lk * 128 + 128, col:col + m], src[:, t - tok0:t - tok0 + m], reads=reads)
            t += m

    def nb(self, lo=0, hi=7):
        n = hi - lo
        i = lo + (self.pi % n)
        self.pi += 1
        return self.pb[i], self.pbb[i]

    def mla(self, hT_d, win_d, wuq_d, wukv_d, qn_d, kvn_d, cos_d, sin_d, onorm_d, mixT_d):
        P = self.P
        with ExitStack() as st:
            sb = lambda n, s, d: P.sb(n, s, d, st)
            QN = sb("QN", [128, 2, NTOK], BF16)
            QPE = sb("QPE", [65, 2, NTOK], BF16)
            KN = sb("KN", [128, 2, NTOK], BF16)
            KPE = sb("KPE", [65, NTOK], BF16)
            V = sb("V", [128, NTILE, 2, 129], BF16)
            qb = [Buf(f"q{i}") for i in range(9)]
            kb = [Buf(f"k{i}") for i in range(9)]
            initb = Buf("init")
            gain = sb("ogain", [128, 2, 128], F32)
            gainb = Buf("ogain")
            qn = sb("qnfm", [128, 4], F32)
            kvn = sb("kvnfm", [128, 2], F32)
            nb_ = Buf("normfm")
            P.dma("sp", qn[:], qn_d, writes=[nb_])
            P.dma("sp", kvn[:], kvn_d, writes=[nb_])
            P.dma("sp", gain[:], onorm_d.partition_broadcast(128), writes=[gainb])
            P.op("pool", lambda e: e.memset(KPE[64:65, :], 1.0), writes=[initb])
            P.op("pool", lambda e: e.memset(QPE[64:65, :, :], 0.0), writes=[initb])
            P.op("pool", lambda e: e.memset(V[:, :, :, 128:129], 1.0), writes=[initb])
            with ExitStack() as st1:
                sb1 = lambda n, s, d: P.sb(n, s, d, st1)
                win = sb1("win", [128, 16, 896], BF16)
                wuq = sb1("wuq", [128, 4, 512], BF16)
                wukv = sb1("wukv", [128, 2, 512], BF16)
                wb = Buf("mlaw")
                P.dma("pool", win[:], win_d.rearrange("(k p) c -> p k c", p=128), writes=[wb])
                P.dma("pool", wuq[:], wuq_d.rearrange("(k p) c -> p k c", p=128), writes=[wb])
                P.dma("pool", wukv[:], wukv_d.rearrange("(k p) c -> p k c", p=128), writes=[wb])
                hc = [sb1(f"hc{i}", [128, 16, 512], BF16) for i in range(2)]
                hcb = [Buf(f"hc{i}") for i in range(2)]
                cs = [sb1(f"cs{i}", [64, 2, 512], F32) for i in range(2)]
                csb = [Buf(f"cs{i}") for i in range(2)]
                cq = sb1("cq", [128, 6, 512], F32)
                cqb = Buf("cq")
                sq = sb1("sq", [128, 6, 512], BF16)
                sqb = Buf("sq")
                cn = sb1("cn", [128, 6, 512], BF16)
                cnb = Buf("cn")
                rs = [sb1(f"rs{i}", [128, 512], F32) for i in range(2)]
                rsb = [Buf(f"rs{i}") for i in range(2)]
                rt = [sb1(f"rt{i}", [64, 512], F32) for i in range(4)]
                rtb = [Buf(f"rt{i}") for i in range(4)]
                rti = [0]

                def rope(psA, psAb, psB, psBb, n, cst, cstb, dst):
                    i = rti[0]
                    rti[0] = (i + 2) % 4
                    t1, t1b, t2, t2b = rt[i], rtb[i], rt[i + 1], rtb[i + 1]
                    P.op("dve", lambda e: e.tensor_tensor(t1[:, 0:n], psA[0:64, 0:n], cst[:, 0, 0:n], ALU.mult),
                         reads=[psAb, cstb], writes=[t1b])
                    P.op("dve", lambda e: e.tensor_tensor(t2[:, 0:n], psB[0:64, 0:n], cst[:, 1, 0:n], ALU.mult),
                         reads=[psBb, cstb], writes=[t2b])
                    return t1, t1b, t2, t2b

                import os as _os
                _nch = int(_os.environ.get("MLA_NCH", "9"))
                for ci, (c0, n) in enumerate(TCH[:_nch]):
                    h, hb = hc[ci % 2], hcb[ci % 2]
                    self.hload(h, hb, c0, n, hT_d)
                    ct, ctb = cs[ci % 2], csb[ci % 2]
                    P.dma("sp", ct[:, 0, 0:n], cos_d[:, c0:c0 + n], writes=[ctb])
                    P.dma("sp", ct[:, 1, 0:n], sin_d[:, c0:c0 + n], writes=[ctb])
                    _stg = int(_os.environ.get("MLA_STAGE", "9"))
                    if _stg < 1:
                        continue
                    for grp in range(6):
                        ps, psb = self.nb()
                        for k in range(16):
                            P.op("pe", lambda e: e.matmul(ps[:, 0:n], win[:, k, grp * 128:(grp + 1) * 128], h[:, k, 0:n],
                                                          start=(k == 0), stop=(k == 15)), reads=[wb, hb], writes=[psb])
                        P.op("dve", lambda e: e.tensor_copy(cq[:, grp, 0:n], ps[:, 0:n]), reads=[psb], writes=[cqb])
                        P.op("act", lambda e: e.activation(sq[:, grp, 0:n], ps[:, 0:n], AF.Square), reads=[psb], writes=[sqb])
                    _stg = int(_os.environ.get("MLA_STAGE", "9"))
                    if _stg < 2:
                        continue
                    for (g0, g1, nfeat, nv, r_, r_b) in ((0, 4, 512, qn, rs[0], rsb[0]), (4, 6, 256, kvn, rs[1], rsb[1])):
                        ps, psb = self.nb()
                        for grp in range(g0, g1):
                            P.op("pe", lambda e: e.matmul(ps[:, 0:n], self.onesb[:], sq[:, grp, 0:n], start=(grp == g0), stop=(grp == g1 - 1)),
                                 reads=[self.cb, sqb], writes=[psb])
                        P.op("act", lambda e: e.activation(r_[:, 0:n], ps[:, 0:n], AF.Sqrt, bias=EPS, scale=1.0 / nfeat),
                             reads=[psb], writes=[r_b])
                        P.op("dve", lambda e: e.reciprocal(r_[:, 0:n], r_[:, 0:n]), reads=[r_b], writes=[r_b])
                        for grp in range(g0, g1):
                            P.op("dve", lambda e: e.scalar_tensor_tensor(cn[:, grp, 0:n], cq[:, grp, 0:n], nv[:, grp - g0:grp - g0 + 1],
                                                                         r_[:, 0:n], ALU.mult, ALU.mult),
                                 reads=[cqb, nb_, r_b], writes=[cnb])
                    if _stg < 3:
                        continue
                    psA, psAb = self.nb()
                    psB, psBb = self.nb()
                    for (ps, psb, c_) in ((psA, psAb, 768), (psB, psBb, 832)):
                        for k in range(16):
                            P.op("pe", lambda e: e.matmul(ps[0:64, 0:n], win[:, k, c_:c_ + 64], h[:, k, 0:n], start=(k == 0), stop=(k == 15)),
                                 reads=[wb, hb], writes=[psb])
                    t1, t1b, t2, t2b = rope(psA, psAb, psB, psBb, n, ct, ctb, None)
                    P.op("pool", lambda e: e.tensor_tensor(KPE[0:64, c0:c0 + n], t1[:, 0:n], t2[:, 0:n], ALU.add),
                         reads=[t1b, t2b], writes=[kb[ci]])
                    if _stg < 4:
                        continue
                    for hh in range(2):
                        ps, psb = self.nb()
                        for k in range(4):
                            P.op("pe", lambda e: e.matmul(ps[:, 0:n], wuq[:, k, hh * 128:(hh + 1) * 128], cn[:, k, 0:n], start=(k == 0), stop=(k == 3)),
                                 reads=[wb, cnb], writes=[psb])
                        P.op("act", lambda e: e.copy(QN[:, hh, c0:c0 + n], ps[:, 0:n]), reads=[psb], writes=[qb[ci]])
                        psA, psAb = self.nb()
                        psB, psBb = self.nb()
                        for (ps, psb, c_) in ((psA, psAb, 256 + hh * 128), (psB, psBb, 256 + hh * 128 + 64)):
                            for k in range(4):
                                P.op("pe", lambda e: e.matmul(ps[0:64, 0:n], wuq[:, k, c_:c_ + 64], cn[:, k, 0:n], start=(k == 0), stop=(k == 3)),
                                     reads=[wb, cnb], writes=[psb])
                        t1, t1b, t2, t2b = rope(psA, psAb, psB, psBb, n, ct, ctb, None)
                        P.op("pool", lambda e: e.tensor_tensor(QPE[0:64, hh, c0:c0 + n], t1[:, 0:n], t2[:, 0:n], ALU.add),
                             reads=[t1b, t2b], writes=[qb[ci]])
                        ps, psb = self.nb()
                        for k in range(2):
                            P.op("pe", lambda e: e.matmul(ps[:, 0:n], wukv[:, k, hh * 128:(hh + 1) * 128], cn[:, 4 + k, 0:n], start=(k == 0), stop=(k == 1)),
                                 reads=[wb, cnb], writes=[psb])
                        P.op("dve", lambda e: e.tensor_copy(KN[:, hh, c0:c0 + n], ps[:, 0:n]), reads=[psb], writes=[kb[ci]])
                    if _stg < 5:
                        continue
                    for j in range(n // 128):
                        t = (c0 + j * 128) // 128
                        ps, psb = self.nb()
                        for k in range(2):
                            P.op("pe", lambda e: e.matmul(ps[:, 0:256], cn[:, 4 + k, j * 128:(j + 1) * 128], wukv[:, k, 256:512], start=(k == 0), stop=(k == 1)),
                                 reads=[wb, cnb], writes=[psb])
                        P.op("act", lambda e: e.copy(V[:, t, :, 0:128], ps[:, 0:256].rearrange("p (h d) -> p h d", h=2)),
                             reads=[psb], writes=[kb[ci]])
                self.barrier()
            import os
            if os.environ.get("MLA_DBG") == "1":
                P.dma("sp", mixT_d[0:128, :], QN[:, 0, :], reads=qb)
                P.dma("sp", mixT_d[128:256, :], KN[:, 0, :], reads=kb)
                P.dma("sp", mixT_d[256:320, :], QPE[0:64, 0, :], reads=qb)
                P.dma("sp", mixT_d[320:384, :], KPE[0:64, :], reads=kb)
                self.barrier()
                return
            with ExitStack() as st2:
                sb2 = lambda n, s, d: P.sb(n, s, d, st2)
                PT = [sb2(f"PT{i}", [128, 512], BF16) for i in range(3)]
                PTb = [Buf(f"PT{i}") for i in range(3)]
                ostg = [sb2(f"ostg{i}", [128, 512], BF16) for i in range(2)]
                ostgb = [Buf(f"ostg{i}") for i in range(2)]
                yb_ = [sb2(f"yb{i}", [128, 128], F32) for i in range(4)]
                ya_ = [sb2(f"ya{i}", [128, 128], BF16) for i in range(4)]
                sc_ = [sb2(f"sc{i}", [128, 8], F32) for i in range(4)]
                ybb = [Buf(f"yb{i}") for i in range(4)]
                junk = sb2("junk", [128, 128], F32)
                junkb = Buf("junk")
                pti = 0
                oi = 0
                qlist = [(256 + 512 * i, 512, 0, NTILE) for i in range(8)] + [(0, 256, 0, 2)]
                for hh in range(2):
                    for (q0, qn_, kt0, kt1) in qlist:
                        nqs = qn_ // 128
                        qci = 0 if q0 == 0 else 1 + (q0 - 256) // 512
                        def emit_qk(kt):
                            kci = 0 if kt < 2 else 1 + (kt - 2) // 4
                            ps, psb = self.nb(0, 3)
                            P.op("pe", lambda e: e.matmul(ps[:, 0:qn_], KN[:, hh, kt * 128:(kt + 1) * 128], QN[:, hh, q0:q0 + qn_], start=True, stop=False),
                                 reads=[kb[kci], qb[qci]], writes=[psb])
                            P.op("pe", lambda e: e.matmul(ps[:, 0:qn_], KPE[0:65, kt * 128:(kt + 1) * 128], QPE[0:65, hh, q0:q0 + qn_], start=False, stop=True),
                                 reads=[kb[kci], qb[qci], initb], writes=[psb])
                            return ps, psb, kci

                        nxt = emit_qk(kt0)
                        for kt in range(kt0, kt1):
                            ps, psb, kci = nxt
                            if kt + 1 < kt1:
                                nxt = emit_qk(kt + 1)
                            pt, ptb_ = PT[pti % 3], PTb[pti % 3]
                            pti += 1
                            P.op("act", lambda e: e.activation(pt[:, 0:qn_], ps[:, 0:qn_], AF.Exp, scale=ATT_SCALE), reads=[psb], writes=[ptb_])
                            for qs in range(nqs):
                                P.op("pe", lambda e: e.matmul(self.pb[3 + qs][:, 0:129], pt[:, qs * 128:(qs + 1) * 128], V[:, kt, hh, :],
                                                              start=(kt == kt0), stop=(kt == kt1 - 1)),
                                     reads=[ptb_, kb[kci], initb], writes=[self.pbb[3 + qs]])
                        og, ogb = ostg[oi % 2], ostgb[oi % 2]
                        oi += 1
                        for qs in range(nqs):
                            o, ob = self.pb[3 + qs], self.pbb[3 + qs]
                            sc, y, ya, yb2 = sc_[qs], yb_[qs], ya_[qs], ybb[qs]
                            P.op("dve", lambda e: e.reciprocal(sc[:, 0:1], o[:, 128:129]), reads=[ob], writes=[yb2])
                            P.op("dve", lambda e: e.tensor_scalar(y[:], o[:, 0:128], sc[:, 0:1], None, ALU.mult), reads=[ob, yb2], writes=[yb2])
                            P.op("act", lambda e: e.activation(junk[:], y[:], AF.Square, accum_out=sc[:, 1:2]), reads=[yb2], writes=[yb2, junkb])
                            P.op("act", lambda e: e.activation(sc[:, 2:3], sc[:, 1:2], AF.Sqrt, bias=EPS, scale=1.0 / 128), reads=[yb2], writes=[yb2])
                            P.op("dve", lambda e: e.reciprocal(sc[:, 3:4], sc[:, 2:3]), reads=[yb2], writes=[yb2])
                            P.op("dve", lambda e: e.scalar_tensor_tensor(ya[:], y[:], sc[:, 3:4], gain[:, hh, :], ALU.mult, ALU.mult),
                                 reads=[yb2, gainb], writes=[yb2])
                            P.op("pe", lambda e: e.transpose(self.ptb[:], ya[:], self.idb[:]), reads=[yb2, self.cb], writes=[self.ptbb])
                            P.op("dve", lambda e: e.tensor_copy(og[:, qs * 128:(qs + 1) * 128], self.ptb[:]), reads=[self.ptbb], writes=[ogb])
                        self.mix_store(mixT_d, hh * 128, q0, qn_, og, [ogb])
                self.barrier()

    def barrier(self):
        P = self.P
        toks = []
        for e in ("pe", "act", "dve", "pool"):
            if e in P.last:
                toks.append(P.last[e])
        for s in range(NDS):
            if P.dcnt[s] > 0:
                toks.append(("dma", P.dsems[s], P.dcnt[s]))
        for e in P.ENG:
            for t in toks:
                if t[0] != e:
                    P._wait(e, t)


def rope_tables():
    n = 4096
    t = np.arange(n)
    row = (t // 64).astype(np.float32)
    col = (t % 64).astype(np.float32)
    half = 32
    inv = (np.float32(10000.0) ** (-np.arange(0, half, 2, dtype=np.float32) / np.float32(half))).astype(np.float32)
    ar = (row[:, None] * inv).astype(np.float32)
    ac = (col[:, None] * inv).astype(np.float32)
    cos = np.ones((64, NTOK), np.float32)
    sin = np.zeros((64, NTOK), np.float32)
    for base, a in ((0, ar), (32, ac)):
        c, s = np.cos(a).T.astype(np.float32), np.sin(a).T.astype(np.float32)
        cos[base:base + 16, 256:] = c
        cos[base + 16:base + 32, 256:] = c
        sin[base:base + 16, 256:] = -s
        sin[base + 16:base + 32, 256:] = s
    return cos, sin


ROPE_SW = np.concatenate([np.arange(16, 32), np.arange(0, 16), np.arange(48, 64), np.arange(32, 48)])


def fm(v):
    return np.ascontiguousarray(v.reshape(-1, 128).T)


def mixer_mla_inputs(inp, l, g):
    w_in = inp["w_in"][l]
    kpe = w_in[:, 768:832]
    win = np.concatenate([w_in[:, 0:768], kpe, kpe[:, ROPE_SW]], axis=1)
    wuq = inp["mla_w_uq"][l]
    cols = []
    for hh in range(2):
        cols.append(wuq[:, (2 * g + hh) * 192:(2 * g + hh) * 192 + 128])
    for hh in range(2):
        pe = wuq[:, (2 * g + hh) * 192 + 128:(2 * g + hh) * 192 + 192]
        cols += [pe, pe[:, ROPE_SW]]
    wuq_g = np.concatenate(cols, axis=1)
    wukv = inp["mla_w_ukv"][l]
    cols = [wukv[:, (2 * g + hh) * 256:(2 * g + hh) * 256 + 128] for hh in range(2)]
    cols += [wukv[:, (2 * g + hh) * 256 + 128:(2 * g + hh) * 256 + 256] for hh in range(2)]
    wukv_g = np.concatenate(cols, axis=1)
    return dict(win=np.ascontiguousarray(win), wuq=np.ascontiguousarray(wuq_g), wukv=np.ascontiguousarray(wukv_g),
                qn=fm(inp["mla_q_norm"][l]), kvn=fm(inp["mla_kv_norm"][l]),
                onorm=np.ascontiguousarray(inp["mla_out_norm"][l][2 * g * 128:(2 * g + 2) * 128]))


def build_mixer(parts=("mla", "rec")):
    P = Prog()
    ident = P.dram("ident", [128, 128], F32, "ExternalInput")
    hT = P.dram("hT", [D, NTOK], BF16, "ExternalInput")
    mixT = P.dram("mixT", [512, NTOK], BF16, "ExternalOutput")
    mx = Mixer(P, P.es)
    mx.consts(ident[:])
    if "mla" in parts:
        win = P.dram("win", [D, 896], F32, "ExternalInput")
        wuq = P.dram("wuq", [512, 512], F32, "ExternalInput")
        wukv = P.dram("wukv", [256, 512], F32, "ExternalInput")
        qn = P.dram("qn", [128, 4], F32, "ExternalInput")
        kvn = P.dram("kvn", [128, 2], F32, "ExternalInput")
        cos = P.dram("cos", [64, NTOK], F32, "ExternalInput")
        sin = P.dram("sin", [64, NTOK], F32, "ExternalInput")
        onorm = P.dram("onorm", [256], F32, "ExternalInput")
        mx.mla(hT, win, wuq, wukv, qn[:], kvn[:], cos, sin, onorm[:], mixT)
    if "rec" in parts or "ml" in parts or "gd" in parts:
        wfm = P.dram("wfm", [D, 512], F32, "ExternalInput")
        wtm = P.dram("wtm", [D, 456], F32, "ExternalInput")
        gbias = P.dram("gbias", [8], F32, "ExternalInput")
        convw = P.dram("convw", [128, 15], F32, "ExternalInput")
        mnorm = P.dram("mnorm", [128], F32, "ExternalInput")
        gnorm = P.dram("gnorm", [128], F32, "ExternalInput")
        cmd = P.dram("cm", [128, 22, 128], F32, "ExternalInput")
        mx.rec(hT, wfm, wtm, gbias[:], convw[:], mnorm[:], gnorm[:], cmd[:], mixT,
               do_ml=("rec" in parts or "ml" in parts), do_gd=("rec" in parts or "gd" in parts))
    P.finish()
    return P


NEGV = -1.0e30


def rec_consts():
    p = np.arange(128)[:, None]
    f = np.arange(128)[None, :]
    cm = np.zeros((128, 22, 128), np.float32)
    for k in range(7):
        sz = 1 << k
        mk = ((p // (2 * sz)) == (f // (2 * sz))) & ((p % (2 * sz)) >= sz) & ((f % (2 * sz)) < sz)
        cm[:, 8 + k] = mk
        cm[:, 15 + k] = mk.T
    cm[:, 0] = (p <= f)
    cm[:, 1] = (p >= f)
    cm[:, 2] = np.where(f <= p, 0.0, NEGV)
    cm[:, 3] = np.where(f < p, 0.0, NEGV)
    cm[:, 4] = np.where(f >= p, 0.0, NEGV)
    cm[:, 5] = np.where(f > p, 0.0, NEGV)
    cm[:, 6] = (p == 127) * np.ones((1, 128))
    cm[:, 7] = (p == 0) * np.ones((1, 128))
    return cm


def mixer_rec_inputs(inp, l, g):
    w_in = inp["w_in"][l]
    mq = w_in[:, 832 + g * 64:832 + (g + 1) * 64]
    mk = w_in[:, 1088 + g * 64:1088 + (g + 1) * 64]
    mv = w_in[:, 1344 + g * 128:1344 + (g + 1) * 128]
    mo = w_in[:, 1856 + g * 128:1856 + (g + 1) * 128]
    mg = w_in[:, [2368 + j * 4 + g for j in range(4)]]
    gq = w_in[:, 2384 + g * 128:2384 + (g + 1) * 128]
    gk = w_in[:, 2896 + g * 128:2896 + (g + 1) * 128]
    gv = w_in[:, 3408 + g * 128:3408 + (g + 1) * 128]
    gz = w_in[:, 3920 + g * 128:3920 + (g + 1) * 128]
    gg = w_in[:, [4432 + j * 4 + g for j in range(4)]]
    wfm = np.concatenate([mq, mk, gq, gk, gv], axis=1)
    wtm = np.concatenate([mk, mv, mo, gz, mg, gg], axis=1)
    mb = inp["ml_gate_bias"][l]
    gbias = np.array([mb[0 * 4 + g], mb[1 * 4 + g], mb[2 * 4 + g], mb[3 * 4 + g],
                      inp["gd_dt_bias"][l][0, g], inp["gd_dt_bias"][l][1, g],
                      inp["gd_a_log"][l][0, g], inp["gd_a_log"][l][1, g]], np.float32)
    cw = inp["gd_conv"][l]
    convw = np.concatenate([cw[:, j * 512 + g * 128:j * 512 + (g + 1) * 128].T for j in range(3)], axis=1)
    return dict(wfm=np.ascontiguousarray(wfm), wtm=np.ascontiguousarray(wtm), gbias=gbias, convw=np.ascontiguousarray(convw),
                mnorm=np.ascontiguousarray(inp["ml_out_norm"][l][g * 128:(g + 1) * 128]),
                gnorm=np.ascontiguousarray(inp["gd_out_norm"][l]), cm=rec_consts())


def bc_last(ap2d, n):
    p, k = ap2d.shape
    return ap2d.unsqueeze(2).to_broadcast([p, k, n])


def bc_midrep(ap2d, m):
    p, n = ap2d.shape
    return ap2d.unsqueeze(1).to_broadcast([p, m, n])


def mixer_rec(self, hT_d, wfm_d, wtm_d, gbias_d, convw_d, mnorm_d, gnorm_d, cm_d, mixT_d, do_ml=True, do_gd=True):
    P = self.P
    idf, idb, onesf, onesb, cb = self.idf, self.idb, self.onesf, self.onesb, self.cb
    FWD = list(range(NTILE))
    BWD = [1, 0] + list(range(NTILE - 1, 1, -1))
    with ExitStack() as st:
        sb = lambda n, s, d: P.sb(n, s, d, st)
        MQT = sb("MQT", [64, NTOK], BF16)
        MKT = sb("MKT", [64, NTOK], BF16)
        MK = sb("MK", [128, NTILE, 64], BF16)
        MV = sb("MV", [128, NTILE, 129], BF16)
        MO = sb("MO", [128, NTILE, 128], BF16)
        GZ = sb("GZ", [128, NTILE, 128], BF16)
        GQT = sb("GQT", [128, NTOK], BF16)
        GKT = sb("GKT", [128, NTOK], BF16)
        GK = sb("GK", [128, NTILE, 128], BF16)
        GQ = sb("GQ", [128, NTILE, 128], BF16)
        GV = sb("GV", [128, NTILE, 128], BF16)
        GA = sb("GA", [128, 8, NTILE], F32)
        cm = sb("cm", [128, 8, 128], F32)
        bm = sb("bm", [128, 7, 128], F32)
        bmT = sb("bmT", [128, 7, 128], F32)
        gb = sb("gbias", [128, 8], F32)
        cw = sb("convw", [128, 15], F32)
        mgain = sb("mgain", [128, 128], F32)
        ggain = sb("ggain", [128, 128], F32)
        mlb, gdb, gab, cmb = Buf("ml"), Buf("gd"), Buf("ga"), Buf("cmb")
        P.dma("sp", cm[:], cm_d[:, 0:8, :], writes=[cmb])
        P.dma("sp", bm[:], cm_d[:, 8:15, :], writes=[cmb])
        P.dma("sp", bmT[:], cm_d[:, 15:22, :], writes=[cmb])
        P.dma("sp", gb[:], gbias_d.partition_broadcast(128), writes=[cmb])
        P.dma("sp", cw[:], convw_d, writes=[cmb])
        P.dma("sp", mgain[:], mnorm_d.partition_broadcast(128), writes=[cmb])
        P.dma("sp", ggain[:], gnorm_d.partition_broadcast(128), writes=[cmb])
        P.op("pool", lambda e: e.memset(MV[:, :, 128:129], 1.0), writes=[mlb])
        with ExitStack() as st12:
            sb12 = lambda n, s, d: P.sb(n, s, d, st12)
            GRAW = sb12("GRAW", [128, 3, NTOK], BF16)
            grb = Buf("graw")
            with ExitStack() as st1:
                sb1 = lambda n, s, d: P.sb(n, s, d, st1)
                wfm = sb1("wfm", [128, 16, 512], BF16)
                wtm = sb1("wtm", [128, 16, 456], BF16)
                wb = Buf("recw")
                P.dma("pool", wfm[:], wfm_d.rearrange("(k p) c -> p k c", p=128), writes=[wb])
                P.dma("pool", wtm[:], wtm_d.rearrange("(k p) c -> p k c", p=128), writes=[wb])
                hc = [sb1(f"rhc{i}", [128, 16, 512], BF16) for i in range(2)]
                hcb = [Buf(f"rhc{i}") for i in range(2)]
                for ci, (c0, n) in enumerate(TCH):
                    h, hb = hc[ci % 2], hcb[ci % 2]
                    self.hload(h, hb, c0, n, hT_d)
                    for gi, (cols, M) in enumerate(((0, 64), (64, 64), (128, 128), (256, 128), (384, 128))):
                        ps, psb = self.nb()
                        for k in range(16):
                            P.op("pe", lambda e: e.matmul(ps[0:M, 0:n], wfm[:, k, cols:cols + M], h[:, k, 0:n], start=(k == 0), stop=(k == 15)),
                                 reads=[wb, hb], writes=[psb])
                        if gi == 0:
                            P.op("act", lambda e: e.copy(MQT[:, c0:c0 + n], ps[0:64, 0:n]), reads=[psb], writes=[mlb])
                        elif gi == 1:
                            P.op("act", lambda e: e.mul(MKT[:, c0:c0 + n], ps[0:64, 0:n], 0.125), reads=[psb], writes=[mlb])
                        else:
                            P.op("dve", lambda e: e.tensor_copy(GRAW[:, gi - 2, c0:c0 + n], ps[:, 0:n]), reads=[psb], writes=[grb])
                    for j in range(n // 128):
                        t = (c0 + j * 128) // 128
                        ps, psb = self.nb()
                        for k in range(16):
                            P.op("pe", lambda e: e.matmul(ps[:, 0:456], h[:, k, j * 128:(j + 1) * 128], wtm[:, k, :], start=(k == 0), stop=(k == 15)),
                                 reads=[wb, hb], writes=[psb])
                        P.op("dve", lambda e: e.tensor_scalar(MK[:, t, :], ps[:, 0:64], 0.125, None, ALU.mult), reads=[psb], writes=[mlb])
                        P.op("dve", lambda e: e.tensor_copy(GA[:, :, t], ps[:, 448:456]), reads=[psb], writes=[gab])
                        P.op("act", lambda e: e.copy(MV[:, t, 0:128], ps[:, 64:192]), reads=[psb], writes=[mlb])
                        P.op("act", lambda e: e.activation(MO[:, t, :], ps[:, 192:320], AF.Sigmoid), reads=[psb], writes=[mlb])
                        P.op("act", lambda e: e.activation(GZ[:, t, :], ps[:, 320:448], AF.Silu), reads=[psb], writes=[gdb])
                self.barrier()
            with ExitStack() as st2:
                sb2 = lambda n, s, d: P.sb(n, s, d, st2)
                acc = sb2("acc", [128, NTOK], F32)
                sqb = sb2("sqb", [128, NTOK], BF16)
                GVT = sb2("GVT", [128, NTOK], BF16)
                rq = [sb2(f"rq{i}", [128, 512], F32) for i in range(2)]
                accb, sqbb, gvtb = Buf("acc"), Buf("sqb"), Buf("gvt")
                rqb = [Buf(f"rq{i}") for i in range(2)]
                segs = [(0, 256), (256, NTOK)]
                for j in range(3):
                    if not do_gd:
                        break
                    raw = GRAW[:, j, :]
                    P.op("dve", lambda e: e.tensor_scalar(acc[:], raw, cw[:, j * 5 + 2:j * 5 + 3], None, ALU.mult), reads=[grb, cmb], writes=[accb])
                    for dlt in (-2, -1, 1, 2):
                        for (s0, s1) in segs:
                            a_, b_ = max(s0, s0 - dlt), min(s1, s1 - dlt)
                            P.op("dve", lambda e: e.scalar_tensor_tensor(acc[:, a_:b_], GRAW[:, j, a_ + dlt:b_ + dlt], cw[:, j * 5 + 2 + dlt:j * 5 + 3 + dlt],
                                                                         acc[:, a_:b_], ALU.mult, ALU.add), reads=[grb, cmb, accb], writes=[accb])
                    if j == 2:
                        P.op("act", lambda e: e.activation(GVT[:], acc[:], AF.Silu), reads=[accb], writes=[gvtb])
                        continue
                    P.op("act", lambda e: e.activation(acc[:], acc[:], AF.Silu), reads=[accb], writes=[accb])
                    P.op("act", lambda e: e.activation(sqb[:], acc[:], AF.Square), reads=[accb], writes=[sqbb])
                    dst = GQT if j == 0 else GKT
                    scl = (128 ** -0.5) if j == 0 else 1.0
                    for ci, (c0, n) in enumerate(TCH):
                        ps, psb = self.nb()
                        P.op("pe", lambda e: e.matmul(ps[:, 0:n], onesb[:], sqb[:, c0:c0 + n], start=True, stop=True), reads=[cb, sqbb], writes=[psb])
                        r_, r_b = rq[ci % 2], rqb[ci % 2]
                        P.op("act", lambda e: e.activation(r_[:, 0:n], ps[:, 0:n], AF.Sqrt, bias=EPS, scale=1.0), reads=[psb], writes=[r_b])
                        P.op("dve", lambda e: e.reciprocal(r_[:, 0:n], r_[:, 0:n]), reads=[r_b], writes=[r_b])
                        P.op("dve", lambda e: e.scalar_tensor_tensor(dst[:, c0:c0 + n], acc[:, c0:c0 + n], scl, r_[:, 0:n], ALU.mult, ALU.mult),
                             reads=[accb, r_b], writes=[gdb])
                if do_gd:
                    for t in range(NTILE):
                        for si, (src, srcb, dstt) in enumerate(((GKT, gdb, GK), (GQT, gdb, GQ), (GVT, gvtb, GV))):
                            P.op("pe", lambda e: e.transpose(self.ptb[:], src[:, t * 128:(t + 1) * 128], idb[:]), reads=[srcb, cb], writes=[self.ptbb])
                            if si == 1:
                                P.op("act", lambda e: e.copy(dstt[:, t, :], self.ptb[:]), reads=[self.ptbb], writes=[gdb])
                            else:
                                P.op("dve", lambda e: e.tensor_copy(dstt[:, t, :], self.ptb[:]), reads=[self.ptbb], writes=[gdb])
                self.barrier()
        with ExitStack() as st3:
            sb3 = lambda n, s, d: P.sb(n, s, d, st3)
            SC = sb3("SC", [128, 80, NTILE], F32)
            scb = Buf("sc")
            OUT = sb3("RO", [128, NTILE, 128], F32)
            outb = [Buf(f"ro{t}") for t in range(NTILE)]
            OUT2 = sb3("RO2", [128, NTILE, 128], F32)
            outb2 = [Buf(f"ro2_{t}") for t in range(NTILE)]
            streams = []
            LD4 = sb3("LD4", [128, 4, 128], F32)
            ld4b = Buf("ld4")
            Kw = [sb3(f"Kw{d}", [128, NTILE, 64], BF16) for d in range(2)]
            kwb = [Buf(f"kw{d}") for d in range(2)]
            NT = 24
            T = [[sb3(f"T{d}_{i}", [128, 128], F32) for i in range(NT)] for d in range(2)]
            Tb = [[Buf(f"T{d}_{i}") for i in range(NT)] for d in range(2)]
            TB16 = [[sb3(f"Tb{d}_{i}", [128, 128], BF16) for i in range(16)] for d in range(2)]
            TB16b = [[Buf(f"Tb{d}_{i}") for i in range(16)] for d in range(2)]
            DG4 = sb3("DG4", [128, 4, 128], F32)
            dg4b = Buf("dg4")
            W129 = [[sb3(f"W129_{d}_{i}", [128, 129], F32) for i in range(2)] for d in range(2)]
            W129b = [[Buf(f"W129_{d}_{i}") for i in range(2)] for d in range(2)]
            Cn = [sb3(f"Cn{d}", [64, 129], F32) for d in range(2)]
            Cb = [sb3(f"Cb{d}", [64, 129], BF16) for d in range(2)]
            cnb = [Buf(f"cn{d}") for d in range(2)]
            Sst = [[sb3(f"Sst{d}_{i}", [128, 128], F32) for i in range(2)] for d in range(2)]
            sstb = [[Buf(f"Sst{d}_{i}") for i in range(2)] for d in range(2)]
            stg = [sb3(f"rstg{i}", [128, 512], BF16) for i in range(2)]
            stgb = [Buf(f"rstg{i}") for i in range(2)]
            sc4 = [sb3(f"sc4_{d}", [128, 8], F32) for d in range(2)]
            sc4b = [Buf(f"sc4_{d}") for d in range(2)]
            ti = [0, 0]
            tbi = [0, 0]
            ORD = [FWD, BWD]

            def tmpf(d):
                i = ti[d] % NT
                ti[d] += 1
                return T[d][i], Tb[d][i]

            def tmpb(d):
                i = tbi[d] % 16
                tbi[d] += 1
                return TB16[d][i], TB16b[d][i]

            A = lambda s: SC[:, s, :]
            C = lambda s, t: SC[:, s, t:t + 1]

            def softplus_core(xs, ts):
                P.op("act", lambda e: e.activation(A(ts), A(xs), AF.Abs), reads=[scb], writes=[scb])
                P.op("act", lambda e: e.activation(A(ts), A(ts), AF.Exp, scale=-1.0), reads=[scb], writes=[scb])
                P.op("act", lambda e: e.activation(A(ts), A(ts), AF.Ln, bias=1.0), reads=[scb], writes=[scb])

            def cumsum(src, dst_cum, dst_sum, bwd):
                ps, psb = self.nb()
                P.op("pe", lambda e: e.matmul(ps[:, 0:NTILE], cm[:, 1 if bwd else 0, :], A(src), start=True, stop=True), reads=[cmb, scb], writes=[psb])
                P.op("dve", lambda e: e.tensor_copy(A(dst_cum), ps[:, 0:NTILE]), reads=[psb], writes=[scb])
                ps, psb = self.nb()
                P.op("pe", lambda e: e.matmul(ps[:, 0:NTILE], onesf[:], A(src), start=True, stop=True), reads=[cb, scb], writes=[psb])
                P.op("dve", lambda e: e.tensor_copy(A(dst_sum), ps[:, 0:NTILE]), reads=[psb], writes=[scb])

            def zero_out():
                P.op("pool", lambda e: e.memset(OUT[:], 0.0), writes=outb)
                P.op("pool", lambda e: e.memset(OUT2[:], 0.0), writes=outb2)

            def finish_out(OUT, outb, gain, gate, row0):
                oi = 0
                for t0 in range(0, NTILE, 4):
                    m = min(4, NTILE - t0)
                    sg, sgb = stg[oi % 2], stgb[oi % 2]
                    oi += 1
                    for q in range(m):
                        t = t0 + q
                        d_ = q % 2
                        y, yb = tmpf(d_)
                        y2, y2b = tmpb(d_)
                        s4, s4b = sc4[d_], sc4b[d_]
                        P.op("act", lambda e: e.activation(y[:], OUT[:, t, :], AF.Square, accum_out=s4[:, 0:1]), reads=[outb[t]], writes=[yb, s4b])
                        P.op("act", lambda e: e.activation(s4[:, 1:2], s4[:, 0:1], AF.Sqrt, bias=EPS, scale=1.0 / 128), reads=[s4b], writes=[s4b])
                        P.op("dve", lambda e: e.reciprocal(s4[:, 2:3], s4[:, 1:2]), reads=[s4b], writes=[s4b])
                        P.op("dve", lambda e: e.scalar_tensor_tensor(y[:], OUT[:, t, :], s4[:, 2:3], gain[:], ALU.mult, ALU.mult),
                             reads=[outb[t], s4b, cmb], writes=[yb])
                        P.op("pool", lambda e: e.tensor_tensor(y2[:], y[:], gate[:, t, :], ALU.mult), reads=[yb, mlb, gdb], writes=[y2b])
                        P.op("pe", lambda e: e.transpose(self.ptb[:], y2[:], idb[:]), reads=[y2b, cb], writes=[self.ptbb])
                        P.op("dve", lambda e: e.tensor_copy(sg[:, q * 128:(q + 1) * 128], self.ptb[:]), reads=[self.ptbb], writes=[sgb])
                    self.mix_store(mixT_d, row0, t0 * 128, m * 128, sg, [sgb])

            if do_ml:
                zero_out()
                for dr in range(2):
                    bwd = dr == 1
                    order = ORD[dr]
                    o_ = dr * 20
                    s_ig, s_lf, s_b, s_bs, s_a, s_rm, s_am, s_mp, s_mm, s_ml, s_nm, s_tmp = [o_ + i for i in range(12)]
                    s_x = o_ + 16
                    P.op("dve", lambda e: e.tensor_scalar(A(s_ig), GA[:, 2 * dr, :], gb[:, 2 * dr:2 * dr + 1], None, ALU.add), reads=[gab, cmb], writes=[scb])
                    P.op("dve", lambda e: e.tensor_scalar(A(s_lf), GA[:, 2 * dr + 1, :], gb[:, 2 * dr + 1:2 * dr + 2], None, ALU.add), reads=[gab, cmb], writes=[scb])
                    softplus_core(s_lf, s_tmp)
                    P.op("dve", lambda e: e.scalar_tensor_tensor(A(s_lf), A(s_lf), 0.0, A(s_tmp), ALU.min, ALU.subtract), reads=[scb], writes=[scb])
                    cumsum(s_lf, s_b, s_bs, bwd)
                    P.op("dve", lambda e: e.tensor_tensor(A(s_a), A(s_ig), A(s_b), ALU.subtract), reads=[scb], writes=[scb])
                    neg = cm[:, 4 if bwd else 2, :]
                    for t0 in range(0, NTILE, 4):
                        m = min(4, NTILE - t0)
                        P.op("pool", lambda e: e.tensor_tensor(DG4[:, 0:m, :], bc_midrep(idf[:], m), bc_last(SC[:, s_a, t0:t0 + m], 128), ALU.mult),
                             reads=[cb, scb], writes=[dg4b])
                        ps, psb = self.nb()
                        P.op("pe", lambda e: e.matmul(ps[:, 0:m * 128], onesf[:], DG4[:, 0:m, :].rearrange("p m s -> p (m s)"), start=True, stop=True),
                             reads=[cb, dg4b], writes=[psb])
                        P.op("dve", lambda e: e.tensor_tensor(LD4[:, 0:m, :], ps[:, 0:m * 128].rearrange("p (m s) -> p m s", m=m), bc_midrep(neg, m), ALU.add),
                             reads=[psb, cmb], writes=[ld4b])
                        P.op("dve", lambda e: e.tensor_reduce(SC[:, s_rm, t0:t0 + m], LD4[:, 0:m, :], AX.X, ALU.max), reads=[ld4b], writes=[scb])
                    ps, psb = self.nb()
                    P.op("pe", lambda e: e.matmul(ps[:, 0:NTILE], cm[:, 7 if bwd else 6, :], A(s_rm), start=True, stop=True), reads=[cmb, scb], writes=[psb])
                    P.op("dve", lambda e: e.tensor_copy(A(s_am), ps[:, 0:NTILE]), reads=[psb], writes=[scb])
                    P.op("dve", lambda e: e.memset(C(s_mp, order[0]), NEGV), writes=[scb])
                    for i in range(NTILE - 1):
                        t, tn = order[i], order[i + 1]
                        P.op("dve", lambda e: e.scalar_tensor_tensor(C(s_mp, tn), C(s_mp, t), C(s_am, t), C(s_bs, t), ALU.max, ALU.add), reads=[scb], writes=[scb])
                    P.op("dve", lambda e: e.tensor_tensor(A(s_mm), A(s_mp), A(s_rm), ALU.max), reads=[scb], writes=[scb])
                    P.op("dve", lambda e: e.tensor_tensor(A(s_ml), A(s_mp), A(s_am), ALU.max), reads=[scb], writes=[scb])
                    P.op("dve", lambda e: e.tensor_tensor(A(s_x + 0), A(s_mp), A(s_mm), ALU.subtract), reads=[scb], writes=[scb])
                    P.op("dve", lambda e: e.scalar_tensor_tensor(A(s_x + 1), A(s_b), -1.0, A(s_mm), ALU.mult, ALU.subtract), reads=[scb], writes=[scb])
                    P.op("dve", lambda e: e.tensor_tensor(A(s_x + 2), A(s_a), A(s_ml), ALU.subtract), reads=[scb], writes=[scb])
                    P.op("dve", lambda e: e.tensor_tensor(A(s_x + 3), A(s_mp), A(s_ml), ALU.subtract), reads=[scb], writes=[scb])
                    P.op("dve", lambda e: e.tensor_scalar(SC[:, s_x:s_x + 4, :], SC[:, s_x:s_x + 4, :], -150.0, None, ALU.max), reads=[scb], writes=[scb])
                    P.op("act", lambda e: e.activation(SC[:, s_x:s_x + 4, :], SC[:, s_x:s_x + 4, :], AF.Exp), reads=[scb], writes=[scb])
                    P.op("dve", lambda e: e.tensor_scalar(A(s_nm), A(s_mm), -1.0, None, ALU.mult), reads=[scb], writes=[scb])
                    P.op("dve", lambda e: e.tensor_tensor(Kw[dr][:], MK[:], bc_last(A(s_x + 2), 64), ALU.mult), reads=[mlb, scb], writes=[kwb[dr]])
                    P.op("dve", lambda e: e.memset(Cn[dr][:], 0.0), writes=[cnb[dr]])
                    P.op("dve", lambda e: e.memset(Cb[dr][:], 0.0), writes=[cnb[dr]])
                def ml_chunk(dr, t):
                    bwd = dr == 1
                    o_ = dr * 20
                    s_a, s_nm, s_x = o_ + 4, o_ + 10, o_ + 16
                    neg = cm[:, 4 if bwd else 2, :]
                    s4, s4b = sc4[dr], sc4b[dr]
                    tok = slice(t * 128, (t + 1) * 128)
                    DG, DGb = tmpf(dr)
                    yield P.op("pool", lambda e: e.tensor_scalar(DG[:], idf[:], C(s_a, t), None, ALU.mult), reads=[cb, scb], writes=[DGb])
                    psL, psLb = self.nb()
                    yield P.op("pe", lambda e: e.matmul(psL[:, 0:128], onesf[:], DG[:], start=True, stop=True), reads=[cb, DGb], writes=[psLb])
                    Dm, Dmb = tmpf(dr)
                    yield P.op("dve", lambda e: e.tensor_tensor(Dm[:], psL[:, 0:128], neg, ALU.add), reads=[psLb, cmb], writes=[Dmb])
                    yield P.op("act", lambda e: e.activation(Dm[:], Dm[:], AF.Exp, bias=C(s_nm, t)), reads=[Dmb, scb], writes=[Dmb])
                    ps, psb = self.nb()
                    yield P.op("pe", lambda e: e.matmul(ps[:, 0:128], MQT[:, tok], MKT[:, tok], start=True, stop=True), reads=[mlb], writes=[psb])
                    SD, SDb = tmpf(dr)
                    yield P.op("dve", lambda e: e.tensor_tensor(SD[:], ps[:, 0:128], Dm[:], ALU.mult), reads=[psb, Dmb], writes=[SDb])
                    psT, psTb = self.nb()
                    yield P.op("pe", lambda e: e.transpose(psT[:, 0:128], SD[:], idf[:]), reads=[SDb, cb], writes=[psTb])
                    SDT, SDTb = tmpb(dr)
                    yield P.op("act", lambda e: e.copy(SDT[:], psT[:, 0:128]), reads=[psTb], writes=[SDTb])
                    psN, psNb = self.nb()
                    yield P.op("pe", lambda e: e.matmul(psN[:, 0:129], SDT[:], MV[:, t, :], start=True, stop=True), reads=[SDTb, mlb], writes=[psNb])
                    psI, psIb = self.nb()
                    yield P.op("pe", lambda e: e.matmul(psI[:, 0:129], MQT[:, tok], Cb[dr][:], start=True, stop=True), reads=[mlb, cnb[dr]], writes=[psIb])
                    w1, w1b = W129[dr][0], W129b[dr][0]
                    w2, w2b = W129[dr][1], W129b[dr][1]
                    yield P.op("dve", lambda e: e.tensor_scalar(w1[:], psI[:, 0:129], C(s_x + 0, t), None, ALU.mult), reads=[psIb, scb], writes=[w1b])
                    yield P.op("dve", lambda e: e.tensor_tensor(w2[:], psN[:, 0:129], w1[:], ALU.add), reads=[psNb, w1b], writes=[w2b])
                    yield P.op("act", lambda e: e.activation(s4[:, 4:5], w2[:, 128:129], AF.Abs), reads=[w2b], writes=[s4b])
                    yield P.op("dve", lambda e: e.tensor_tensor(s4[:, 4:5], s4[:, 4:5], C(s_x + 1, t), ALU.max), reads=[s4b, scb], writes=[s4b])
                    yield P.op("dve", lambda e: e.reciprocal(s4[:, 5:6], s4[:, 4:5]), reads=[s4b], writes=[s4b])
                    yield P.op("dve", lambda e: e.scalar_tensor_tensor(OUT[:, t, :], w2[:, 0:128], s4[:, 5:6], OUT[:, t, :], ALU.mult, ALU.add),
                         reads=[w2b, s4b, outb[t]], writes=[outb[t]])
                    psC, psCb = self.nb()
                    yield P.op("pe", lambda e: e.matmul(psC[0:64, 0:129], Kw[dr][:, t, :], MV[:, t, :], start=True, stop=True), reads=[kwb[dr], mlb], writes=[psCb])
                    yield P.op("dve", lambda e: e.scalar_tensor_tensor(Cn[dr][:], Cn[dr][:], SC[0:64, s_x + 3, t:t + 1], psC[0:64, 0:129], ALU.mult, ALU.add),
                         reads=[cnb[dr], scb, psCb], writes=[cnb[dr]])
                    yield P.op("act", lambda e: e.copy(Cb[dr][:], Cn[dr][:]), reads=[cnb[dr]], writes=[cnb[dr]])

                streams.append(ml_chunk)
            if do_gd:
                if not do_ml:
                    zero_out()
                for dr in range(2):
                    bwd = dr == 1
                    o_ = 40 + dr * 20
                    s_be, s_x, s_t, s_gl, s_gc, s_ge, s_nb, s_ng, s_e = [o_ + i for i in range(9)]
                    s_bg = o_ + 11
                    s4, s4b = sc4[dr], sc4b[dr]
                    P.op("act", lambda e: e.activation(A(s_be), GA[:, 4 + 2 * dr, :], AF.Sigmoid), reads=[gab], writes=[scb])
                    P.op("dve", lambda e: e.tensor_scalar(A(s_x), GA[:, 5 + 2 * dr, :], gb[:, 4 + dr:5 + dr], None, ALU.add), reads=[gab, cmb], writes=[scb])
                    softplus_core(s_x, s_t)
                    P.op("dve", lambda e: e.scalar_tensor_tensor(A(s_gl), A(s_x), 0.0, A(s_t), ALU.max, ALU.add), reads=[scb], writes=[scb])
                    P.op("act", lambda e: e.activation(s4[:, 6:7], gb[:, 6 + dr:7 + dr], AF.Exp), reads=[cmb], writes=[s4b])
                    P.op("dve", lambda e: e.tensor_scalar(A(s_gl), A(s_gl), s4[:, 6:7], -1.0, ALU.mult, ALU.mult), reads=[scb, s4b], writes=[scb])
                    cumsum(s_gl, s_gc, s_ge, bwd)
                    P.op("dve", lambda e: e.tensor_scalar(A(s_nb), A(s_be), -1.0, None, ALU.mult), reads=[scb], writes=[scb])
                    P.op("dve", lambda e: e.tensor_scalar(A(s_ng), A(s_gc), -1.0, None, ALU.mult), reads=[scb], writes=[scb])
                    P.op("dve", lambda e: e.tensor_copy(A(s_e + 0), A(s_gc)), reads=[scb], writes=[scb])
                    P.op("dve", lambda e: e.tensor_copy(A(s_e + 1), A(s_ge)), reads=[scb], writes=[scb])
                    P.op("dve", lambda e: e.tensor_tensor(A(s_e + 2), A(s_ge), A(s_gc), ALU.subtract), reads=[scb], writes=[scb])
                    P.op("act", lambda e: e.activation(SC[:, s_e:s_e + 3, :], SC[:, s_e:s_e + 3, :], AF.Exp), reads=[scb], writes=[scb])
                    P.op("dve", lambda e: e.tensor_tensor(A(s_bg), A(s_be), A(s_e + 0), ALU.mult), reads=[scb], writes=[scb])
                    P.op("dve", lambda e: e.memset(Sst[dr][0][:], 0.0), writes=[sstb[dr][0]])
                scur = [0, 0]
                def gd_chunk(dr, t):
                    bwd = dr == 1
                    o_ = 40 + dr * 20
                    s_be, s_x, s_t, s_gl, s_gc, s_ge, s_nb, s_ng, s_e = [o_ + i for i in range(9)]
                    s_bg = o_ + 11
                    negst = cm[:, 5 if bwd else 3, :]
                    neginT = cm[:, 2 if bwd else 4, :]
                    tok = slice(t * 128, (t + 1) * 128)
                    DG, DGb = tmpf(dr)
                    yield P.op("pool", lambda e: e.tensor_scalar(DG[:], idf[:], C(s_gc, t), None, ALU.mult), reads=[cb, scb], writes=[DGb])
                    ps1, ps1b = self.nb()
                    yield P.op("pe", lambda e: e.matmul(ps1[:, 0:128], onesf[:], DG[:], start=True, stop=True), reads=[cb, DGb], writes=[ps1b])
                    a1, a1b = tmpf(dr)
                    a2, a2b = tmpf(dr)
                    yield P.op("dve", lambda e: e.scalar_tensor_tensor(a1[:], ps1[:, 0:128], -1.0, negst, ALU.mult, ALU.add), reads=[ps1b, cmb], writes=[a1b])
                    yield P.op("dve", lambda e: e.tensor_tensor(a2[:], ps1[:, 0:128], neginT, ALU.add), reads=[ps1b, cmb], writes=[a2b])
                    yield P.op("act", lambda e: e.activation(a1[:], a1[:], AF.Exp, bias=C(s_gc, t)), reads=[a1b, scb], writes=[a1b])
                    dT, dTb = tmpb(dr)
                    yield P.op("act", lambda e: e.activation(dT[:], a2[:], AF.Exp, bias=C(s_ng, t)), reads=[a2b, scb], writes=[dTb])
                    ps2, ps2b = self.nb()
                    yield P.op("pe", lambda e: e.matmul(ps2[:, 0:128], GKT[:, tok], GKT[:, tok], start=True, stop=True), reads=[gdb], writes=[ps2b])
                    Pm, Pmb = tmpf(dr)
                    yield P.op("dve", lambda e: e.scalar_tensor_tensor(Pm[:], ps2[:, 0:128], C(s_nb, t), a1[:], ALU.mult, ALU.mult), reads=[ps2b, scb, a1b], writes=[Pmb])
                    ps3, ps3b = self.nb()
                    yield P.op("pe", lambda e: e.transpose(ps3[:, 0:128], Pm[:], idf[:]), reads=[Pmb, cb], writes=[ps3b])
                    PT, PTb = tmpf(dr)
                    yield P.op("act", lambda e: e.copy(PT[:], ps3[:, 0:128]), reads=[ps3b], writes=[PTb])
                    mkN = bmT if bwd else bm
                    mkT = bm if bwd else bmT
                    X, Xb = tmpf(dr)
                    Y, Yb = tmpf(dr)
                    yield P.op("pool", lambda e: e.tensor_tensor(X[:], Pm[:], mkN[:, 0, :], ALU.mult), reads=[Pmb, cmb], writes=[Xb])
                    yield P.op("pool", lambda e: e.tensor_tensor(X[:], X[:], idf[:], ALU.add), reads=[Xb, cb], writes=[Xb])
                    yield P.op("pool", lambda e: e.tensor_tensor(Y[:], PT[:], mkT[:, 0, :], ALU.mult), reads=[PTb, cmb], writes=[Yb])
                    yield P.op("pool", lambda e: e.tensor_tensor(Y[:], Y[:], idf[:], ALU.add), reads=[Yb, cb], writes=[Yb])
                    for lev in range(1, 7):
                        GT, GTb = tmpf(dr)
                        yield P.op("pool", lambda e: e.tensor_tensor(GT[:], PT[:], mkT[:, lev, :], ALU.mult), reads=[PTb, cmb], writes=[GTb])
                        psA, psAb = self.nb()
                        yield P.op("pe", lambda e: e.matmul(psA[:, 0:128], GT[:], X[:], start=True, stop=True), reads=[GTb, Xb], writes=[psAb])
                        W1, W1b = tmpf(dr)
                        yield P.op("act", lambda e: e.copy(W1[:], psA[:, 0:128]), reads=[psAb], writes=[W1b])
                        if lev < 6:
                            psB, psBb = self.nb()
                            yield P.op("pe", lambda e: e.matmul(psB[:, 0:128], Y[:], W1[:], start=True, stop=True), reads=[Yb, W1b], writes=[psBb])
                            X2, X2b = tmpf(dr)
                            yield P.op("dve", lambda e: e.tensor_tensor(X2[:], psB[:, 0:128], X[:], ALU.add), reads=[psBb, Xb], writes=[X2b])
                        psC, psCb = self.nb()
                        yield P.op("pe", lambda e: e.matmul(psC[:, 0:128], W1[:], Y[:], start=True, stop=True), reads=[W1b, Yb], writes=[psCb])
                        Y2, Y2b = tmpf(dr)
                        yield P.op("dve", lambda e: e.tensor_tensor(Y2[:], psC[:, 0:128], Y[:], ALU.add), reads=[psCb, Yb], writes=[Y2b])
                        Y, Yb = Y2, Y2b
                        if lev < 6:
                            X, Xb = X2, X2b
                    Y16, Y16b = tmpb(dr)
                    yield P.op("act", lambda e: e.copy(Y16[:], Y[:]), reads=[Yb], writes=[Y16b])
                    Vb, Vbb = tmpb(dr)
                    Kbg, Kbgb = tmpb(dr)
                    Kd, Kdb = tmpb(dr)
                    Qg, Qgb = tmpb(dr)
                    yield P.op("pool", lambda e: e.tensor_scalar(Vb[:], GV[:, t, :], C(s_be, t), None, ALU.mult), reads=[gdb, scb], writes=[Vbb])
                    yield P.op("pool", lambda e: e.tensor_scalar(Kbg[:], GK[:, t, :], C(s_bg, t), None, ALU.mult), reads=[gdb, scb], writes=[Kbgb])
                    yield P.op("pool", lambda e: e.tensor_scalar(Kd[:], GK[:, t, :], C(s_e + 2, t), None, ALU.mult), reads=[gdb, scb], writes=[Kdb])
                    yield P.op("pool", lambda e: e.tensor_scalar(Qg[:], GQ[:, t, :], C(s_e + 0, t), None, ALU.mult), reads=[gdb, scb], writes=[Qgb])
                    psU, psUb = self.nb()
                    yield P.op("pe", lambda e: e.matmul(psU[:, 0:128], Y16[:], Vb[:], start=True, stop=True), reads=[Y16b, Vbb], writes=[psUb])
                    psW, psWb = self.nb()
                    yield P.op("pe", lambda e: e.matmul(psW[:, 0:128], Y16[:], Kbg[:], start=True, stop=True), reads=[Y16b, Kbgb], writes=[psWb])
                    Us, Usb = tmpb(dr)
                    Wn, Wnb = tmpb(dr)
                    yield P.op("act", lambda e: e.copy(Us[:], psU[:, 0:128]), reads=[psUb], writes=[Usb])
                    yield P.op("dve", lambda e: e.tensor_scalar(Wn[:], psW[:, 0:128], -1.0, None, ALU.mult), reads=[psWb], writes=[Wnb])
                    psQ, psQb = self.nb()
                    yield P.op("pe", lambda e: e.matmul(psQ[:, 0:128], GKT[:, tok], GQT[:, tok], start=True, stop=True), reads=[gdb], writes=[psQb])
                    AT, ATb = tmpb(dr)
                    yield P.op("dve", lambda e: e.tensor_tensor(AT[:], psQ[:, 0:128], dT[:], ALU.mult), reads=[psQb, dTb], writes=[ATb])
                    psP, psPb = self.nb()
                    yield P.op("pe", lambda e: e.matmul(psP[:, 0:128], Qg[:], idb[:], start=True, stop=False), reads=[Qgb, cb], writes=[psPb])
                    yield P.op("pe", lambda e: e.matmul(psP[:, 0:128], Wn[:], AT[:], start=False, stop=True), reads=[Wnb, ATb], writes=[psPb])
                    QpT, QpTb = tmpf(dr)
                    yield P.op("act", lambda e: e.copy(QpT[:], psP[:, 0:128]), reads=[psPb], writes=[QpTb])
                    psA_, psA_b = self.nb()
                    yield P.op("pe", lambda e: e.matmul(psA_[:, 0:128], Wn[:], Kd[:], start=True, stop=True), reads=[Wnb, Kdb], writes=[psA_b])
                    AcT, AcTb = tmpf(dr)
                    yield P.op("dve", lambda e: e.scalar_tensor_tensor(AcT[:], idf[:], C(s_e + 1, t), psA_[:, 0:128], ALU.mult, ALU.add), reads=[cb, scb, psA_b], writes=[AcTb])
                    psBc, psBcb = self.nb()
                    yield P.op("pe", lambda e: e.matmul(psBc[:, 0:128], Kd[:], Us[:], start=True, stop=True), reads=[Kdb, Usb], writes=[psBcb])
                    Bc, Bcb = tmpf(dr)
                    yield P.op("act", lambda e: e.copy(Bc[:], psBc[:, 0:128]), reads=[psBcb], writes=[Bcb])
                    S, Sb = Sst[dr][scur[dr]], sstb[dr][scur[dr]]
                    S2, S2b = Sst[dr][1 - scur[dr]], sstb[dr][1 - scur[dr]]
                    psO, psOb = self.nb()
                    yield P.op("pe", lambda e: e.matmul(psO[:, 0:128], AT[:], Us[:], start=True, stop=False), reads=[ATb, Usb], writes=[psOb])
                    yield P.op("pe", lambda e: e.matmul(psO[:, 0:128], QpT[:], S[:], start=False, stop=True), reads=[QpTb, Sb], writes=[psOb])
                    yield P.op("dve", lambda e: e.tensor_tensor(OUT2[:, t, :], psO[:, 0:128], OUT2[:, t, :], ALU.add), reads=[psOb, outb2[t]], writes=[outb2[t]])
                    psS, psSb = self.nb()
                    yield P.op("pe", lambda e: e.matmul(psS[:, 0:128], AcT[:], S[:], start=True, stop=True), reads=[AcTb, Sb], writes=[psSb])
                    yield P.op("dve", lambda e: e.tensor_tensor(S2[:], psS[:, 0:128], Bc[:], ALU.add), reads=[psSb, Bcb], writes=[S2b])
                    scur[dr] = 1 - scur[dr]

                streams.append(gd_chunk)
            for step in range(NTILE):
                gens = [fn(dr, ORD[dr][step]) for fn in streams for dr in range(2)]
                alive = [True] * len(gens)
                while any(alive):
                    for gi_ in range(len(gens)):
                        if alive[gi_]:
                            try:
                                next(gens[gi_])
                            except StopIteration:
                                alive[gi_] = False
            if do_ml:
                finish_out(OUT, outb, mgain, MO, 256)
            if do_gd:
                finish_out(OUT2, outb2, ggain, GZ, 384)
            self.barrier()


Mixer.rec = mixer_rec


_PROGS = {}
_DBG = None


def _prog(name, builder):
    if name not in _PROGS:
        _PROGS[name] = builder()
    return _PROGS[name]


def _run(P, ins):
    return run_bass_kernel_spmd(P.nc, ins, core_ids=list(range(8))).results


def kernel_unfused(**inp):
    import ml_dtypes
    inp = {k: np.asarray(v) for k, v in inp.items()}
    x, c, ctx, c_ctx = inp["x"], inp["c"], inp["ctx"], inp["c_ctx"]
    ident = np.eye(128, dtype=np.float32)
    cos, sin = rope_tables()
    cT = np.ascontiguousarray(np.stack([c[0], c[1], c_ctx]).T)
    res = _run(_prog("ada", build_ada),
               [dict(cT=cT, wada=np.ascontiguousarray(inp["w_ada"][:, :, i * 1536:(i + 1) * 1536]),
                     bada=np.ascontiguousarray(inp["b_ada"][:, i * 1536:(i + 1) * 1536])) for i in range(8)])
    mod = np.concatenate([r["modp"] for r in res], axis=2).reshape(4, 3, 6, D)

    def modv(l, b):
        return np.ascontiguousarray(np.stack([mod[l, b], mod[l, 2]]))

    xs = []
    for core in range(8):
        b, g = divmod(core, 4)
        xs.append(np.ascontiguousarray(np.concatenate([ctx[b, 64 * g:64 * g + 64], x[b, 1024 * g:1024 * (g + 1)]], 0)))
    res = _run(_prog("A", lambda: build_dense("A")),
               [dict(ident=ident, x_in=xs[core], modn=modv(0, core // 4), norm1=inp["norm1"][0]) for core in range(8)])
    hTs = [r["hT"] for r in res]
    out = None
    for l in range(4):
        hfull = []
        for b in range(2):
            parts = [hTs[b * 4 + g][:, 0:64] for g in range(4)] + [hTs[b * 4 + g][:, 64:] for g in range(4)]
            hfull.append(np.ascontiguousarray(np.concatenate(parts, axis=1)))
        ins = []
        for core in range(8):
            b, g = divmod(core, 4)
            dd = dict(ident=ident, hT=hfull[b], cos=cos, sin=sin)
            dd.update(mixer_mla_inputs(inp, l, g))
            dd.update(mixer_rec_inputs(inp, l, g))
            ins.append(dd)
        res = _run(_prog("mix", build_mixer), ins)
        mixs = [r["mixT"] for r in res]
        if _DBG is not None:
            _DBG[f"hfull{l}"] = [np.asarray(h).astype(np.float32) for h in hfull]
            _DBG[f"mixs{l}"] = [np.asarray(m).astype(np.float32) for m in mixs]
        ins = []
        for core in range(8):
            b, g = divmod(core, 4)
            rows = [mixs[b * 4 + gg][0:256] for gg in range(4)] + [mixs[b * 4 + gg][256:384] for gg in range(4)] + \
                   [mixs[b * 4 + gg][384:512] for gg in range(4)]
            full = np.concatenate(rows, axis=0)
            mt = np.ascontiguousarray(np.concatenate([full[:, 64 * g:64 * g + 64], full[:, 256 + 1024 * g:256 + 1024 * (g + 1)]], axis=1))
            dd = dict(ident=ident, x_in=xs[core], mixT=mt, modv=modv(l, b), norm2=inp["norm2"][l],
                      w_out=inp["w_out"][l], w1=inp["w_mlp1"][l], w2=inp["w_mlp2"][l])
            if l < 3:
                dd.update(modn=modv(l + 1, b), norm1=inp["norm1"][l + 1])
            else:
                dd.update(final_norm=inp["final_norm"])
            ins.append(dd)
        if l < 3:
            res = _run(_prog("C", lambda: build_dense("C")), ins)
            xs = [r["x_out"] for r in res]
            hTs = [r["hT"] for r in res]
            if _DBG is not None:
                _DBG[f"xs{l}"] = [np.asarray(v) for v in xs]
        else:
            res = _run(_prog("CF", lambda: build_dense("CF")), ins)
            out = np.stack([np.concatenate([res[b * 4 + g]["y"] for g in range(4)], axis=0) for b in range(2)])
    return np.ascontiguousarray(out.astype(np.float32))


GROUPS4 = [[0, 1, 2, 3], [4, 5, 6, 7]]
MIXCHK = [3, 3, 2, 3, 3, 2]
MIXOFF = [0, 3, 6, 8, 11, 14]


def mix_chunk(kk):
    for c in range(6):
        if MIXOFF[c] <= kk < MIXOFF[c] + MIXCHK[c]:
            return c, kk - MIXOFF[c]


CHK = [3, 3, 3, 3, 3, 1]
MIX_CHUNK = [(2 * r + i) if i < 2 else (8 + r if i == 2 else 12 + r) for r in range(4) for i in range(4)]


def dense_load_mix_gathered(dn, mix_g, gbuf, sel_d):
    P = dn.P
    sel = dn.st
    P.dma("sp", sel[:, 8:12], sel_d, writes=[dn.stb])
    n = 0
    for r in range(4):
        for i in range(4):
            k = MIX_CHUNK[r * 4 + i]
            wa, wab = dn.wA[n % 2], dn.wAb[n % 2]
            n += 1
            cand = wa[:].rearrange("p k c -> p (k c)")[:, 0:4 * NT_CORE].rearrange("p (j t) -> p j t", j=4)
            for j in range(4):
                c_, lk = mix_chunk(i * 4 + j)
                row = r * MIXCHK[c_] * 128 + lk * 128
                P.dma("sp", cand[:, j, :], mix_g[c_][row:row + 128, :], reads=[gbuf], writes=[wab])
            P.op("dve", lambda e: e.tensor_scalar(dn.hm[:, k, :], cand[:, 0, :], sel[:, 8:9], None, ALU.mult), reads=[wab, dn.stb], writes=[dn.hmb])
            for j in range(1, 4):
                P.op("dve", lambda e: e.scalar_tensor_tensor(dn.hm[:, k, :], cand[:, j, :], sel[:, 8 + j:9 + j], dn.hm[:, k, :], ALU.mult, ALU.add),
                     reads=[wab, dn.stb, dn.hmb], writes=[dn.hmb])


def fused_ada(P, cT2, wada, bada, modp):
    with ExitStack() as st:
        sT = P.sb("sT", [128, 16, 2], F32, st)
        sTb = Buf("sT")
        w = [P.sb(f"w{i}", [128, 16, 512], F32, st) for i in range(2)]
        wb = [Buf(f"w{i}") for i in range(2)]
        bt = [P.sb(f"bt{i}", [2, 512], F32, st) for i in range(2)]
        btb = [Buf(f"bt{i}") for i in range(2)]
        ot = [P.sb(f"ot{i}", [2, 512], F32, st) for i in range(2)]
        otb = [Buf(f"ot{i}") for i in range(2)]
        ps = [P.ps(f"ps{i}", [128, 512], F32, st) for i in range(2)]
        psb = [Buf(f"ps{i}", True) for i in range(2)]
        P.dma("sp", sT[:], cT2.rearrange("(k p) r -> p k r", p=128), writes=[sTb])
        P.op("act", lambda e: e.activation(sT[:], sT[:], AF.Silu), reads=[sTb], writes=[sTb])
        i = 0
        for l in range(4):
            for cc in range(6):
                j = i % 2
                i += 1
                P.dma("sp", w[j][:], wada[l, :, cc * 512:(cc + 1) * 512].rearrange("(k p) c -> p k c", p=128), writes=[wb[j]])
                P.dma("sp", bt[j][:], bada[l, cc * 512:(cc + 1) * 512].partition_broadcast(2), writes=[btb[j]])
                for k in range(16):
                    P.op("pe", lambda e: e.matmul(ps[j][0:2, :], sT[:, k, :], w[j][:, k, :], start=(k == 0), stop=(k == 15)),
                         reads=[sTb, wb[j]], writes=[psb[j]])
                P.op("dve", lambda e: e.tensor_tensor(ot[j][:], ps[j][0:2, :], bt[j][:], ALU.add), reads=[psb[j], btb[j]], writes=[otb[j]])
                P.dma("sp", modp[2 * l:2 * l + 2, cc * 512:(cc + 1) * 512], ot[j][:], reads=[otb[j]])
        full_barrier(P)


def full_barrier(P):
    toks = []
    for e in ("pe", "act", "dve", "pool"):
        if e in P.last:
            toks.append(P.last[e])
    for s in range(NDS):
        if P.dcnt[s] > 0:
            toks.append(("dma", P.dsems[s], P.dcnt[s]))
    if hasattr(P, "csem") and P.ccnt > 0:
        toks.append(("cc", P.csem, P.ccnt))
    for e in P.ENG:
        for t in toks:
            if t[0] != e:
                P._wait(e, t)


def build_fused(nlayers=4, nslices=16, probe=None, stop=None):
    P = Prog()
    nc = P.nc
    probe = probe or (lambda P, tag: None)
    decl = {}
    SH = dict(ident=[128, 128], x_in=[NT_CORE, D], cT2=[D, 2], wada=[4, D, 3072], bada=[4, 3072], norm1=[4, D], norm2=[4, D],
              final_norm=[D], w_out=[4, D, D], w1=[4, D, DFF], w2=[4, DFF, D], win=[4, D, 896], wuq=[4, 512, 512], wukv=[4, 256, 512],
              qn=[4, 128, 4], kvn=[4, 128, 2], onorm=[4, 256], cos=[64, NTOK], sin=[64, NTOK], wfm=[4, D, 512], wtm=[4, D, 456],
              gbias=[4, 8], convw=[4, 128, 15], mnorm=[4, 128], gnorm=[4, 128], cm=[128, 22, 128], sel=[128, 4])

    def IN(n):
        if n not in decl:
            decl[n] = P.dram(n, SH[n], F32, "ExternalInput")
        return decl[n]

    P.used_inputs = decl
    y = P.dram("y", [1024, D], F32, "ExternalOutput")
    modp = nc.dram_tensor("i_modp", [8, 3072], F32, kind="Internal")
    modg = nc.dram_tensor("i_modg", [4 * 4 * 2, 3072], F32, kind="Internal")
    mymod = nc.dram_tensor("i_mymod", [4, 2, 6 * D], F32, kind="Internal")
    hT_loc = [nc.dram_tensor(f"i_hT{k}", [CHK[k] * 128, NT_CORE], BF16, kind="Internal") for k in range(6)]
    hT_g = [nc.dram_tensor(f"i_hTg{k}", [4 * CHK[k] * 128, NT_CORE], BF16, kind="Internal") for k in range(6)]
    mix_loc = [nc.dram_tensor(f"i_mix{k}", [MIXCHK[k] * 128, NT_CORE], BF16, kind="Internal") for k in range(6)]
    mix_g = [nc.dram_tensor(f"i_mixg{k}", [4 * MIXCHK[k] * 128, NT_CORE], BF16, kind="Internal") for k in range(6)]
    x_dr = nc.dram_tensor("i_x", [NT_CORE, D], F32, kind="Internal")
    modgb, mymodb, hTgb, mixgb = Buf("modg"), Buf("mymod"), Buf("hTg"), Buf("mixg")

    def early_out(src2d_f32):
        P.dma("sp", y[0:8, 0:3072 if False else D], src2d_f32, reads=[mymodb])
        P.finish()
        return P

    fused_ada(P, IN("cT2"), IN("wada"), IN("bada"), modp)
    if stop == "ada0":
        P.dma("sp", y[0:8, :], modp[:, 0:D])
        P.finish()
        return P
    P.coll("AllGather", [modp.ap().opt()], [modg.ap().opt()], GROUPS4, writes=[modgb])
    for r in range(4):
        P.dma("sp", mymod[:, :, r * 3072:(r + 1) * 3072], modg[r * 8:(r + 1) * 8, :].rearrange("(l s) c -> l s c", l=4),
              reads=[modgb], writes=[mymodb])
    full_barrier(P)
    if stop == "ada":
        P.dma("sp", y[0:8, :], mymod[:, :, 0:D].rearrange("l s c -> (l s) c"), reads=[mymodb])
        P.finish()
        return P

    def modv(l):
        return mymod[l].rearrange("s (j v) -> s j v", j=6)

    ident, norm1, norm2 = IN("ident"), IN("norm1"), IN("norm2")
    x_in = IN("x_in")
    with ExitStack() as st:
        dn = Dense(P, st)
        dn.load_ident(ident[:])
        dn.load_x(x_in)
        m = modv(0)
        dn.prep_mod(norm1[0], m[0, 0, :], m[0, 1, :], m[1, 0, :], m[1, 1, :])
        dn.norm_mod_T()
        dn.store_hT(hT_loc)
        full_barrier(P)
    x_src = x_in
    for l in range(nlayers):
        for k in range(6):
            P.coll("AllGather", [hT_loc[k].ap().opt()], [hT_g[k].ap().opt()], GROUPS4, writes=[hTgb])
        if stop == "hT":
            with ExitStack() as st:
                tt = P.sb("dbg16", [128, NT_CORE], BF16, st)
                tf = P.sb("dbg32", [128, NT_CORE], F32, st)
                ttb = Buf("dbg")
                P.dma("sp", tt[:], hT_g[4][3 * 384:3 * 384 + 128, :], reads=[hTgb], writes=[ttb])
                P.op("dve", lambda e: e.tensor_copy(tf[:], tt[:]), reads=[ttb], writes=[ttb])
                P.dma("sp", y[0:128, 0:NT_CORE], tf[:], reads=[ttb])
            P.finish()
            return P
        with ExitStack() as st:
            mx = Mixer(P, st)
            mx.consts(ident[:])
            mx.mla((hT_g, hTgb), IN("win")[l], IN("wuq")[l], IN("wukv")[l], IN("qn")[l], IN("kvn")[l], IN("cos"), IN("sin"), IN("onorm")[l], ("dest", mix_loc))
            for k in range(3):
                P.coll("AllGather", [mix_loc[k].ap().opt()], [mix_g[k].ap().opt()], GROUPS4, writes=[mixgb])
            mx.rec((hT_g, hTgb), IN("wfm")[l], IN("wtm")[l], IN("gbias")[l], IN("convw")[l], IN("mnorm")[l], IN("gnorm")[l], IN("cm")[:], ("dest", mix_loc))
            full_barrier(P)
        for k in range(3, 6):
            P.coll("AllGather", [mix_loc[k].ap().opt()], [mix_g[k].ap().opt()], GROUPS4, writes=[mixgb])
        if stop == "mix":
            with ExitStack() as st:
                tt = P.sb("dbg16", [128, NT_CORE], BF16, st)
                tf = P.sb("dbg32", [128, NT_CORE], F32, st)
                ttb = Buf("dbg")
                P.dma("sp", tt[:], mix_g[3][2 * 384:2 * 384 + 128, :], reads=[mixgb], writes=[ttb])
                P.op("dve", lambda e: e.tensor_copy(tf[:], tt[:]), reads=[ttb], writes=[ttb])
                P.dma("sp", y[0:128, 0:NT_CORE], tf[:], reads=[ttb])
            P.finish()
            return P
        with ExitStack() as st:
            dn = Dense(P, st)
            dn.load_ident(ident[:])
            dn.load_x(x_src)
            dense_load_mix_gathered(dn, mix_g, mixgb, IN("sel")[:])
            m = modv(l)
            dn.load_gates(m[0, 2, :], m[1, 2, :])
            dn.wout(IN("w_out")[l])
            dn.prep_mod(norm2[l], m[0, 3, :], m[0, 4, :], m[1, 3, :], m[1, 4, :])
            dn.norm_mod_T()
            dn.load_gates(m[0, 5, :], m[1, 5, :])
            dn.mlp(IN("w1")[l], IN("w2")[l], nslices)
            if l < nlayers - 1:
                mn = modv(l + 1)
                dn.prep_mod(norm1[l + 1], mn[0, 0, :], mn[0, 1, :], mn[1, 0, :], mn[1, 1, :])
                dn.norm_mod_T()
                dn.store_hT(hT_loc)
                dn.store_x(x_dr)
                x_src = x_dr
            else:
                dn.final_norm(IN("final_norm")[:], y)
            full_barrier(P)
    P.finish()
    return P


def fused_inputs(inp, core):
    b, g = divmod(core, 4)
    x, c, ctx, c_ctx = inp["x"], inp["c"], inp["ctx"], inp["c_ctx"]
    cos, sin = rope_tables()
    dd = dict(ident=np.eye(128, dtype=np.float32),
              x_in=np.ascontiguousarray(np.concatenate([ctx[b, 64 * g:64 * g + 64], x[b, 1024 * g:1024 * (g + 1)]], 0)),
              cT2=np.ascontiguousarray(np.stack([c[b], c_ctx]).T),
              wada=np.ascontiguousarray(inp["w_ada"][:, :, g * 3072:(g + 1) * 3072]),
              bada=np.ascontiguousarray(inp["b_ada"][:, g * 3072:(g + 1) * 3072]),
              norm1=inp["norm1"], norm2=inp["norm2"], final_norm=inp["final_norm"],
              w_out=inp["w_out"], w1=inp["w_mlp1"], w2=inp["w_mlp2"], cos=cos, sin=sin, cm=rec_consts(),
              sel=np.ascontiguousarray(np.tile((np.arange(4) == g).astype(np.float32)[None, :], (128, 1))))
    ml = [mixer_mla_inputs(inp, l, g) for l in range(4)]
    rc = [mixer_rec_inputs(inp, l, g) for l in range(4)]
    for k in ("win", "wuq", "wukv", "qn", "kvn", "onorm"):
        dd[k] = np.ascontiguousarray(np.stack([m[k] for m in ml]))
    for k in ("wfm", "wtm", "gbias", "convw", "mnorm", "gnorm"):
        dd[k] = np.ascontiguousarray(np.stack([m[k] for m in rc]))
    return dd


def kernel_fused(**inp):
    inp = {k: np.asarray(v) for k, v in inp.items()}
    P = _prog("fused", build_fused)
    res = _run(P, [{k: v for k, v in fused_inputs(inp, core).items() if k in P.used_inputs} for core in range(8)])
    out = np.stack([np.concatenate([res[b * 4 + g]["y"] for g in range(4)], axis=0) for b in range(2)])
    return np.ascontiguousarray(out.astype(np.float32))


def kernel(**inp):
    return kernel_fused(**inp)
```

```python
import numpy as np
from contextlib import ExitStack
import concourse.bass as bass
import concourse.mybir as mybir
from concourse.bass_utils import run_bass_kernel_spmd

F32 = mybir.dt.float32
BF16 = mybir.dt.bfloat16
AF = mybir.ActivationFunctionType
ALU = mybir.AluOpType
AX = mybir.AxisListType

SEM_LIMIT = 30000
NDS = 24


class Buf:
    __slots__ = ("name", "w", "r", "excl")

    def __init__(self, name="", excl=False):
        self.name = name
        self.w = None
        self.r = {}
        self.excl = excl


class Prog:
    ENG = ("pe", "act", "dve", "pool", "sp")

    def __init__(self):
        self.nc = bass.Bass("TRN2", target_bir_lowering=False)
        self.es = ExitStack()
        nc = self.nc
        self.eng = dict(pe=nc.tensor, act=nc.scalar, dve=nc.vector, pool=nc.gpsimd, sp=nc.sync)
        self.sem = {}
        self.cnt = {}
        self.nsem = 0
        for e in self.ENG:
            self._newsem(e)
        self.known = {e: {} for e in self.ENG}
        self.dsems = [self.es.enter_context(nc.semaphore(f"dq{i}")) for i in range(NDS)]
        self.dcnt = [0] * NDS
        self.dq_range = dict(sp=(0, 14), pool=(14, 22), act=(22, 24))
        self.dq_next = dict(sp=0, pool=14, act=22)
        self.ninst = 0
        self.last = {}

    def _newsem(self, e):
        self.sem[e] = self.es.enter_context(self.nc.semaphore(f"s_{e}_{self.nsem}"))
        self.nsem += 1
        self.cnt[e] = 0

    def _wait(self, eng, tok):
        sem, val = tok[1], tok[2]
        k = self.known[eng]
        if k.get(id(sem), 0) >= val:
            return
        self.eng[eng].wait_ge(sem, val)
        k[id(sem)] = val

    def _deps(self, eng, reads, writes):
        for b in reads:
            if b.w is not None:
                if not (b.w[0] == "pe" and eng == "pe"):
                    self._wait(eng, b.w)
            if b.excl:
                for t in b.r.values():
                    if t[0] != eng:
                        self._wait(eng, t)
        for b in writes:
            if b.w is not None:
                if not (b.w[0] == "pe" and eng == "pe"):
                    self._wait(eng, b.w)
            for t in b.r.values():
                if not (t[0] == "pe" and eng == "pe"):
                    self._wait(eng, t)

    def _mark(self, tok, reads, writes):
        key = (tok[0], id(tok[1]))
        for b in reads:
            b.r[key] = tok
        for b in writes:
            b.w = tok
            b.r = {}

    def op(self, eng, fn, reads=(), writes=()):
        self._deps(eng, reads, writes)
        inst = fn(self.eng[eng])
        self.cnt[eng] += 1
        inst.then_inc(self.sem[eng], 1)
        tok = (eng, self.sem[eng], self.cnt[eng])
        self.last[eng] = tok
        self._mark(tok, reads, writes)
        if self.cnt[eng] >= SEM_LIMIT:
            self._newsem(eng)
        self.ninst += 1
        return inst

    def dma(self, q, out, in_, reads=(), writes=()):
        self._deps(q, reads, writes)
        s = self.dq_next[q]
        lo, hi = self.dq_range[q]
        self.dq_next[q] = lo + (s + 1 - lo) % (hi - lo)
        if self.dcnt[s] > 0:
            self._wait(q, ("dma", self.dsems[s], self.dcnt[s]))
        if self.dcnt[s] >= SEM_LIMIT:
            self.dsems[s] = self.es.enter_context(self.nc.semaphore(f"dq{s}_{self.nsem}"))
            self.nsem += 1
            self.dcnt[s] = 0
        self.dcnt[s] += 16
        self.eng[q].dma_start(out=out, in_=in_).then_inc(self.dsems[s], 16)
        tok = ("dma", self.dsems[s], self.dcnt[s])
        self._mark(tok, reads, writes)
        self.ninst += 1

    def coll(self, kind, ins, outs, groups, reads=(), writes=()):
        self._deps("pool", reads, writes)
        if not hasattr(self, "csem"):
            self.csem = self.es.enter_context(self.nc.semaphore("ccsem"))
            self.ccnt = 0
        self.ccnt += 1
        self.nc.gpsimd.collective_compute(kind, ALU.bypass, replica_groups=groups, ins=list(ins), outs=list(outs)).then_inc(self.csem, 1)
        tok = ("cc", self.csem, self.ccnt)
        self._mark(tok, reads, writes)

    def finish(self, bufs=()):
        for s in range(NDS):
            if self.dcnt[s] > 0:
                self._wait("sp", ("dma", self.dsems[s], self.dcnt[s]))
        for e in ("pe", "act", "dve", "pool"):
            if e in self.last:
                self._wait("sp", self.last[e])
        if hasattr(self, "csem") and self.ccnt > 0:
            self._wait("sp", ("cc", self.csem, self.ccnt))

    def sb(self, name, shape, dtype, stack=None):
        self.uid = getattr(self, "uid", 0) + 1
        return (stack or self.es).enter_context(self.nc.sbuf_tensor(f"s_{name}_{self.uid}", list(shape), dtype))

    def ps(self, name, shape, dtype, stack=None):
        self.uid = getattr(self, "uid", 0) + 1
        return (stack or self.es).enter_context(self.nc.psum_tensor(f"p_{name}_{self.uid}", list(shape), dtype))

    def dram(self, name, shape, dtype, kind):
        return self.nc.dram_tensor(name, list(shape), dtype, kind=kind)


D = 2048
NT_CORE = 1088
NTOK = 4352
DFF = 8192
EPS = 1e-6
DTILES = [(64, 0)] + [(128, 64 + 128 * i) for i in range(8)]


def bc_mid(ap2d, n):
    p, k = ap2d.shape
    return ap2d.unsqueeze(2).to_broadcast([p, k, n])


class Dense:
    def __init__(self, P, stack):
        self.P = P
        sb = lambda n, s, d: P.sb(n, s, d, stack)
        self.xs = sb("xs", [128, 9, D], F32)
        self.xb = [Buf(f"x{i}") for i in range(9)]
        self.hm = sb("hm", [128, 16, NT_CORE], BF16)
        self.hmb = Buf("hm")
        self.wA = [sb(f"wA{i}", [128, 16, 512], BF16) for i in range(2)]
        self.wAb = [Buf(f"wA{i}") for i in range(2)]
        self.wB = sb("wB", [128, 4, D], BF16)
        self.wBb = Buf("wB")
        self.gl = sb("gate_l", [128, D], F32)
        self.gc = sb("gate_c", [128, D], F32)
        self.glb = Buf("gl")
        self.gcb = Buf("gc")
        self.xn = sb("xn", [128, D], F32)
        self.xnb = Buf("xn")
        self.a1 = [sb(f"a1T{i}", [128, 4, 512], BF16) for i in range(3)]
        self.a1b = [Buf(f"a1T{i}") for i in range(3)]
        self.tmp = [sb(f"dtmp{i}", [128, 512], F32) for i in range(4)]
        self.tmpb = [Buf(f"dtmp{i}") for i in range(4)]
        self.idf = sb("idf", [128, 128], F32)
        self.idfb = Buf("idf")
        self.fm = sb("fmvec", [128, 16, 16], F32)
        self.fmb = Buf("fm")
        self.st = sb("stat", [128, 16], F32)
        self.stb = Buf("stat")
        self.pT = P.ps("pT", [128, 16, 128], F32, stack)
        self.pTb = Buf("pT", True)
        self.pm = [P.ps(f"pm{i}", [128, 512], F32, stack) for i in range(4)]
        self.pmb = [Buf(f"pm{i}", True) for i in range(4)]
        self.pmi = 0
        self.tmi = 0

    def load_ident(self, ident_d):
        self.P.dma("sp", self.idf[:], ident_d, writes=[self.idfb])

    def load_x(self, x_d):
        P = self.P
        for ti, (r, off) in enumerate(DTILES):
            P.dma("sp", self.xs[0:r, ti, :], x_d[off:off + r, :], writes=[self.xb[ti]])

    def store_x(self, x_d):
        P = self.P
        for ti, (r, off) in enumerate(DTILES):
            P.dma("sp", x_d[off:off + r, :], self.xs[0:r, ti, :], reads=[self.xb[ti]])

    def load_fm_vec(self, slot, src1d):
        P = self.P
        with P.nc.allow_non_contiguous_dma(reason="small vector"):
            P.dma("sp", self.fm[:, slot, :], src1d.rearrange("(k p) -> p k", p=128), writes=[self.fmb])

    def prep_mod(self, norm_d, shift_l, scale_l, shift_c, scale_c):
        P = self.P
        fm = self.fm
        self.load_fm_vec(4, norm_d)
        self.load_fm_vec(5, scale_l)
        self.load_fm_vec(1, shift_l)
        self.load_fm_vec(6, scale_c)
        self.load_fm_vec(3, shift_c)
        P.op("dve", lambda e: e.scalar_tensor_tensor(fm[:, 0, :], fm[:, 5, :], 1.0, fm[:, 4, :], ALU.add, ALU.mult),
             reads=[self.fmb], writes=[self.fmb])
        P.op("dve", lambda e: e.scalar_tensor_tensor(fm[:, 2, :], fm[:, 6, :], 1.0, fm[:, 4, :], ALU.add, ALU.mult),
             reads=[self.fmb], writes=[self.fmb])

    def norm_mod_T(self):
        P = self.P
        xs, xn, st, pT, hm, fm, idf = self.xs, self.xn, self.st, self.pT, self.hm, self.fm, self.idf
        for ti, (r, off) in enumerate(DTILES):
            xb = self.xb[ti]
            P.op("act", lambda e: e.activation(xn[0:r, :], xs[0:r, ti, :], AF.Square, accum_out=st[0:r, 0:1]),
                 reads=[xb], writes=[self.xnb, self.stb])
            P.op("act", lambda e: e.activation(st[0:r, 1:2], st[0:r, 0:1], AF.Sqrt, bias=EPS, scale=1.0 / D),
                 reads=[self.stb], writes=[self.stb])
            P.op("dve", lambda e: e.reciprocal(st[0:r, 2:3], st[0:r, 1:2]), reads=[self.stb], writes=[self.stb])
            P.op("act", lambda e: e.activation(xn[0:r, :], xs[0:r, ti, :], AF.Copy, scale=st[0:r, 2:3]),
                 reads=[xb, self.stb], writes=[self.xnb])
            for k in range(16):
                P.op("pe", lambda e: e.transpose(pT[:, k, 0:r], xn[0:r, k * 128:(k + 1) * 128], idf[0:r, 0:r]),
                     reads=[self.xnb, self.idfb], writes=[self.pTb])
            gs, ss = (2, 3) if ti == 0 else (0, 1)
            t3 = self.xn[:].rearrange("p (k t) -> p k t", k=16)
            P.op("dve", lambda e: e.tensor_tensor(t3[:, :, 0:r], pT[:, :, 0:r], bc_mid(fm[:, gs, :], r), ALU.mult),
                 reads=[self.pTb, self.fmb], writes=[self.xnb])
            P.op("pool", lambda e: e.tensor_tensor(hm[:, :, off:off + r], t3[:, :, 0:r], bc_mid(fm[:, ss, :], r), ALU.add),
                 reads=[self.xnb, self.fmb], writes=[self.hmb])

    def store_hT(self, hT_d):
        P = self.P
        if isinstance(hT_d, list):
            for k in range(6):
                P.dma("sp", hT_d[k].rearrange("(b p) t -> p b t", p=128), self.hm[:, 3 * k:3 * k + CHK[k], :], reads=[self.hmb])
            return
        P.dma("sp", hT_d.rearrange("(k p) t -> p k t", p=128), self.hm[:], reads=[self.hmb])

    def load_mixT(self, mixT_d):
        P = self.P
        P.dma("sp", self.hm[:], mixT_d.rearrange("(k p) t -> p k t", p=128), writes=[self.hmb])

    def load_gates(self, g_l, g_c):
        P = self.P
        P.dma("sp", self.gl[:], g_l.partition_broadcast(128), writes=[self.glb])
        P.dma("sp", self.gc[:], g_c.partition_broadcast(128), writes=[self.gcb])

    def _acc(self, ps, psb, ti, r, cc, prescaled=False):
        P = self.P
        if prescaled:
            xsl = self.xs[0:r, ti, cc * 512:(cc + 1) * 512]
            P.op("dve", lambda e: e.tensor_tensor(xsl, ps[0:r, :], xsl, ALU.add), reads=[psb, self.xb[ti]], writes=[self.xb[ti]])
            return
        g, gb = (self.gc, self.gcb) if ti == 0 else (self.gl, self.glb)
        j = self.tmi
        self.tmi = (j + 1) % 4
        tmp, tb = self.tmp[j], self.tmpb[j]
        P.op("dve", lambda e: e.tensor_tensor(tmp[0:r, :], ps[0:r, :], g[0:r, cc * 512:(cc + 1) * 512], ALU.mult),
             reads=[psb, gb], writes=[tb])
        xsl = self.xs[0:r, ti, cc * 512:(cc + 1) * 512]
        P.op("pool", lambda e: e.tensor_tensor(xsl, xsl, tmp[0:r, :], ALU.add), reads=[tb, self.xb[ti]], writes=[self.xb[ti]])

    def _nextpm(self):
        i = self.pmi
        self.pmi = (i + 1) % 4
        return self.pm[i], self.pmb[i]

    def wout(self, w_d):
        P = self.P
        for cc in range(4):
            wa, wab = self.wA[cc % 2], self.wAb[cc % 2]
            P.dma("pool", wa[:], w_d[:, cc * 512:(cc + 1) * 512].rearrange("(k p) c -> p k c", p=128), writes=[wab])
            for ti, (r, off) in enumerate(DTILES):
                if ti == 1:
                    P.op("pool", lambda e: e.tensor_tensor(wa[:], wa[:], bc_midrep(self.gl[:, cc * 512:(cc + 1) * 512], 16), ALU.mult),
                         reads=[wab, self.glb], writes=[wab])
                ps, psb = self._nextpm()
                for k in range(16):
                    P.op("pe", lambda e: e.matmul(ps[0:r, :], self.hm[:, k, off:off + r], wa[:, k, :], start=(k == 0), stop=(k == 15)),
                         reads=[self.hmb, wab], writes=[psb])
                self._acc(ps, psb, ti, r, cc, prescaled=(ti >= 1))

    def mlp(self, w1_d, w2_d, nslices=16):
        P = self.P
        blocks = [(0, 64, [0]), (64, 512, [1, 2, 3, 4]), (576, 512, [5, 6, 7, 8])]
        for s in range(nslices):
            wa, wab = self.wA[s % 2], self.wAb[s % 2]
            P.dma("pool", wa[:], w1_d[:, s * 512:(s + 1) * 512].rearrange("(k p) c -> p k c", p=128), writes=[wab])
            P.dma("pool", self.wB[:], w2_d[s * 512:(s + 1) * 512, :].rearrange("(f p) c -> p f c", p=128), writes=[self.wBb])
            for bi, (boff, bn, tis) in enumerate(blocks):
                a1, a1b = self.a1[bi], self.a1b[bi]
                for f in range(4):
                    ps, psb = self._nextpm()
                    for k in range(16):
                        P.op("pe", lambda e: e.matmul(ps[:, 0:bn], wa[:, k, f * 128:(f + 1) * 128], self.hm[:, k, boff:boff + bn],
                                                      start=(k == 0), stop=(k == 15)),
                             reads=[self.hmb, wab], writes=[psb])
                    j = self.tmi
                    self.tmi = (j + 1) % 4
                    tmp, tb = self.tmp[j], self.tmpb[j]
                    P.op("act", lambda e: e.activation(tmp[:, 0:bn], ps[:, 0:bn], AF.Relu), reads=[psb], writes=[tb])
                    P.op("dve", lambda e: e.tensor_tensor(a1[:, f, 0:bn], tmp[:, 0:bn], tmp[:, 0:bn], ALU.mult), reads=[tb], writes=[a1b])
            for bi, (boff, bn, tis) in enumerate(blocks):
                a1, a1b = self.a1[bi], self.a1b[bi]
                if bi == 1:
                    P.op("pool", lambda e: e.tensor_tensor(self.wB[:], self.wB[:], bc_midrep(self.gl[:], 4), ALU.mult),
                         reads=[self.wBb, self.glb], writes=[self.wBb])
                for ti in tis:
                    r, off = DTILES[ti]
                    lo = off - boff
                    for cc in range(4):
                        ps, psb = self._nextpm()
                        for f in range(4):
                            P.op("pe", lambda e: e.matmul(ps[0:r, :], a1[:, f, lo:lo + r], self.wB[:, f, cc * 512:(cc + 1) * 512],
                                                          start=(f == 0), stop=(f == 3)),
                                 reads=[a1b, self.wBb], writes=[psb])
                        self._acc(ps, psb, ti, r, cc, prescaled=(ti >= 1))

    def final_norm(self, fn_d, y_d):
        P = self.P
        xs, xn, st = self.xs, self.xn, self.st
        P.dma("sp", self.gl[:], fn_d.partition_broadcast(128), writes=[self.glb])
        for ti in range(1, 9):
            r, off = DTILES[ti]
            xb = self.xb[ti]
            P.op("act", lambda e: e.activation(xn[0:r, :], xs[0:r, ti, :], AF.Square, accum_out=st[0:r, 0:1]),
                 reads=[xb], writes=[self.xnb, self.stb])
            P.op("act", lambda e: e.activation(st[0:r, 1:2], st[0:r, 0:1], AF.Sqrt, bias=EPS, scale=1.0 / D),
                 reads=[self.stb], writes=[self.stb])
            P.op("dve", lambda e: e.reciprocal(st[0:r, 2:3], st[0:r, 1:2]), reads=[self.stb], writes=[self.stb])
            P.op("dve", lambda e: e.scalar_tensor_tensor(xn[0:r, :], xs[0:r, ti, :], st[0:r, 2:3], self.gl[0:r, :], ALU.mult, ALU.mult),
                 reads=[xb, self.stb, self.glb], writes=[self.xnb])
            P.dma("sp", y_d[off - 64:off - 64 + r, :], xn[0:r, :], reads=[self.xnb])


def build_dense(mode, nslices=16):
    P = Prog()
    ident = P.dram("ident", [128, 128], F32, "ExternalInput")
    x_in = P.dram("x_in", [NT_CORE, D], F32, "ExternalInput")
    dn = Dense(P, P.es)
    dn.load_ident(ident[:])
    dn.load_x(x_in)
    if mode in ("C", "CF"):
        mixT = P.dram("mixT", [D, NT_CORE], BF16, "ExternalInput")
        modv = P.dram("modv", [2, 6, D], F32, "ExternalInput")
        norm2 = P.dram("norm2", [D], F32, "ExternalInput")
        w_out = P.dram("w_out", [D, D], F32, "ExternalInput")
        w1 = P.dram("w1", [D, DFF], F32, "ExternalInput")
        w2 = P.dram("w2", [DFF, D], F32, "ExternalInput")
        dn.load_mixT(mixT)
        dn.load_gates(modv[0, 2, :], modv[1, 2, :])
        dn.wout(w_out)
        dn.prep_mod(norm2[:], modv[0, 3, :], modv[0, 4, :], modv[1, 3, :], modv[1, 4, :])
        dn.norm_mod_T()
        dn.load_gates(modv[0, 5, :], modv[1, 5, :])
        dn.mlp(w1, w2, nslices)
    if mode in ("A", "C"):
        modn = P.dram("modn", [2, 6, D], F32, "ExternalInput")
        norm1 = P.dram("norm1", [D], F32, "ExternalInput")
        hT = P.dram("hT", [D, NT_CORE], BF16, "ExternalOutput")
        dn.prep_mod(norm1[:], modn[0, 0, :], modn[0, 1, :], modn[1, 0, :], modn[1, 1, :])
        dn.norm_mod_T()
        dn.store_hT(hT)
    if mode == "C":
        x_out = P.dram("x_out", [NT_CORE, D], F32, "ExternalOutput")
        dn.store_x(x_out)
    if mode == "CF":
        fn = P.dram("final_norm", [D], F32, "ExternalInput")
        y = P.dram("y", [1024, D], F32, "ExternalOutput")
        dn.final_norm(fn[:], y)
    P.finish()
    return P


def build_ada():
    P = Prog()
    cT = P.dram("cT", [D, 3], F32, "ExternalInput")
    wada = P.dram("wada", [4, D, 1536], F32, "ExternalInput")
    bada = P.dram("bada", [4, 1536], F32, "ExternalInput")
    modp = P.dram("modp", [4, 3, 1536], F32, "ExternalOutput")
    sT = P.sb("sT", [128, 16, 3], F32)
    sTb = Buf("sT")
    w = [P.sb(f"w{i}", [128, 16, 512], F32) for i in range(2)]
    wb = [Buf(f"w{i}") for i in range(2)]
    bt = [P.sb(f"bt{i}", [3, 512], F32) for i in range(2)]
    btb = [Buf(f"bt{i}") for i in range(2)]
    ot = [P.sb(f"ot{i}", [3, 512], F32) for i in range(2)]
    otb = [Buf(f"ot{i}") for i in range(2)]
    ps = [P.ps(f"ps{i}", [128, 512], F32) for i in range(2)]
    psb = [Buf(f"ps{i}", True) for i in range(2)]
    P.dma("sp", sT[:], cT.rearrange("(k p) r -> p k r", p=128), writes=[sTb])
    P.op("act", lambda e: e.activation(sT[:], sT[:], AF.Silu), reads=[sTb], writes=[sTb])
    i = 0
    for l in range(4):
        for cc in range(3):
            j = i % 2
            i += 1
            P.dma("sp", w[j][:], wada[l, :, cc * 512:(cc + 1) * 512].rearrange("(k p) c -> p k c", p=128), writes=[wb[j]])
            P.dma("sp", bt[j][:], bada[l, cc * 512:(cc + 1) * 512].partition_broadcast(3), writes=[btb[j]])
            for k in range(16):
                P.op("pe", lambda e: e.matmul(ps[j][0:3, :], sT[:, k, :], w[j][:, k, :], start=(k == 0), stop=(k == 15)),
                     reads=[sTb, wb[j]], writes=[psb[j]])
            P.op("dve", lambda e: e.tensor_tensor(ot[j][:], ps[j][0:3, :], bt[j][:], ALU.add), reads=[psb[j], btb[j]], writes=[otb[j]])
            P.dma("sp", modp[2 * l:2 * l + 2, cc * 512:(cc + 1) * 512], ot[j][:], reads=[otb[j]])
    P.finish()
    return P


TCH = [(0, 256)] + [(256 + 512 * i, 512) for i in range(8)]
NTILE = 34
ATT_SCALE = 192 ** -0.5


class Mixer:
    def __init__(self, P, stack):
        self.P = P
        self.stack = stack
        sb = lambda n, s, d: P.sb(n, s, d, stack)
        self.idf = sb("m_idf", [128, 128], F32)
        self.idb = sb("m_idb", [128, 128], BF16)
        self.onesb = sb("m_onesb", [128, 128], BF16)
        self.onesf = sb("m_onesf", [128, 128], F32)
        self.cb = Buf("consts")
        self.pb = [P.ps(f"mpb{i}", [128, 512], F32, stack) for i in range(7)]
        self.pbb = [Buf(f"mpb{i}", True) for i in range(7)]
        self.ptb = P.ps("mptb", [128, 128], BF16, stack)
        self.ptbb = Buf("mptb", True)
        self.pi = 0

    def consts(self, ident_d):
        P = self.P
        P.dma("sp", self.idf[:], ident_d, writes=[self.cb])
        P.dma("pool", self.idb[:], ident_d, writes=[self.cb])
        P.op("dve", lambda e: e.memset(self.onesb[:], 1.0), writes=[self.cb])
        P.op("dve", lambda e: e.memset(self.onesf[:], 1.0), writes=[self.cb])

    def hload(self, h, hb, c0, n, hT_d):
        P = self.P
        if not isinstance(hT_d, tuple):
            P.dma("sp", h[:, :, 0:n], hT_d[:, c0:c0 + n].rearrange("(k p) t -> p k t", p=128), writes=[hb])
            return
        gts, gbuf = hT_d
        if c0 == 0:
            pieces = [(r, 0, 64, r * 64) for r in range(4)]
        else:
            i = (c0 - 256) // 512
            pieces = [(i // 2, 64 + (i % 2) * 512, n, 0)]
        for (r, col0, m, dst0) in pieces:
            for c_ in range(6):
                nk = CHK[c_]
                P.dma("sp", h[:, 3 * c_:3 * c_ + nk, dst0:dst0 + m],
                      gts[c_][r * nk * 128:(r + 1) * nk * 128, col0:col0 + m].rearrange("(b p) t -> p b t", p=128),
                      reads=[gbuf], writes=[hb])

    def mix_store(self, mixT_d, row0, tok0, n, src, reads):
        P = self.P
        if not isinstance(mixT_d, tuple):
            P.dma("sp", mixT_d[row0:row0 + 128, tok0:tok0 + n], src[:, 0:n], reads=reads)
            return
        dt_ = mixT_d[1]
        t = tok0
        while t < tok0 + n:
            if t < 256:
                j, col, seg_end = t // 64, t % 64, (t // 64 + 1) * 64
            else:
                j, col, seg_end = (t - 256) // 1024, 64 + (t - 256) % 1024, 256 + ((t - 256) // 1024 + 1) * 1024
            m = min(seg_end, tok0 + n) - t
            c_, lk = mix_chunk((row0 // 128) * 4 + j)
            P.dma("sp", dt_[c_][lk * 128:lk * 128 + 128, col:col + m], src[:, t - tok0:t - tok0 + m], reads=reads)
            t += m

    def nb(self, lo=0, hi=7):
        n = hi - lo
        i = lo + (self.pi % n)
        self.pi += 1
        return self.pb[i], self.pbb[i]

    def mla(self, hT_d, win_d, wuq_d, wukv_d, qn_d, kvn_d, cos_d, sin_d, onorm_d, mixT_d):
        P = self.P
        with ExitStack() as st:
            sb = lambda n, s, d: P.sb(n, s, d, st)
            QN = sb("QN", [128, 2, NTOK], BF16)
            QPE = sb("QPE", [65, 2, NTOK], BF16)
            KN = sb("KN", [128, 2, NTOK], BF16)
            KPE = sb("KPE", [65, NTOK], BF16)
            V = sb("V", [128, NTILE, 2, 129], BF16)
            qb = [Buf(f"q{i}") for i in range(9)]
            kb = [Buf(f"k{i}") for i in range(9)]
            initb = Buf("init")
            gain = sb("ogain", [128, 2, 128], F32)
            gainb = Buf("ogain")
            qn = sb("qnfm", [128, 4], F32)
            kvn = sb("kvnfm", [128, 2], F32)
            nb_ = Buf("normfm")
            P.dma("sp", qn[:], qn_d, writes=[nb_])
            P.dma("sp", kvn[:], kvn_d, writes=[nb_])
            P.dma("sp", gain[:], onorm_d.partition_broadcast(128), writes=[gainb])
            P.op("pool", lambda e: e.memset(KPE[64:65, :], 1.0), writes=[initb])
            P.op("pool", lambda e: e.memset(QPE[64:65, :, :], 0.0), writes=[initb])
            P.op("pool", lambda e: e.memset(V[:, :, :, 128:129], 1.0), writes=[initb])
            with ExitStack() as st1:
                sb1 = lambda n, s, d: P.sb(n, s, d, st1)
                win = sb1("win", [128, 16, 896], BF16)
                wuq = sb1("wuq", [128, 4, 512], BF16)
                wukv = sb1("wukv", [128, 2, 512], BF16)
                wb = Buf("mlaw")
                P.dma("pool", win[:], win_d.rearrange("(k p) c -> p k c", p=128), writes=[wb])
                P.dma("pool", wuq[:], wuq_d.rearrange("(k p) c -> p k c", p=128), writes=[wb])
                P.dma("pool", wukv[:], wukv_d.rearrange("(k p) c -> p k c", p=128), writes=[wb])
                hc = [sb1(f"hc{i}", [128, 16, 512], BF16) for i in range(2)]
                hcb = [Buf(f"hc{i}") for i in range(2)]
                cs = [sb1(f"cs{i}", [64, 2, 512], F32) for i in range(2)]
                csb = [Buf(f"cs{i}") for i in range(2)]
                cq = sb1("cq", [128, 6, 512], F32)
                cqb = Buf("cq")
                sq = sb1("sq", [128, 6, 512], BF16)
                sqb = Buf("sq")
                cn = sb1("cn", [128, 6, 512], BF16)
                cnb = Buf("cn")
                rs = [sb1(f"rs{i}", [128, 512], F32) for i in range(2)]
                rsb = [Buf(f"rs{i}") for i in range(2)]
                rt = [sb1(f"rt{i}", [64, 512], F32) for i in range(4)]
                rtb = [Buf(f"rt{i}") for i in range(4)]
                rti = [0]

                def rope(psA, psAb, psB, psBb, n, cst, cstb, dst):
                    i = rti[0]
                    rti[0] = (i + 2) % 4
                    t1, t1b, t2, t2b = rt[i], rtb[i], rt[i + 1], rtb[i + 1]
                    P.op("dve", lambda e: e.tensor_tensor(t1[:, 0:n], psA[0:64, 0:n], cst[:, 0, 0:n], ALU.mult),
                         reads=[psAb, cstb], writes=[t1b])
                    P.op("dve", lambda e: e.tensor_tensor(t2[:, 0:n], psB[0:64, 0:n], cst[:, 1, 0:n], ALU.mult),
                         reads=[psBb, cstb], writes=[t2b])
                    return t1, t1b, t2, t2b

                import os as _os
                _nch = int(_os.environ.get("MLA_NCH", "9"))
                for ci, (c0, n) in enumerate(TCH[:_nch]):
                    h, hb = hc[ci % 2], hcb[ci % 2]
                    self.hload(h, hb, c0, n, hT_d)
                    ct, ctb = cs[ci % 2], csb[ci % 2]
                    P.dma("sp", ct[:, 0, 0:n], cos_d[:, c0:c0 + n], writes=[ctb])
                    P.dma("sp", ct[:, 1, 0:n], sin_d[:, c0:c0 + n], writes=[ctb])
                    _stg = int(_os.environ.get("MLA_STAGE", "9"))
                    if _stg < 1:
                        continue
                    for grp in range(6):
                        ps, psb = self.nb()
                        for k in range(16):
                            P.op("pe", lambda e: e.matmul(ps[:, 0:n], win[:, k, grp * 128:(grp + 1) * 128], h[:, k, 0:n],
                                                          start=(k == 0), stop=(k == 15)), reads=[wb, hb], writes=[psb])
                        P.op("dve", lambda e: e.tensor_copy(cq[:, grp, 0:n], ps[:, 0:n]), reads=[psb], writes=[cqb])
                        P.op("act", lambda e: e.activation(sq[:, grp, 0:n], ps[:, 0:n], AF.Square), reads=[psb], writes=[sqb])
                    _stg = int(_os.environ.get("MLA_STAGE", "9"))
                    if _stg < 2:
                        continue
                    for (g0, g1, nfeat, nv, r_, r_b) in ((0, 4, 512, qn, rs[0], rsb[0]), (4, 6, 256, kvn, rs[1], rsb[1])):
                        ps, psb = self.nb()
                        for grp in range(g0, g1):
                            P.op("pe", lambda e: e.matmul(ps[:, 0:n], self.onesb[:], sq[:, grp, 0:n], start=(grp == g0), stop=(grp == g1 - 1)),
                                 reads=[self.cb, sqb], writes=[psb])
                        P.op("act", lambda e: e.activation(r_[:, 0:n], ps[:, 0:n], AF.Sqrt, bias=EPS, scale=1.0 / nfeat),
                             reads=[psb], writes=[r_b])
                        P.op("dve", lambda e: e.reciprocal(r_[:, 0:n], r_[:, 0:n]), reads=[r_b], writes=[r_b])
                        for grp in range(g0, g1):
                            P.op("dve", lambda e: e.scalar_tensor_tensor(cn[:, grp, 0:n], cq[:, grp, 0:n], nv[:, grp - g0:grp - g0 + 1],
                                                                         r_[:, 0:n], ALU.mult, ALU.mult),
                                 reads=[cqb, nb_, r_b], writes=[cnb])
                    if _stg < 3:
                        continue
                    psA, psAb = self.nb()
                    psB, psBb = self.nb()
                    for (ps, psb, c_) in ((psA, psAb, 768), (psB, psBb, 832)):
                        for k in range(16):
                            P.op("pe", lambda e: e.matmul(ps[0:64, 0:n], win[:, k, c_:c_ + 64], h[:, k, 0:n], start=(k == 0), stop=(k == 15)),
                                 reads=[wb, hb], writes=[psb])
                    t1, t1b, t2, t2b = rope(psA, psAb, psB, psBb, n, ct, ctb, None)
                    P.op("pool", lambda e: e.tensor_tensor(KPE[0:64, c0:c0 + n], t1[:, 0:n], t2[:, 0:n], ALU.add),
                         reads=[t1b, t2b], writes=[kb[ci]])
                    if _stg < 4:
                        continue
                    for hh in range(2):
                        ps, psb = self.nb()
                        for k in range(4):
                            P.op("pe", lambda e: e.matmul(ps[:, 0:n], wuq[:, k, hh * 128:(hh + 1) * 128], cn[:, k, 0:n], start=(k == 0), stop=(k == 3)),
                                 reads=[wb, cnb], writes=[psb])
                        P.op("act", lambda e: e.copy(QN[:, hh, c0:c0 + n], ps[:, 0:n]), reads=[psb], writes=[qb[ci]])
                        psA, psAb = self.nb()
                        psB, psBb = self.nb()
                        for (ps, psb, c_) in ((psA, psAb, 256 + hh * 128), (psB, psBb, 256 + hh * 128 + 64)):
                            for k in range(4):
                                P.op("pe", lambda e: e.matmul(ps[0:64, 0:n], wuq[:, k, c_:c_ + 64], cn[:, k, 0:n], start=(k == 0), stop=(k == 3)),
                                     reads=[wb, cnb], writes=[psb])
                        t1, t1b, t2, t2b = rope(psA, psAb, psB, psBb, n, ct, ctb, None)
                        P.op("pool", lambda e: e.tensor_tensor(QPE[0:64, hh, c0:c0 + n], t1[:, 0:n], t2[:, 0:n], ALU.add),
                             reads=[t1b, t2b], writes=[qb[ci]])
                        ps, psb = self.nb()
                        for k in range(2):
                            P.op("pe", lambda e: e.matmul(ps[:, 0:n], wukv[:, k, hh * 128:(hh + 1) * 128], cn[:, 4 + k, 0:n], start=(k == 0), stop=(k == 1)),
                                 reads=[wb, cnb], writes=[psb])
                        P.op("dve", lambda e: e.tensor_copy(KN[:, hh, c0:c0 + n], ps[:, 0:n]), reads=[psb], writes=[kb[ci]])
                    if _stg < 5:
                        continue
                    for j in range(n // 128):
                        t = (c0 + j * 128) // 128
                        ps, psb = self.nb()
                        for k in range(2):
                            P.op("pe", lambda e: e.matmul(ps[:, 0:256], cn[:, 4 + k, j * 128:(j + 1) * 128], wukv[:, k, 256:512], start=(k == 0), stop=(k == 1)),
                                 reads=[wb, cnb], writes=[psb])
                        P.op("act", lambda e: e.copy(V[:, t, :, 0:128], ps[:, 0:256].rearrange("p (h d) -> p h d", h=2)),
                             reads=[psb], writes=[kb[ci]])
                self.barrier()
            import os
            if os.environ.get("MLA_DBG") == "1":
                P.dma("sp", mixT_d[0:128, :], QN[:, 0, :], reads=qb)
                P.dma("sp", mixT_d[128:256, :], KN[:, 0, :], reads=kb)
                P.dma("sp", mixT_d[256:320, :], QPE[0:64, 0, :], reads=qb)
                P.dma("sp", mixT_d[320:384, :], KPE[0:64, :], reads=kb)
                self.barrier()
                return
            with ExitStack() as st2:
                sb2 = lambda n, s, d: P.sb(n, s, d, st2)
                PT = [sb2(f"PT{i}", [128, 512], BF16) for i in range(3)]
                PTb = [Buf(f"PT{i}") for i in range(3)]
                ostg = [sb2(f"ostg{i}", [128, 512], BF16) for i in range(2)]
                ostgb = [Buf(f"ostg{i}") for i in range(2)]
                yb_ = [sb2(f"yb{i}", [128, 128], F32) for i in range(4)]
                ya_ = [sb2(f"ya{i}", [128, 128], BF16) for i in range(4)]
                sc_ = [sb2(f"sc{i}", [128, 8], F32) for i in range(4)]
                ybb = [Buf(f"yb{i}") for i in range(4)]
                junk = sb2("junk", [128, 128], F32)
                junkb = Buf("junk")
                pti = 0
                oi = 0
                qlist = [(256 + 512 * i, 512, 0, NTILE) for i in range(8)] + [(0, 256, 0, 2)]
                for hh in range(2):
                    for (q0, qn_, kt0, kt1) in qlist:
                        nqs = qn_ // 128
                        qci = 0 if q0 == 0 else 1 + (q0 - 256) // 512
                        def emit_qk(kt):
                            kci = 0 if kt < 2 else 1 + (kt - 2) // 4
                            ps, psb = self.nb(0, 3)
                            P.op("pe", lambda e: e.matmul(ps[:, 0:qn_], KN[:, hh, kt * 128:(kt + 1) * 128], QN[:, hh, q0:q0 + qn_], start=True, stop=False),
                                 reads=[kb[kci], qb[qci]], writes=[psb])
                            P.op("pe", lambda e: e.matmul(ps[:, 0:qn_], KPE[0:65, kt * 128:(kt + 1) * 128], QPE[0:65, hh, q0:q0 + qn_], start=False, stop=True),
                                 reads=[kb[kci], qb[qci], initb], writes=[psb])
                            return ps, psb, kci

                        nxt = emit_qk(kt0)
                        for kt in range(kt0, kt1):
                            ps, psb, kci = nxt
                            if kt + 1 < kt1:
                                nxt = emit_qk(kt + 1)
                            pt, ptb_ = PT[pti % 3], PTb[pti % 3]
                            pti += 1
                            P.op("act", lambda e: e.activation(pt[:, 0:qn_], ps[:, 0:qn_], AF.Exp, scale=ATT_SCALE), reads=[psb], writes=[ptb_])
                            for qs in range(nqs):
                                P.op("pe", lambda e: e.matmul(self.pb[3 + qs][:, 0:129], pt[:, qs * 128:(qs + 1) * 128], V[:, kt, hh, :],
                                                              start=(kt == kt0), stop=(kt == kt1 - 1)),
                                     reads=[ptb_, kb[kci], initb], writes=[self.pbb[3 + qs]])
                        og, ogb = ostg[oi % 2], ostgb[oi % 2]
                        oi += 1
                        for qs in range(nqs):
                            o, ob = self.pb[3 + qs], self.pbb[3 + qs]
                            sc, y, ya, yb2 = sc_[qs], yb_[qs], ya_[qs], ybb[qs]
                            P.op("dve", lambda e: e.reciprocal(sc[:, 0:1], o[:, 128:129]), reads=[ob], writes=[yb2])
                            P.op("dve", lambda e: e.tensor_scalar(y[:], o[:, 0:128], sc[:, 0:1], None, ALU.mult), reads=[ob, yb2], writes=[yb2])
                            P.op("dve", lambda e: e.scalar_tensor_tensor(junk[:], y[:], 1.0, y[:], ALU.mult, ALU.mult, accum_out=sc[:, 1:2]),
                                 reads=[yb2], writes=[yb2, junkb])
                            P.op("act", lambda e: e.activation(sc[:, 2:3], sc[:, 1:2], AF.Ln, bias=EPS, scale=1.0 / 128), reads=[yb2], writes=[yb2])
                            P.op("act", lambda e: e.activation(sc[:, 3:4], sc[:, 2:3], AF.Exp, scale=-0.5), reads=[yb2], writes=[yb2])
                            P.op("dve", lambda e: e.scalar_tensor_tensor(ya[:], y[:], sc[:, 3:4], gain[:, hh, :], ALU.mult, ALU.mult),
                                 reads=[yb2, gainb], writes=[yb2])
                            P.op("pe", lambda e: e.transpose(self.ptb[:], ya[:], self.idb[:]), reads=[yb2, self.cb], writes=[self.ptbb])
                            P.op("dve", lambda e: e.tensor_copy(og[:, qs * 128:(qs + 1) * 128], self.ptb[:]), reads=[self.ptbb], writes=[ogb])
                        self.mix_store(mixT_d, hh * 128, q0, qn_, og, [ogb])
                self.barrier()

    def barrier(self):
        P = self.P
        toks = []
        for e in ("pe", "act", "dve", "pool"):
            if e in P.last:
                toks.append(P.last[e])
        for s in range(NDS):
            if P.dcnt[s] > 0:
                toks.append(("dma", P.dsems[s], P.dcnt[s]))
        for e in P.ENG:
            for t in toks:
                if t[0] != e:
                    P._wait(e, t)


def rope_tables():
    n = 4096
    t = np.arange(n)
    row = (t // 64).astype(np.float32)
    col = (t % 64).astype(np.float32)
    half = 32
    inv = (np.float32(10000.0) ** (-np.arange(0, half, 2, dtype=np.float32) / np.float32(half))).astype(np.float32)
    ar = (row[:, None] * inv).astype(np.float32)
    ac = (col[:, None] * inv).astype(np.float32)
    cos = np.ones((64, NTOK), np.float32)
    sin = np.zeros((64, NTOK), np.float32)
    for base, a in ((0, ar), (32, ac)):
        c, s = np.cos(a).T.astype(np.float32), np.sin(a).T.astype(np.float32)
        cos[base:base + 16, 256:] = c
        cos[base + 16:base + 32, 256:] = c
        sin[base:base + 16, 256:] = -s
        sin[base + 16:base + 32, 256:] = s
    return cos, sin


ROPE_SW = np.concatenate([np.arange(16, 32), np.arange(0, 16), np.arange(48, 64), np.arange(32, 48)])


def fm(v):
    return np.ascontiguousarray(v.reshape(-1, 128).T)


def mixer_mla_inputs(inp, l, g):
    w_in = inp["w_in"][l]
    kpe = w_in[:, 768:832]
    win = np.concatenate([w_in[:, 0:768], kpe, kpe[:, ROPE_SW]], axis=1)
    wuq = inp["mla_w_uq"][l]
    cols = []
    for hh in range(2):
        cols.append(wuq[:, (2 * g + hh) * 192:(2 * g + hh) * 192 + 128])
    for hh in range(2):
        pe = wuq[:, (2 * g + hh) * 192 + 128:(2 * g + hh) * 192 + 192]
        cols += [pe, pe[:, ROPE_SW]]
    wuq_g = np.concatenate(cols, axis=1)
    wukv = inp["mla_w_ukv"][l]
    cols = [wukv[:, (2 * g + hh) * 256:(2 * g + hh) * 256 + 128] for hh in range(2)]
    cols += [wukv[:, (2 * g + hh) * 256 + 128:(2 * g + hh) * 256 + 256] for hh in range(2)]
    wukv_g = np.concatenate(cols, axis=1)
    return dict(win=np.ascontiguousarray(win), wuq=np.ascontiguousarray(wuq_g), wukv=np.ascontiguousarray(wukv_g),
                qn=fm(inp["mla_q_norm"][l]), kvn=fm(inp["mla_kv_norm"][l]),
                onorm=np.ascontiguousarray(inp["mla_out_norm"][l][2 * g * 128:(2 * g + 2) * 128]))


def build_mixer(parts=("mla", "rec")):
    P = Prog()
    ident = P.dram("ident", [128, 128], F32, "ExternalInput")
    hT = P.dram("hT", [D, NTOK], BF16, "ExternalInput")
    mixT = P.dram("mixT", [512, NTOK], BF16, "ExternalOutput")
    mx = Mixer(P, P.es)
    mx.consts(ident[:])
    if "mla" in parts:
        win = P.dram("win", [D, 896], F32, "ExternalInput")
        wuq = P.dram("wuq", [512, 512], F32, "ExternalInput")
        wukv = P.dram("wukv", [256, 512], F32, "ExternalInput")
        qn = P.dram("qn", [128, 4], F32, "ExternalInput")
        kvn = P.dram("kvn", [128, 2], F32, "ExternalInput")
        cos = P.dram("cos", [64, NTOK], F32, "ExternalInput")
        sin = P.dram("sin", [64, NTOK], F32, "ExternalInput")
        onorm = P.dram("onorm", [256], F32, "ExternalInput")
        mx.mla(hT, win, wuq, wukv, qn[:], kvn[:], cos, sin, onorm[:], mixT)
    if "rec" in parts or "ml" in parts or "gd" in parts:
        wfm = P.dram("wfm", [D, 512], F32, "ExternalInput")
        wtm = P.dram("wtm", [D, 456], F32, "ExternalInput")
        gbias = P.dram("gbias", [8], F32, "ExternalInput")
        convw = P.dram("convw", [128, 15], F32, "ExternalInput")
        mnorm = P.dram("mnorm", [128], F32, "ExternalInput")
        gnorm = P.dram("gnorm", [128], F32, "ExternalInput")
        cmd = P.dram("cm", [128, 22, 128], F32, "ExternalInput")
        mx.rec(hT, wfm, wtm, gbias[:], convw[:], mnorm[:], gnorm[:], cmd[:], mixT,
               do_ml=("rec" in parts or "ml" in parts), do_gd=("rec" in parts or "gd" in parts))
    P.finish()
    return P


NEGV = -1.0e30


def rec_consts():
    p = np.arange(128)[:, None]
    f = np.arange(128)[None, :]
    cm = np.zeros((128, 22, 128), np.float32)
    for k in range(7):
        sz = 1 << k
        mk = ((p // (2 * sz)) == (f // (2 * sz))) & ((p % (2 * sz)) >= sz) & ((f % (2 * sz)) < sz)
        cm[:, 8 + k] = mk
        cm[:, 15 + k] = mk.T
    cm[:, 0] = (p <= f)
    cm[:, 1] = (p >= f)
    cm[:, 2] = np.where(f <= p, 0.0, NEGV)
    cm[:, 3] = np.where(f < p, 0.0, NEGV)
    cm[:, 4] = np.where(f >= p, 0.0, NEGV)
    cm[:, 5] = np.where(f > p, 0.0, NEGV)
    cm[:, 6] = (p == 127) * np.ones((1, 128))
    cm[:, 7] = (p == 0) * np.ones((1, 128))
    return cm


def mixer_rec_inputs(inp, l, g):
    w_in = inp["w_in"][l]
    mq = w_in[:, 832 + g * 64:832 + (g + 1) * 64]
    mk = w_in[:, 1088 + g * 64:1088 + (g + 1) * 64]
    mv = w_in[:, 1344 + g * 128:1344 + (g + 1) * 128]
    mo = w_in[:, 1856 + g * 128:1856 + (g + 1) * 128]
    mg = w_in[:, [2368 + j * 4 + g for j in range(4)]]
    gq = w_in[:, 2384 + g * 128:2384 + (g + 1) * 128]
    gk = w_in[:, 2896 + g * 128:2896 + (g + 1) * 128]
    gv = w_in[:, 3408 + g * 128:3408 + (g + 1) * 128]
    gz = w_in[:, 3920 + g * 128:3920 + (g + 1) * 128]
    gg = w_in[:, [4432 + j * 4 + g for j in range(4)]]
    wfm = np.concatenate([mq, mk, gq, gk, gv], axis=1)
    wtm = np.concatenate([mk, mv, mo, gz, mg, gg], axis=1)
    mb = inp["ml_gate_bias"][l]
    gbias = np.array([mb[0 * 4 + g], mb[1 * 4 + g], mb[2 * 4 + g], mb[3 * 4 + g],
                      inp["gd_dt_bias"][l][0, g], inp["gd_dt_bias"][l][1, g],
                      inp["gd_a_log"][l][0, g], inp["gd_a_log"][l][1, g]], np.float32)
    cw = inp["gd_conv"][l]
    convw = np.concatenate([cw[:, j * 512 + g * 128:j * 512 + (g + 1) * 128].T for j in range(3)], axis=1)
    return dict(wfm=np.ascontiguousarray(wfm), wtm=np.ascontiguousarray(wtm), gbias=gbias, convw=np.ascontiguousarray(convw),
                mnorm=np.ascontiguousarray(inp["ml_out_norm"][l][g * 128:(g + 1) * 128]),
                gnorm=np.ascontiguousarray(inp["gd_out_norm"][l]), cm=rec_consts())


def bc_last(ap2d, n):
    p, k = ap2d.shape
    return ap2d.unsqueeze(2).to_broadcast([p, k, n])


def bc_midrep(ap2d, m):
    p, n = ap2d.shape
    return ap2d.unsqueeze(1).to_broadcast([p, m, n])


def mixer_rec(self, hT_d, wfm_d, wtm_d, gbias_d, convw_d, mnorm_d, gnorm_d, cm_d, mixT_d, do_ml=True, do_gd=True):
    P = self.P
    idf, idb, onesf, onesb, cb = self.idf, self.idb, self.onesf, self.onesb, self.cb
    FWD = list(range(NTILE))
    BWD = [1, 0] + list(range(NTILE - 1, 1, -1))
    with ExitStack() as st:
        sb = lambda n, s, d: P.sb(n, s, d, st)
        MQT = sb("MQT", [64, NTOK], BF16)
        MKT = sb("MKT", [64, NTOK], BF16)
        MK = sb("MK", [128, NTILE, 64], BF16)
        MV = sb("MV", [128, NTILE, 129], BF16)
        MO = sb("MO", [128, NTILE, 128], BF16)
        GZ = sb("GZ", [128, NTILE, 128], BF16)
        GQT = sb("GQT", [128, NTOK], BF16)
        GKT = sb("GKT", [128, NTOK], BF16)
        GK = sb("GK", [128, NTILE, 128], BF16)
        GQ = sb("GQ", [128, NTILE, 128], BF16)
        GV = sb("GV", [128, NTILE, 128], BF16)
        GA = sb("GA", [128, 8, NTILE], F32)
        cm = sb("cm", [128, 8, 128], F32)
        bm = sb("bm", [128, 7, 128], F32)
        bmT = sb("bmT", [128, 7, 128], F32)
        gb = sb("gbias", [128, 8], F32)
        cw = sb("convw", [128, 15], F32)
        mgain = sb("mgain", [128, 128], F32)
        ggain = sb("ggain", [128, 128], F32)
        mlb, gdb, gab, cmb = Buf("ml"), Buf("gd"), Buf("ga"), Buf("cmb")
        P.dma("sp", cm[:], cm_d[:, 0:8, :], writes=[cmb])
        P.dma("sp", bm[:], cm_d[:, 8:15, :], writes=[cmb])
        P.dma("sp", bmT[:], cm_d[:, 15:22, :], writes=[cmb])
        P.dma("sp", gb[:], gbias_d.partition_broadcast(128), writes=[cmb])
        P.dma("sp", cw[:], convw_d, writes=[cmb])
        P.dma("sp", mgain[:], mnorm_d.partition_broadcast(128), writes=[cmb])
        P.dma("sp", ggain[:], gnorm_d.partition_broadcast(128), writes=[cmb])
        P.op("pool", lambda e: e.memset(MV[:, :, 128:129], 1.0), writes=[mlb])
        with ExitStack() as st12:
            sb12 = lambda n, s, d: P.sb(n, s, d, st12)
            GRAW = sb12("GRAW", [128, 3, NTOK], BF16)
            grb = Buf("graw")
            with ExitStack() as st1:
                sb1 = lambda n, s, d: P.sb(n, s, d, st1)
                wfm = sb1("wfm", [128, 16, 512], BF16)
                wtm = sb1("wtm", [128, 16, 456], BF16)
                wb = Buf("recw")
                P.dma("pool", wfm[:], wfm_d.rearrange("(k p) c -> p k c", p=128), writes=[wb])
                P.dma("pool", wtm[:], wtm_d.rearrange("(k p) c -> p k c", p=128), writes=[wb])
                hc = [sb1(f"rhc{i}", [128, 16, 512], BF16) for i in range(2)]
                hcb = [Buf(f"rhc{i}") for i in range(2)]
                for ci, (c0, n) in enumerate(TCH):
                    h, hb = hc[ci % 2], hcb[ci % 2]
                    self.hload(h, hb, c0, n, hT_d)
                    for gi, (cols, M) in enumerate(((0, 64), (64, 64), (128, 128), (256, 128), (384, 128))):
                        ps, psb = self.nb()
                        for k in range(16):
                            P.op("pe", lambda e: e.matmul(ps[0:M, 0:n], wfm[:, k, cols:cols + M], h[:, k, 0:n], start=(k == 0), stop=(k == 15)),
                                 reads=[wb, hb], writes=[psb])
                        if gi == 0:
                            P.op("act", lambda e: e.copy(MQT[:, c0:c0 + n], ps[0:64, 0:n]), reads=[psb], writes=[mlb])
                        elif gi == 1:
                            P.op("act", lambda e: e.mul(MKT[:, c0:c0 + n], ps[0:64, 0:n], 0.125), reads=[psb], writes=[mlb])
                        else:
                            P.op("dve", lambda e: e.tensor_copy(GRAW[:, gi - 2, c0:c0 + n], ps[:, 0:n]), reads=[psb], writes=[grb])
                    for j in range(n // 128):
                        t = (c0 + j * 128) // 128
                        ps, psb = self.nb()
                        for k in range(16):
                            P.op("pe", lambda e: e.matmul(ps[:, 0:456], h[:, k, j * 128:(j + 1) * 128], wtm[:, k, :], start=(k == 0), stop=(k == 15)),
                                 reads=[wb, hb], writes=[psb])
                        P.op("dve", lambda e: e.tensor_scalar(MK[:, t, :], ps[:, 0:64], 0.125, None, ALU.mult), reads=[psb], writes=[mlb])
                        P.op("dve", lambda e: e.tensor_copy(GA[:, :, t], ps[:, 448:456]), reads=[psb], writes=[gab])
                        P.op("act", lambda e: e.copy(MV[:, t, 0:128], ps[:, 64:192]), reads=[psb], writes=[mlb])
                        P.op("act", lambda e: e.activation(MO[:, t, :], ps[:, 192:320], AF.Sigmoid), reads=[psb], writes=[mlb])
                        P.op("act", lambda e: e.activation(GZ[:, t, :], ps[:, 320:448], AF.Silu), reads=[psb], writes=[gdb])
                self.barrier()
            with ExitStack() as st2:
                sb2 = lambda n, s, d: P.sb(n, s, d, st2)
                acc = sb2("acc", [128, NTOK], F32)
                sqb = sb2("sqb", [128, NTOK], BF16)
                GVT = sb2("GVT", [128, NTOK], BF16)
                rq = [sb2(f"rq{i}", [128, 512], F32) for i in range(2)]
                accb, sqbb, gvtb = Buf("acc"), Buf("sqb"), Buf("gvt")
                rqb = [Buf(f"rq{i}") for i in range(2)]
                segs = [(0, 256), (256, NTOK)]
                for j in range(3):
                    if not do_gd:
                        break
                    raw = GRAW[:, j, :]
                    P.op("dve", lambda e: e.tensor_scalar(acc[:], raw, cw[:, j * 5 + 2:j * 5 + 3], None, ALU.mult), reads=[grb, cmb], writes=[accb])
                    for dlt in (-2, -1, 1, 2):
                        for (s0, s1) in segs:
                            a_, b_ = max(s0, s0 - dlt), min(s1, s1 - dlt)
                            P.op("dve", lambda e: e.scalar_tensor_tensor(acc[:, a_:b_], GRAW[:, j, a_ + dlt:b_ + dlt], cw[:, j * 5 + 2 + dlt:j * 5 + 3 + dlt],
                                                                         acc[:, a_:b_], ALU.mult, ALU.add), reads=[grb, cmb, accb], writes=[accb])
                    if j == 2:
                        P.op("act", lambda e: e.activation(GVT[:], acc[:], AF.Silu), reads=[accb], writes=[gvtb])
                        continue
                    P.op("act", lambda e: e.activation(acc[:], acc[:], AF.Silu), reads=[accb], writes=[accb])
                    P.op("act", lambda e: e.activation(sqb[:], acc[:], AF.Square), reads=[accb], writes=[sqbb])
                    dst = GQT if j == 0 else GKT
                    scl = (128 ** -0.5) if j == 0 else 1.0
                    for ci, (c0, n) in enumerate(TCH):
                        ps, psb = self.nb()
                        P.op("pe", lambda e: e.matmul(ps[:, 0:n], onesb[:], sqb[:, c0:c0 + n], start=True, stop=True), reads=[cb, sqbb], writes=[psb])
                        r_, r_b = rq[ci % 2], rqb[ci % 2]
                        P.op("act", lambda e: e.activation(r_[:, 0:n], ps[:, 0:n], AF.Sqrt, bias=EPS, scale=1.0), reads=[psb], writes=[r_b])
                        P.op("dve", lambda e: e.reciprocal(r_[:, 0:n], r_[:, 0:n]), reads=[r_b], writes=[r_b])
                        P.op("dve", lambda e: e.scalar_tensor_tensor(dst[:, c0:c0 + n], acc[:, c0:c0 + n], scl, r_[:, 0:n], ALU.mult, ALU.mult),
                             reads=[accb, r_b], writes=[gdb])
                if do_gd:
                    for t in range(NTILE):
                        for si, (src, srcb, dstt) in enumerate(((GKT, gdb, GK), (GQT, gdb, GQ), (GVT, gvtb, GV))):
                            P.op("pe", lambda e: e.transpose(self.ptb[:], src[:, t * 128:(t + 1) * 128], idb[:]), reads=[srcb, cb], writes=[self.ptbb])
                            if si == 1:
                                P.op("act", lambda e: e.copy(dstt[:, t, :], self.ptb[:]), reads=[self.ptbb], writes=[gdb])
                            else:
                                P.op("dve", lambda e: e.tensor_copy(dstt[:, t, :], self.ptb[:]), reads=[self.ptbb], writes=[gdb])
                self.barrier()
        with ExitStack() as st3:
            sb3 = lambda n, s, d: P.sb(n, s, d, st3)
            SC = sb3("SC", [128, 80, NTILE], F32)
            scb = Buf("sc")
            OUT = sb3("RO", [128, NTILE, 128], F32)
            outb = [Buf(f"ro{t}") for t in range(NTILE)]
            OUT2 = sb3("RO2", [128, NTILE, 128], F32)
            outb2 = [Buf(f"ro2_{t}") for t in range(NTILE)]
            streams = []
            LD4 = sb3("LD4", [128, 4, 128], F32)
            ld4b = Buf("ld4")
            Kw = [sb3(f"Kw{d}", [128, NTILE, 64], BF16) for d in range(2)]
            kwb = [Buf(f"kw{d}") for d in range(2)]
            NT = 24
            T = [[sb3(f"T{d}_{i}", [128, 128], F32) for i in range(NT)] for d in range(2)]
            Tb = [[Buf(f"T{d}_{i}") for i in range(NT)] for d in range(2)]
            TB16 = [[sb3(f"Tb{d}_{i}", [128, 128], BF16) for i in range(16)] for d in range(2)]
            TB16b = [[Buf(f"Tb{d}_{i}") for i in range(16)] for d in range(2)]
            DG4 = sb3("DG4", [128, 4, 128], F32)
            dg4b = Buf("dg4")
            W129 = [[sb3(f"W129_{d}_{i}", [128, 129], F32) for i in range(2)] for d in range(2)]
            W129b = [[Buf(f"W129_{d}_{i}") for i in range(2)] for d in range(2)]
            Cn = [sb3(f"Cn{d}", [64, 129], F32) for d in range(2)]
            Cb = [sb3(f"Cb{d}", [64, 129], BF16) for d in range(2)]
            cnb = [Buf(f"cn{d}") for d in range(2)]
            Sst = [[sb3(f"Sst{d}_{i}", [128, 128], F32) for i in range(2)] for d in range(2)]
            sstb = [[Buf(f"Sst{d}_{i}") for i in range(2)] for d in range(2)]
            stg = [sb3(f"rstg{i}", [128, 512], BF16) for i in range(2)]
            stgb = [Buf(f"rstg{i}") for i in range(2)]
            sc4 = [sb3(f"sc4_{d}", [128, 8], F32) for d in range(2)]
            sc4b = [Buf(f"sc4_{d}") for d in range(2)]
            ti = [0, 0]
            tbi = [0, 0]
            ORD = [FWD, BWD]

            def tmpf(d):
                i = ti[d] % NT
                ti[d] += 1
                return T[d][i], Tb[d][i]

            def tmpb(d):
                i = tbi[d] % 16
                tbi[d] += 1
                return TB16[d][i], TB16b[d][i]

            A = lambda s: SC[:, s, :]
            C = lambda s, t: SC[:, s, t:t + 1]

            def softplus_core(xs, ts):
                P.op("act", lambda e: e.activation(A(ts), A(xs), AF.Abs), reads=[scb], writes=[scb])
                P.op("act", lambda e: e.activation(A(ts), A(ts), AF.Exp, scale=-1.0), reads=[scb], writes=[scb])
                P.op("act", lambda e: e.activation(A(ts), A(ts), AF.Ln, bias=1.0), reads=[scb], writes=[scb])

            def cumsum(src, dst_cum, dst_sum, bwd):
                ps, psb = self.nb()
                P.op("pe", lambda e: e.matmul(ps[:, 0:NTILE], cm[:, 1 if bwd else 0, :], A(src), start=True, stop=True), reads=[cmb, scb], writes=[psb])
                P.op("dve", lambda e: e.tensor_copy(A(dst_cum), ps[:, 0:NTILE]), reads=[psb], writes=[scb])
                ps, psb = self.nb()
                P.op("pe", lambda e: e.matmul(ps[:, 0:NTILE], onesf[:], A(src), start=True, stop=True), reads=[cb, scb], writes=[psb])
                P.op("dve", lambda e: e.tensor_copy(A(dst_sum), ps[:, 0:NTILE]), reads=[psb], writes=[scb])

            def zero_out():
                P.op("pool", lambda e: e.memset(OUT[:], 0.0), writes=outb)
                P.op("pool", lambda e: e.memset(OUT2[:], 0.0), writes=outb2)

            def finish_out(OUT, outb, gain, gate, row0):
                oi = 0
                for t0 in range(0, NTILE, 4):
                    m = min(4, NTILE - t0)
                    sg, sgb = stg[oi % 2], stgb[oi % 2]
                    oi += 1
                    for q in range(m):
                        t = t0 + q
                        d_ = q % 2
                        y, yb = tmpf(d_)
                        y2, y2b = tmpb(d_)
                        s4, s4b = sc4[d_], sc4b[d_]
                        P.op("act", lambda e: e.activation(y[:], OUT[:, t, :], AF.Square, accum_out=s4[:, 0:1]), reads=[outb[t]], writes=[yb, s4b])
                        P.op("act", lambda e: e.activation(s4[:, 1:2], s4[:, 0:1], AF.Sqrt, bias=EPS, scale=1.0 / 128), reads=[s4b], writes=[s4b])
                        P.op("dve", lambda e: e.reciprocal(s4[:, 2:3], s4[:, 1:2]), reads=[s4b], writes=[s4b])
                        P.op("dve", lambda e: e.scalar_tensor_tensor(y[:], OUT[:, t, :], s4[:, 2:3], gain[:], ALU.mult, ALU.mult),
                             reads=[outb[t], s4b, cmb], writes=[yb])
                        P.op("pool", lambda e: e.tensor_tensor(y2[:], y[:], gate[:, t, :], ALU.mult), reads=[yb, mlb, gdb], writes=[y2b])
                        P.op("pe", lambda e: e.transpose(self.ptb[:], y2[:], idb[:]), reads=[y2b, cb], writes=[self.ptbb])
                        P.op("dve", lambda e: e.tensor_copy(sg[:, q * 128:(q + 1) * 128], self.ptb[:]), reads=[self.ptbb], writes=[sgb])
                    self.mix_store(mixT_d, row0, t0 * 128, m * 128, sg, [sgb])

            if do_ml:
                zero_out()
                for dr in range(2):
                    bwd = dr == 1
                    order = ORD[dr]
                    o_ = dr * 20
                    s_ig, s_lf, s_b, s_bs, s_a, s_rm, s_am, s_mp, s_mm, s_ml, s_nm, s_tmp = [o_ + i for i in range(12)]
                    s_x = o_ + 16
                    P.op("dve", lambda e: e.tensor_scalar(A(s_ig), GA[:, 2 * dr, :], gb[:, 2 * dr:2 * dr + 1], None, ALU.add), reads=[gab, cmb], writes=[scb])
                    P.op("dve", lambda e: e.tensor_scalar(A(s_lf), GA[:, 2 * dr + 1, :], gb[:, 2 * dr + 1:2 * dr + 2], None, ALU.add), reads=[gab, cmb], writes=[scb])
                    softplus_core(s_lf, s_tmp)
                    P.op("dve", lambda e: e.scalar_tensor_tensor(A(s_lf), A(s_lf), 0.0, A(s_tmp), ALU.min, ALU.subtract), reads=[scb], writes=[scb])
                    cumsum(s_lf, s_b, s_bs, bwd)
                    P.op("dve", lambda e: e.tensor_tensor(A(s_a), A(s_ig), A(s_b), ALU.subtract), reads=[scb], writes=[scb])
                    neg = cm[:, 4 if bwd else 2, :]
                    for t0 in range(0, NTILE, 4):
                        m = min(4, NTILE - t0)
                        P.op("pool", lambda e: e.tensor_tensor(DG4[:, 0:m, :], bc_midrep(idf[:], m), bc_last(SC[:, s_a, t0:t0 + m], 128), ALU.mult),
                             reads=[cb, scb], writes=[dg4b])
                        ps, psb = self.nb()
                        P.op("pe", lambda e: e.matmul(ps[:, 0:m * 128], onesf[:], DG4[:, 0:m, :].rearrange("p m s -> p (m s)"), start=True, stop=True),
                             reads=[cb, dg4b], writes=[psb])
                        P.op("dve", lambda e: e.tensor_tensor(LD4[:, 0:m, :], ps[:, 0:m * 128].rearrange("p (m s) -> p m s", m=m), bc_midrep(neg, m), ALU.add),
                             reads=[psb, cmb], writes=[ld4b])
                        P.op("dve", lambda e: e.tensor_reduce(SC[:, s_rm, t0:t0 + m], LD4[:, 0:m, :], AX.X, ALU.max), reads=[ld4b], writes=[scb])
                    ps, psb = self.nb()
                    P.op("pe", lambda e: e.matmul(ps[:, 0:NTILE], cm[:, 7 if bwd else 6, :], A(s_rm), start=True, stop=True), reads=[cmb, scb], writes=[psb])
                    P.op("dve", lambda e: e.tensor_copy(A(s_am), ps[:, 0:NTILE]), reads=[psb], writes=[scb])
                    P.op("dve", lambda e: e.memset(C(s_mp, order[0]), NEGV), writes=[scb])
                    for i in range(NTILE - 1):
                        t, tn = order[i], order[i + 1]
                        P.op("dve", lambda e: e.scalar_tensor_tensor(C(s_mp, tn), C(s_mp, t), C(s_am, t), C(s_bs, t), ALU.max, ALU.add), reads=[scb], writes=[scb])
                    P.op("dve", lambda e: e.tensor_tensor(A(s_mm), A(s_mp), A(s_rm), ALU.max), reads=[scb], writes=[scb])
                    P.op("dve", lambda e: e.tensor_tensor(A(s_ml), A(s_mp), A(s_am), ALU.max), reads=[scb], writes=[scb])
                    P.op("dve", lambda e: e.tensor_tensor(A(s_x + 0), A(s_mp), A(s_mm), ALU.subtract), reads=[scb], writes=[scb])
                    P.op("dve", lambda e: e.scalar_tensor_tensor(A(s_x + 1), A(s_b), -1.0, A(s_mm), ALU.mult, ALU.subtract), reads=[scb], writes=[scb])
                    P.op("dve", lambda e: e.tensor_tensor(A(s_x + 2), A(s_a), A(s_ml), ALU.subtract), reads=[scb], writes=[scb])
                    P.op("dve", lambda e: e.tensor_tensor(A(s_x + 3), A(s_mp), A(s_ml), ALU.subtract), reads=[scb], writes=[scb])
                    P.op("dve", lambda e: e.tensor_scalar(SC[:, s_x:s_x + 4, :], SC[:, s_x:s_x + 4, :], -150.0, None, ALU.max), reads=[scb], writes=[scb])
                    P.op("act", lambda e: e.activation(SC[:, s_x:s_x + 4, :], SC[:, s_x:s_x + 4, :], AF.Exp), reads=[scb], writes=[scb])
                    P.op("dve", lambda e: e.tensor_scalar(A(s_nm), A(s_mm), -1.0, None, ALU.mult), reads=[scb], writes=[scb])
                    P.op("dve", lambda e: e.tensor_tensor(Kw[dr][:], MK[:], bc_last(A(s_x + 2), 64), ALU.mult), reads=[mlb, scb], writes=[kwb[dr]])
                    P.op("dve", lambda e: e.memset(Cn[dr][:], 0.0), writes=[cnb[dr]])
                    P.op("dve", lambda e: e.memset(Cb[dr][:], 0.0), writes=[cnb[dr]])
                def ml_chunk(dr, t):
                    bwd = dr == 1
                    o_ = dr * 20
                    s_a, s_nm, s_x = o_ + 4, o_ + 10, o_ + 16
                    neg = cm[:, 4 if bwd else 2, :]
                    s4, s4b = sc4[dr], sc4b[dr]
                    tok = slice(t * 128, (t + 1) * 128)
                    DG, DGb = tmpf(dr)
                    yield P.op("pool", lambda e: e.tensor_scalar(DG[:], idf[:], C(s_a, t), None, ALU.mult), reads=[cb, scb], writes=[DGb])
                    psL, psLb = self.nb()
                    yield P.op("pe", lambda e: e.matmul(psL[:, 0:128], onesf[:], DG[:], start=True, stop=True), reads=[cb, DGb], writes=[psLb])
                    Dm, Dmb = tmpf(dr)
                    yield P.op("dve", lambda e: e.tensor_tensor(Dm[:], psL[:, 0:128], neg, ALU.add), reads=[psLb, cmb], writes=[Dmb])
                    yield P.op("act", lambda e: e.activation(Dm[:], Dm[:], AF.Exp, bias=C(s_nm, t)), reads=[Dmb, scb], writes=[Dmb])
                    ps, psb = self.nb()
                    yield P.op("pe", lambda e: e.matmul(ps[:, 0:128], MQT[:, tok], MKT[:, tok], start=True, stop=True), reads=[mlb], writes=[psb])
                    SD, SDb = tmpf(dr)
                    yield P.op("dve", lambda e: e.tensor_tensor(SD[:], ps[:, 0:128], Dm[:], ALU.mult), reads=[psb, Dmb], writes=[SDb])
                    psT, psTb = self.nb()
                    yield P.op("pe", lambda e: e.transpose(psT[:, 0:128], SD[:], idf[:]), reads=[SDb, cb], writes=[psTb])
                    SDT, SDTb = tmpb(dr)
                    yield P.op("act", lambda e: e.copy(SDT[:], psT[:, 0:128]), reads=[psTb], writes=[SDTb])
                    psN, psNb = self.nb()
                    yield P.op("pe", lambda e: e.matmul(psN[:, 0:129], SDT[:], MV[:, t, :], start=True, stop=True), reads=[SDTb, mlb], writes=[psNb])
                    psI, psIb = self.nb()
                    yield P.op("pe", lambda e: e.matmul(psI[:, 0:129], MQT[:, tok], Cb[dr][:], start=True, stop=True), reads=[mlb, cnb[dr]], writes=[psIb])
                    w1, w1b = W129[dr][0], W129b[dr][0]
                    w2, w2b = W129[dr][1], W129b[dr][1]
                    yield P.op("dve", lambda e: e.tensor_scalar(w1[:], psI[:, 0:129], C(s_x + 0, t), None, ALU.mult), reads=[psIb, scb], writes=[w1b])
                    yield P.op("dve", lambda e: e.tensor_tensor(w2[:], psN[:, 0:129], w1[:], ALU.add), reads=[psNb, w1b], writes=[w2b])
                    yield P.op("act", lambda e: e.activation(s4[:, 4:5], w2[:, 128:129], AF.Abs), reads=[w2b], writes=[s4b])
                    yield P.op("dve", lambda e: e.tensor_tensor(s4[:, 4:5], s4[:, 4:5], C(s_x + 1, t), ALU.max), reads=[s4b, scb], writes=[s4b])
                    yield P.op("dve", lambda e: e.reciprocal(s4[:, 5:6], s4[:, 4:5]), reads=[s4b], writes=[s4b])
                    yield P.op("dve", lambda e: e.scalar_tensor_tensor(OUT[:, t, :], w2[:, 0:128], s4[:, 5:6], OUT[:, t, :], ALU.mult, ALU.add),
                         reads=[w2b, s4b, outb[t]], writes=[outb[t]])
                    psC, psCb = self.nb()
                    yield P.op("pe", lambda e: e.matmul(psC[0:64, 0:129], Kw[dr][:, t, :], MV[:, t, :], start=True, stop=True), reads=[kwb[dr], mlb], writes=[psCb])
                    yield P.op("dve", lambda e: e.scalar_tensor_tensor(Cn[dr][:], Cn[dr][:], SC[0:64, s_x + 3, t:t + 1], psC[0:64, 0:129], ALU.mult, ALU.add),
                         reads=[cnb[dr], scb, psCb], writes=[cnb[dr]])
                    yield P.op("act", lambda e: e.copy(Cb[dr][:], Cn[dr][:]), reads=[cnb[dr]], writes=[cnb[dr]])

                streams.append(ml_chunk)
            if do_gd:
                if not do_ml:
                    zero_out()
                for dr in range(2):
                    bwd = dr == 1
                    o_ = 40 + dr * 20
                    s_be, s_x, s_t, s_gl, s_gc, s_ge, s_nb, s_ng, s_e = [o_ + i for i in range(9)]
                    s_bg = o_ + 11
                    s4, s4b = sc4[dr], sc4b[dr]
                    P.op("act", lambda e: e.activation(A(s_be), GA[:, 4 + 2 * dr, :], AF.Sigmoid), reads=[gab], writes=[scb])
                    P.op("dve", lambda e: e.tensor_scalar(A(s_x), GA[:, 5 + 2 * dr, :], gb[:, 4 + dr:5 + dr], None, ALU.add), reads=[gab, cmb], writes=[scb])
                    softplus_core(s_x, s_t)
                    P.op("dve", lambda e: e.scalar_tensor_tensor(A(s_gl), A(s_x), 0.0, A(s_t), ALU.max, ALU.add), reads=[scb], writes=[scb])
                    P.op("act", lambda e: e.activation(s4[:, 6:7], gb[:, 6 + dr:7 + dr], AF.Exp), reads=[cmb], writes=[s4b])
                    P.op("dve", lambda e: e.tensor_scalar(A(s_gl), A(s_gl), s4[:, 6:7], -1.0, ALU.mult, ALU.mult), reads=[scb, s4b], writes=[scb])
                    cumsum(s_gl, s_gc, s_ge, bwd)
                    P.op("dve", lambda e: e.tensor_scalar(A(s_nb), A(s_be), -1.0, None, ALU.mult), reads=[scb], writes=[scb])
                    P.op("dve", lambda e: e.tensor_scalar(A(s_ng), A(s_gc), -1.0, None, ALU.mult), reads=[scb], writes=[scb])
                    P.op("dve", lambda e: e.tensor_copy(A(s_e + 0), A(s_gc)), reads=[scb], writes=[scb])
                    P.op("dve", lambda e: e.tensor_copy(A(s_e + 1), A(s_ge)), reads=[scb], writes=[scb])
                    P.op("dve", lambda e: e.tensor_tensor(A(s_e + 2), A(s_ge), A(s_gc), ALU.subtract), reads=[scb], writes=[scb])
                    P.op("act", lambda e: e.activation(SC[:, s_e:s_e + 3, :], SC[:, s_e:s_e + 3, :], AF.Exp), reads=[scb], writes=[scb])
                    P.op("dve", lambda e: e.tensor_tensor(A(s_bg), A(s_be), A(s_e + 0), ALU.mult), reads=[scb], writes=[scb])
                    P.op("dve", lambda e: e.memset(Sst[dr][0][:], 0.0), writes=[sstb[dr][0]])
                scur = [0, 0]
                def gd_chunk(dr, t):
                    bwd = dr == 1
                    o_ = 40 + dr * 20
                    s_be, s_x, s_t, s_gl, s_gc, s_ge, s_nb, s_ng, s_e = [o_ + i for i in range(9)]
                    s_bg = o_ + 11
                    negst = cm[:, 5 if bwd else 3, :]
                    neginT = cm[:, 2 if bwd else 4, :]
                    tok = slice(t * 128, (t + 1) * 128)
                    DG, DGb = tmpf(dr)
                    yield P.op("pool", lambda e: e.tensor_scalar(DG[:], idf[:], C(s_gc, t), None, ALU.mult), reads=[cb, scb], writes=[DGb])
                    ps1, ps1b = self.nb()
                    yield P.op("pe", lambda e: e.matmul(ps1[:, 0:128], onesf[:], DG[:], start=True, stop=True), reads=[cb, DGb], writes=[ps1b])
                    a1, a1b = tmpf(dr)
                    a2, a2b = tmpf(dr)
                    yield P.op("dve", lambda e: e.scalar_tensor_tensor(a1[:], ps1[:, 0:128], -1.0, negst, ALU.mult, ALU.add), reads=[ps1b, cmb], writes=[a1b])
                    yield P.op("dve", lambda e: e.tensor_tensor(a2[:], ps1[:, 0:128], neginT, ALU.add), reads=[ps1b, cmb], writes=[a2b])
                    yield P.op("act", lambda e: e.activation(a1[:], a1[:], AF.Exp, bias=C(s_gc, t)), reads=[a1b, scb], writes=[a1b])
                    dT, dTb = tmpb(dr)
                    yield P.op("act", lambda e: e.activation(dT[:], a2[:], AF.Exp, bias=C(s_ng, t)), reads=[a2b, scb], writes=[dTb])
                    ps2, ps2b = self.nb()
                    yield P.op("pe", lambda e: e.matmul(ps2[:, 0:128], GKT[:, tok], GKT[:, tok], start=True, stop=True), reads=[gdb], writes=[ps2b])
                    Pm, Pmb = tmpf(dr)
                    yield P.op("dve", lambda e: e.scalar_tensor_tensor(Pm[:], ps2[:, 0:128], C(s_nb, t), a1[:], ALU.mult, ALU.mult), reads=[ps2b, scb, a1b], writes=[Pmb])
                    ps3, ps3b = self.nb()
                    yield P.op("pe", lambda e: e.transpose(ps3[:, 0:128], Pm[:], idf[:]), reads=[Pmb, cb], writes=[ps3b])
                    PT, PTb = tmpf(dr)
                    yield P.op("act", lambda e: e.copy(PT[:], ps3[:, 0:128]), reads=[ps3b], writes=[PTb])
                    mkN = bmT if bwd else bm
                    mkT = bm if bwd else bmT
                    X, Xb = tmpf(dr)
                    Y, Yb = tmpf(dr)
                    yield P.op("pool", lambda e: e.tensor_tensor(X[:], Pm[:], mkN[:, 0, :], ALU.mult), reads=[Pmb, cmb], writes=[Xb])
                    yield P.op("pool", lambda e: e.tensor_tensor(X[:], X[:], idf[:], ALU.add), reads=[Xb, cb], writes=[Xb])
                    yield P.op("pool", lambda e: e.tensor_tensor(Y[:], PT[:], mkT[:, 0, :], ALU.mult), reads=[PTb, cmb], writes=[Yb])
                    yield P.op("pool", lambda e: e.tensor_tensor(Y[:], Y[:], idf[:], ALU.add), reads=[Yb, cb], writes=[Yb])
                    for lev in range(1, 7):
                        GT, GTb = tmpf(dr)
                        yield P.op("pool", lambda e: e.tensor_tensor(GT[:], PT[:], mkT[:, lev, :], ALU.mult), reads=[PTb, cmb], writes=[GTb])
                        psA, psAb = self.nb()
                        yield P.op("pe", lambda e: e.matmul(psA[:, 0:128], GT[:], X[:], start=True, stop=True), reads=[GTb, Xb], writes=[psAb])
                        W1, W1b = tmpf(dr)
                        yield P.op("act", lambda e: e.copy(W1[:], psA[:, 0:128]), reads=[psAb], writes=[W1b])
                        if lev < 6:
                            psB, psBb = self.nb()
                            yield P.op("pe", lambda e: e.matmul(psB[:, 0:128], Y[:], W1[:], start=True, stop=True), reads=[Yb, W1b], writes=[psBb])
                            X2, X2b = tmpf(dr)
                            yield P.op("dve", lambda e: e.tensor_tensor(X2[:], psB[:, 0:128], X[:], ALU.add), reads=[psBb, Xb], writes=[X2b])
                        psC, psCb = self.nb()
                        yield P.op("pe", lambda e: e.matmul(psC[:, 0:128], W1[:], Y[:], start=True, stop=True), reads=[W1b, Yb], writes=[psCb])
                        Y2, Y2b = tmpf(dr)
                        yield P.op("dve", lambda e: e.tensor_tensor(Y2[:], psC[:, 0:128], Y[:], ALU.add), reads=[psCb, Yb], writes=[Y2b])
                        Y, Yb = Y2, Y2b
                        if lev < 6:
                            X, Xb = X2, X2b
                    Y16, Y16b = tmpb(dr)
                    yield P.op("act", lambda e: e.copy(Y16[:], Y[:]), reads=[Yb], writes=[Y16b])
                    Vb, Vbb = tmpb(dr)
                    Kbg, Kbgb = tmpb(dr)
                    Kd, Kdb = tmpb(dr)
                    Qg, Qgb = tmpb(dr)
                    yield P.op("pool", lambda e: e.tensor_scalar(Vb[:], GV[:, t, :], C(s_be, t), None, ALU.mult), reads=[gdb, scb], writes=[Vbb])
                    yield P.op("pool", lambda e: e.tensor_scalar(Kbg[:], GK[:, t, :], C(s_bg, t), None, ALU.mult), reads=[gdb, scb], writes=[Kbgb])
                    yield P.op("pool", lambda e: e.tensor_scalar(Kd[:], GK[:, t, :], C(s_e + 2, t), None, ALU.mult), reads=[gdb, scb], writes=[Kdb])
                    yield P.op("pool", lambda e: e.tensor_scalar(Qg[:], GQ[:, t, :], C(s_e + 0, t), None, ALU.mult), reads=[gdb, scb], writes=[Qgb])
                    psU, psUb = self.nb()
                    yield P.op("pe", lambda e: e.matmul(psU[:, 0:128], Y16[:], Vb[:], start=True, stop=True), reads=[Y16b, Vbb], writes=[psUb])
                    psW, psWb = self.nb()
                    yield P.op("pe", lambda e: e.matmul(psW[:, 0:128], Y16[:], Kbg[:], start=True, stop=True), reads=[Y16b, Kbgb], writes=[psWb])
                    Us, Usb = tmpb(dr)
                    Wn, Wnb = tmpb(dr)
                    yield P.op("act", lambda e: e.copy(Us[:], psU[:, 0:128]), reads=[psUb], writes=[Usb])
                    yield P.op("dve", lambda e: e.tensor_scalar(Wn[:], psW[:, 0:128], -1.0, None, ALU.mult), reads=[psWb], writes=[Wnb])
                    psQ, psQb = self.nb()
                    yield P.op("pe", lambda e: e.matmul(psQ[:, 0:128], GKT[:, tok], GQT[:, tok], start=True, stop=True), reads=[gdb], writes=[psQb])
                    AT, ATb = tmpb(dr)
                    yield P.op("dve", lambda e: e.tensor_tensor(AT[:], psQ[:, 0:128], dT[:], ALU.mult), reads=[psQb, dTb], writes=[ATb])
                    psP, psPb = self.nb()
                    yield P.op("pe", lambda e: e.matmul(psP[:, 0:128], Qg[:], idb[:], start=True, stop=False), reads=[Qgb, cb], writes=[psPb])
                    yield P.op("pe", lambda e: e.matmul(psP[:, 0:128], Wn[:], AT[:], start=False, stop=True), reads=[Wnb, ATb], writes=[psPb])
                    QpT, QpTb = tmpf(dr)
                    yield P.op("act", lambda e: e.copy(QpT[:], psP[:, 0:128]), reads=[psPb], writes=[QpTb])
                    psA_, psA_b = self.nb()
                    yield P.op("pe", lambda e: e.matmul(psA_[:, 0:128], Wn[:], Kd[:], start=True, stop=True), reads=[Wnb, Kdb], writes=[psA_b])
                    AcT, AcTb = tmpf(dr)
                    yield P.op("dve", lambda e: e.scalar_tensor_tensor(AcT[:], idf[:], C(s_e + 1, t), psA_[:, 0:128], ALU.mult, ALU.add), reads=[cb, scb, psA_b], writes=[AcTb])
                    psBc, psBcb = self.nb()
                    yield P.op("pe", lambda e: e.matmul(psBc[:, 0:128], Kd[:], Us[:], start=True, stop=True), reads=[Kdb, Usb], writes=[psBcb])
                    Bc, Bcb = tmpf(dr)
                    yield P.op("act", lambda e: e.copy(Bc[:], psBc[:, 0:128]), reads=[psBcb], writes=[Bcb])
                    S, Sb = Sst[dr][scur[dr]], sstb[dr][scur[dr]]
                    S2, S2b = Sst[dr][1 - scur[dr]], sstb[dr][1 - scur[dr]]
                    psO, psOb = self.nb()
                    yield P.op("pe", lambda e: e.matmul(psO[:, 0:128], AT[:], Us[:], start=True, stop=False), reads=[ATb, Usb], writes=[psOb])
                    yield P.op("pe", lambda e: e.matmul(psO[:, 0:128], QpT[:], S[:], start=False, stop=True), reads=[QpTb, Sb], writes=[psOb])
                    yield P.op("dve", lambda e: e.tensor_tensor(OUT2[:, t, :], psO[:, 0:128], OUT2[:, t, :], ALU.add), reads=[psOb, outb2[t]], writes=[outb2[t]])
                    psS, psSb = self.nb()
                    yield P.op("pe", lambda e: e.matmul(psS[:, 0:128], AcT[:], S[:], start=True, stop=True), reads=[AcTb, Sb], writes=[psSb])
                    yield P.op("dve", lambda e: e.tensor_tensor(S2[:], psS[:, 0:128], Bc[:], ALU.add), reads=[psSb, Bcb], writes=[S2b])
                    scur[dr] = 1 - scur[dr]

                streams.append(gd_chunk)
            for step in range(NTILE):
                gens = [fn(dr, ORD[dr][step]) for fn in streams for dr in range(2)]
                alive = [True] * len(gens)
                while any(alive):
                    for gi_ in range(len(gens)):
                        if alive[gi_]:
                            try:
                                next(gens[gi_])
                            except StopIteration:
                                alive[gi_] = False
            if do_ml:
                finish_out(OUT, outb, mgain, MO, 256)
            if do_gd:
                finish_out(OUT2, outb2, ggain, GZ, 384)
            self.barrier()


Mixer.rec = mixer_rec


_PROGS = {}
_DBG = None


def _prog(name, builder):
    if name not in _PROGS:
        _PROGS[name] = builder()
    return _PROGS[name]


def _run(P, ins):
    return run_bass_kernel_spmd(P.nc, ins, core_ids=list(range(8))).results


def kernel_unfused(**inp):
    import ml_dtypes
    inp = {k: np.asarray(v) for k, v in inp.items()}
    x, c, ctx, c_ctx = inp["x"], inp["c"], inp["ctx"], inp["c_ctx"]
    ident = np.eye(128, dtype=np.float32)
    cos, sin = rope_tables()
    cT = np.ascontiguousarray(np.stack([c[0], c[1], c_ctx]).T)
    res = _run(_prog("ada", build_ada),
               [dict(cT=cT, wada=np.ascontiguousarray(inp["w_ada"][:, :, i * 1536:(i + 1) * 1536]),
                     bada=np.ascontiguousarray(inp["b_ada"][:, i * 1536:(i + 1) * 1536])) for i in range(8)])
    mod = np.concatenate([r["modp"] for r in res], axis=2).reshape(4, 3, 6, D)

    def modv(l, b):
        return np.ascontiguousarray(np.stack([mod[l, b], mod[l, 2]]))

    xs = []
    for core in range(8):
        b, g = divmod(core, 4)
        xs.append(np.ascontiguousarray(np.concatenate([ctx[b, 64 * g:64 * g + 64], x[b, 1024 * g:1024 * (g + 1)]], 0)))
    res = _run(_prog("A", lambda: build_dense("A")),
               [dict(ident=ident, x_in=xs[core], modn=modv(0, core // 4), norm1=inp["norm1"][0]) for core in range(8)])
    hTs = [r["hT"] for r in res]
    out = None
    for l in range(4):
        hfull = []
        for b in range(2):
            parts = [hTs[b * 4 + g][:, 0:64] for g in range(4)] + [hTs[b * 4 + g][:, 64:] for g in range(4)]
            hfull.append(np.ascontiguousarray(np.concatenate(parts, axis=1)))
        ins = []
        for core in range(8):
            b, g = divmod(core, 4)
            dd = dict(ident=ident, hT=hfull[b], cos=cos, sin=sin)
            dd.update(mixer_mla_inputs(inp, l, g))
            dd.update(mixer_rec_inputs(inp, l, g))
            ins.append(dd)
        res = _run(_prog("mix", build_mixer), ins)
        mixs = [r["mixT"] for r in res]
        if _DBG is not None:
            _DBG[f"hfull{l}"] = [np.asarray(h).astype(np.float32) for h in hfull]
            _DBG[f"mixs{l}"] = [np.asarray(m).astype(np.float32) for m in mixs]
        ins = []
        for core in range(8):
            b, g = divmod(core, 4)
            rows = [mixs[b * 4 + gg][0:256] for gg in range(4)] + [mixs[b * 4 + gg][256:384] for gg in range(4)] + \
                   [mixs[b * 4 + gg][384:512] for gg in range(4)]
            full = np.concatenate(rows, axis=0)
            mt = np.ascontiguousarray(np.concatenate([full[:, 64 * g:64 * g + 64], full[:, 256 + 1024 * g:256 + 1024 * (g + 1)]], axis=1))
            dd = dict(ident=ident, x_in=xs[core], mixT=mt, modv=modv(l, b), norm2=inp["norm2"][l],
                      w_out=inp["w_out"][l], w1=inp["w_mlp1"][l], w2=inp["w_mlp2"][l])
            if l < 3:
                dd.update(modn=modv(l + 1, b), norm1=inp["norm1"][l + 1])
            else:
                dd.update(final_norm=inp["final_norm"])
            ins.append(dd)
        if l < 3:
            res = _run(_prog("C", lambda: build_dense("C")), ins)
            xs = [r["x_out"] for r in res]
            hTs = [r["hT"] for r in res]
            if _DBG is not None:
                _DBG[f"xs{l}"] = [np.asarray(v) for v in xs]
        else:
            res = _run(_prog("CF", lambda: build_dense("CF")), ins)
            out = np.stack([np.concatenate([res[b * 4 + g]["y"] for g in range(4)], axis=0) for b in range(2)])
    return np.ascontiguousarray(out.astype(np.float32))


GROUPS4 = [[0, 1, 2, 3], [4, 5, 6, 7]]
MIXCHK = [3, 3, 2, 3, 3, 2]
MIXOFF = [0, 3, 6, 8, 11, 14]


def mix_chunk(kk):
    for c in range(6):
        if MIXOFF[c] <= kk < MIXOFF[c] + MIXCHK[c]:
            return c, kk - MIXOFF[c]


CHK = [3, 3, 3, 3, 3, 1]
MIX_CHUNK = [(2 * r + i) if i < 2 else (8 + r if i == 2 else 12 + r) for r in range(4) for i in range(4)]


def dense_load_mix_gathered(dn, mix_g, gbuf, sel_d):
    P = dn.P
    sel = dn.st
    P.dma("sp", sel[:, 8:12], sel_d, writes=[dn.stb])
    n = 0
    for r in range(4):
        for i in range(4):
            k = MIX_CHUNK[r * 4 + i]
            wa, wab = dn.wA[n % 2], dn.wAb[n % 2]
            n += 1
            cand = wa[:].rearrange("p k c -> p (k c)")[:, 0:4 * NT_CORE].rearrange("p (j t) -> p j t", j=4)
            for j in range(4):
                c_, lk = mix_chunk(i * 4 + j)
                row = r * MIXCHK[c_] * 128 + lk * 128
                P.dma("sp", cand[:, j, :], mix_g[c_][row:row + 128, :], reads=[gbuf], writes=[wab])
            P.op("dve", lambda e: e.tensor_scalar(dn.hm[:, k, :], cand[:, 0, :], sel[:, 8:9], None, ALU.mult), reads=[wab, dn.stb], writes=[dn.hmb])
            for j in range(1, 4):
                P.op("dve", lambda e: e.scalar_tensor_tensor(dn.hm[:, k, :], cand[:, j, :], sel[:, 8 + j:9 + j], dn.hm[:, k, :], ALU.mult, ALU.add),
                     reads=[wab, dn.stb, dn.hmb], writes=[dn.hmb])


def fused_ada(P, cT2, wada, bada, modp):
    with ExitStack() as st:
        sT = P.sb("sT", [128, 16, 2], F32, st)
        sTb = Buf("sT")
        w = [P.sb(f"w{i}", [128, 16, 512], F32, st) for i in range(2)]
        wb = [Buf(f"w{i}") for i in range(2)]
        bt = [P.sb(f"bt{i}", [2, 512], F32, st) for i in range(2)]
        btb = [Buf(f"bt{i}") for i in range(2)]
        ot = [P.sb(f"ot{i}", [2, 512], F32, st) for i in range(2)]
        otb = [Buf(f"ot{i}") for i in range(2)]
        ps = [P.ps(f"ps{i}", [128, 512], F32, st) for i in range(2)]
        psb = [Buf(f"ps{i}", True) for i in range(2)]
        P.dma("sp", sT[:], cT2.rearrange("(k p) r -> p k r", p=128), writes=[sTb])
        P.op("act", lambda e: e.activation(sT[:], sT[:], AF.Silu), reads=[sTb], writes=[sTb])
        i = 0
        for l in range(4):
            for cc in range(6):
                j = i % 2
                i += 1
                P.dma("sp", w[j][:], wada[l, :, cc * 512:(cc + 1) * 512].rearrange("(k p) c -> p k c", p=128), writes=[wb[j]])
                P.dma("sp", bt[j][:], bada[l, cc * 512:(cc + 1) * 512].partition_broadcast(2), writes=[btb[j]])
                for k in range(16):
                    P.op("pe", lambda e: e.matmul(ps[j][0:2, :], sT[:, k, :], w[j][:, k, :], start=(k == 0), stop=(k == 15)),
                         reads=[sTb, wb[j]], writes=[psb[j]])
                P.op("dve", lambda e: e.tensor_tensor(ot[j][:], ps[j][0:2, :], bt[j][:], ALU.add), reads=[psb[j], btb[j]], writes=[otb[j]])
                P.dma("sp", modp[2 * l:2 * l + 2, cc * 512:(cc + 1) * 512], ot[j][:], reads=[otb[j]])
        full_barrier(P)


def full_barrier(P):
    toks = []
    for e in ("pe", "act", "dve", "pool"):
        if e in P.last:
            toks.append(P.last[e])
    for s in range(NDS):
        if P.dcnt[s] > 0:
            toks.append(("dma", P.dsems[s], P.dcnt[s]))
    if hasattr(P, "csem") and P.ccnt > 0:
        toks.append(("cc", P.csem, P.ccnt))
    for e in P.ENG:
        for t in toks:
            if t[0] != e:
                P._wait(e, t)


def build_fused(nlayers=4, nslices=16, probe=None, stop=None):
    P = Prog()
    nc = P.nc
    probe = probe or (lambda P, tag: None)
    decl = {}
    SH = dict(ident=[128, 128], x_in=[NT_CORE, D], cT2=[D, 2], wada=[4, D, 3072], bada=[4, 3072], norm1=[4, D], norm2=[4, D],
              final_norm=[D], w_out=[4, D, D], w1=[4, D, DFF], w2=[4, DFF, D], win=[4, D, 896], wuq=[4, 512, 512], wukv=[4, 256, 512],
              qn=[4, 128, 4], kvn=[4, 128, 2], onorm=[4, 256], cos=[64, NTOK], sin=[64, NTOK], wfm=[4, D, 512], wtm=[4, D, 456],
              gbias=[4, 8], convw=[4, 128, 15], mnorm=[4, 128], gnorm=[4, 128], cm=[128, 22, 128], sel=[128, 4])

    def IN(n):
        if n not in decl:
            decl[n] = P.dram(n, SH[n], F32, "ExternalInput")
        return decl[n]

    P.used_inputs = decl
    y = P.dram("y", [1024, D], F32, "ExternalOutput")
    modp = nc.dram_tensor("i_modp", [8, 3072], F32, kind="Internal")
    modg = nc.dram_tensor("i_modg", [4 * 4 * 2, 3072], F32, kind="Internal")
    mymod = nc.dram_tensor("i_mymod", [4, 2, 6 * D], F32, kind="Internal")
    hT_loc = [nc.dram_tensor(f"i_hT{k}", [CHK[k] * 128, NT_CORE], BF16, kind="Internal") for k in range(6)]
    hT_g = [nc.dram_tensor(f"i_hTg{k}", [4 * CHK[k] * 128, NT_CORE], BF16, kind="Internal") for k in range(6)]
    mix_loc = [nc.dram_tensor(f"i_mix{k}", [MIXCHK[k] * 128, NT_CORE], BF16, kind="Internal") for k in range(6)]
    mix_g = [nc.dram_tensor(f"i_mixg{k}", [4 * MIXCHK[k] * 128, NT_CORE], BF16, kind="Internal") for k in range(6)]
    x_dr = nc.dram_tensor("i_x", [NT_CORE, D], F32, kind="Internal")
    modgb, mymodb, hTgb, mixgb = Buf("modg"), Buf("mymod"), Buf("hTg"), Buf("mixg")

    def early_out(src2d_f32):
        P.dma("sp", y[0:8, 0:3072 if False else D], src2d_f32, reads=[mymodb])
        P.finish()
        return P

    fused_ada(P, IN("cT2"), IN("wada"), IN("bada"), modp)
    if stop == "ada0":
        P.dma("sp", y[0:8, :], modp[:, 0:D])
        P.finish()
        return P
    P.coll("AllGather", [modp.ap().opt()], [modg.ap().opt()], GROUPS4, writes=[modgb])
    for r in range(4):
        P.dma("sp", mymod[:, :, r * 3072:(r + 1) * 3072], modg[r * 8:(r + 1) * 8, :].rearrange("(l s) c -> l s c", l=4),
              reads=[modgb], writes=[mymodb])
    full_barrier(P)
    if stop == "ada":
        P.dma("sp", y[0:8, :], mymod[:, :, 0:D].rearrange("l s c -> (l s) c"), reads=[mymodb])
        P.finish()
        return P

    def modv(l):
        return mymod[l].rearrange("s (j v) -> s j v", j=6)

    ident, norm1, norm2 = IN("ident"), IN("norm1"), IN("norm2")
    x_in = IN("x_in")
    with ExitStack() as st:
        dn = Dense(P, st)
        dn.load_ident(ident[:])
        dn.load_x(x_in)
        m = modv(0)
        dn.prep_mod(norm1[0], m[0, 0, :], m[0, 1, :], m[1, 0, :], m[1, 1, :])
        dn.norm_mod_T()
        dn.store_hT(hT_loc)
        full_barrier(P)
    x_src = x_in
    for l in range(nlayers):
        for k in range(6):
            P.coll("AllGather", [hT_loc[k].ap().opt()], [hT_g[k].ap().opt()], GROUPS4, writes=[hTgb])
        if stop == "hT":
            with ExitStack() as st:
                tt = P.sb("dbg16", [128, NT_CORE], BF16, st)
                tf = P.sb("dbg32", [128, NT_CORE], F32, st)
                ttb = Buf("dbg")
                P.dma("sp", tt[:], hT_g[4][3 * 384:3 * 384 + 128, :], reads=[hTgb], writes=[ttb])
                P.op("dve", lambda e: e.tensor_copy(tf[:], tt[:]), reads=[ttb], writes=[ttb])
                P.dma("sp", y[0:128, 0:NT_CORE], tf[:], reads=[ttb])
            P.finish()
            return P
        with ExitStack() as st:
            mx = Mixer(P, st)
            mx.consts(ident[:])
            mx.mla((hT_g, hTgb), IN("win")[l], IN("wuq")[l], IN("wukv")[l], IN("qn")[l], IN("kvn")[l], IN("cos"), IN("sin"), IN("onorm")[l], ("dest", mix_loc))
            for k in range(3):
                P.coll("AllGather", [mix_loc[k].ap().opt()], [mix_g[k].ap().opt()], GROUPS4, writes=[mixgb])
            mx.rec((hT_g, hTgb), IN("wfm")[l], IN("wtm")[l], IN("gbias")[l], IN("convw")[l], IN("mnorm")[l], IN("gnorm")[l], IN("cm")[:], ("dest", mix_loc))
            full_barrier(P)
        for k in range(3, 6):
            P.coll("AllGather", [mix_loc[k].ap().opt()], [mix_g[k].ap().opt()], GROUPS4, writes=[mixgb])
        if stop == "mix":
            with ExitStack() as st:
                tt = P.sb("dbg16", [128, NT_CORE], BF16, st)
                tf = P.sb("dbg32", [128, NT_CORE], F32, st)
                ttb = Buf("dbg")
                P.dma("sp", tt[:], mix_g[3][2 * 384:2 * 384 + 128, :], reads=[mixgb], writes=[ttb])
                P.op("dve", lambda e: e.tensor_copy(tf[:], tt[:]), reads=[ttb], writes=[ttb])
                P.dma("sp", y[0:128, 0:NT_CORE], tf[:], reads=[ttb])
            P.finish()
            return P
        with ExitStack() as st:
            dn = Dense(P, st)
            dn.load_ident(ident[:])
            dn.load_x(x_src)
            dense_load_mix_gathered(dn, mix_g, mixgb, IN("sel")[:])
            m = modv(l)
            dn.load_gates(m[0, 2, :], m[1, 2, :])
            dn.wout(IN("w_out")[l])
            dn.prep_mod(norm2[l], m[0, 3, :], m[0, 4, :], m[1, 3, :], m[1, 4, :])
            dn.norm_mod_T()
            dn.load_gates(m[0, 5, :], m[1, 5, :])
            dn.mlp(IN("w1")[l], IN("w2")[l], nslices)
            if l < nlayers - 1:
                mn = modv(l + 1)
                dn.prep_mod(norm1[l + 1], mn[0, 0, :], mn[0, 1, :], mn[1, 0, :], mn[1, 1, :])
                dn.norm_mod_T()
                dn.store_hT(hT_loc)
                dn.store_x(x_dr)
                x_src = x_dr
            else:
                dn.final_norm(IN("final_norm")[:], y)
            full_barrier(P)
    P.finish()
    return P


def fused_inputs(inp, core):
    b, g = divmod(core, 4)
    x, c, ctx, c_ctx = inp["x"], inp["c"], inp["ctx"], inp["c_ctx"]
    cos, sin = rope_tables()
    dd = dict(ident=np.eye(128, dtype=np.float32),
              x_in=np.ascontiguousarray(np.concatenate([ctx[b, 64 * g:64 * g + 64], x[b, 1024 * g:1024 * (g + 1)]], 0)),
              cT2=np.ascontiguousarray(np.stack([c[b], c_ctx]).T),
              wada=np.ascontiguousarray(inp["w_ada"][:, :, g * 3072:(g + 1) * 3072]),
              bada=np.ascontiguousarray(inp["b_ada"][:, g * 3072:(g + 1) * 3072]),
              norm1=inp["norm1"], norm2=inp["norm2"], final_norm=inp["final_norm"],
              w_out=inp["w_out"], w1=inp["w_mlp1"], w2=inp["w_mlp2"], cos=cos, sin=sin, cm=rec_consts(),
              sel=np.ascontiguousarray(np.tile((np.arange(4) == g).astype(np.float32)[None, :], (128, 1))))
    ml = [mixer_mla_inputs(inp, l, g) for l in range(4)]
    rc = [mixer_rec_inputs(inp, l, g) for l in range(4)]
    for k in ("win", "wuq", "wukv", "qn", "kvn", "onorm"):
        dd[k] = np.ascontiguousarray(np.stack([m[k] for m in ml]))
    for k in ("wfm", "wtm", "gbias", "convw", "mnorm", "gnorm"):
        dd[k] = np.ascontiguousarray(np.stack([m[k] for m in rc]))
    return dd


def kernel_fused(**inp):
    inp = {k: np.asarray(v) for k, v in inp.items()}
    P = _prog("fused", build_fused)
    res = _run(P, [{k: v for k, v in fused_inputs(inp, core).items() if k in P.used_inputs} for core in range(8)])
    out = np.stack([np.concatenate([res[b * 4 + g]["y"] for g in range(4)], axis=0) for b in range(2)])
    return np.ascontiguousarray(out.astype(np.float32))


def kernel(**inp):
    return kernel_fused(**inp)
```
